# Optimizing a Trainium2 kernel written in Bass

```python
import math
import jax, jax.numpy as jnp
from jax import lax
import numpy as np

D_MODEL = 1024
BATCH = 2
SEQ = 8192
DEPTH = 1

BLOCK = 128
N_META = 16
N_LEAD = BLOCK
FIRST_VALID = N_LEAD - N_META
HEAD_DIM = 64
BRANCH_WIDTH = D_MODEL // 2
N_BRANCH = 2
SB_HEADS = BRANCH_WIDTH // HEAD_DIM
SW_HEADS = BRANCH_WIDTH // HEAD_DIM
SW_KV_HEADS = 2
SW_GROUP = SW_HEADS // SW_KV_HEADS
SW_KV_WIDTH = SW_KV_HEADS * HEAD_DIM
WINDOW = 128
N_BUCKETS = 32
MAX_DISTANCE = 128
RMS_EPS = 1e-6
SPLIT_SIZES = (
    BRANCH_WIDTH, BRANCH_WIDTH, BRANCH_WIDTH, BRANCH_WIDTH,
    BRANCH_WIDTH, SW_KV_WIDTH, SW_KV_WIDTH, BRANCH_WIDTH,
    N_BRANCH * D_MODEL,
)
PROJ_WIDTH = sum(SPLIT_SIZES)
SPLIT_POINTS = tuple(int(i) for i in np.cumsum(SPLIT_SIZES)[:-1])

kernel_name = "hybrid_stickbreak_swa_sink_block"


def _rmsnorm(x, w):
    xf = x.astype(jnp.float32)
    xf = xf * lax.rsqrt(jnp.mean(xf * xf, axis=-1, keepdims=True) + RMS_EPS)
    return (xf * w.astype(jnp.float32)).astype(x.dtype)


def _t5_buckets(rel):
    n = np.maximum(rel, 0)
    max_exact = N_BUCKETS // 2
    large = max_exact + (np.log(np.maximum(n, 1) / max_exact)
                         / math.log(MAX_DISTANCE / max_exact)
                         * (N_BUCKETS - max_exact)).astype(np.int32)
    large = np.minimum(large, N_BUCKETS - 1)
    return np.where(n < max_exact, n, large).astype(np.int32)


def _stick_breaking(q, k, v):
    b, h, lp, dh = q.shape
    nb = lp // BLOCK
    kpos = jnp.arange(lp)
    key_ok = kpos >= FIRST_VALID
    scale = dh ** -0.5
    qblocks = q.reshape(b, h, nb, BLOCK, dh).transpose(2, 0, 1, 3, 4)
    qpos = jnp.arange(lp).reshape(nb, BLOCK)

    def one_block(args):
        qb, qp = args
        z = jnp.einsum('bhqd,bhkd->bhqk', qb, k, preferred_element_type=jnp.float32) * scale
        visible = (kpos[None, :] < qp[:, None]) & key_ok[None, :]
        log_1m = jnp.where(visible, jax.nn.log_sigmoid(-z), 0.0)
        later = lax.cumsum(log_1m, axis=3, reverse=True) - log_1m
        w = jnp.where(visible, jnp.exp(jax.nn.log_sigmoid(z) + later), 0.0)
        return jnp.einsum('bhqk,bhkd->bhqd', w.astype(v.dtype), v,
                          preferred_element_type=jnp.float32)

    out = lax.map(one_block, (qblocks, qpos))
    return out.transpose(1, 2, 0, 3, 4).reshape(b, h, lp, dh)


def _sliding_window_gqa(q, k, v, q_gain, k_gain, sinks, rel_bias):
    b, _, lp, dh = q.shape
    nb = lp // BLOCK
    q = _rmsnorm(q, q_gain)
    k = _rmsnorm(k, k_gain)
    qb = q.reshape(b, SW_KV_HEADS, SW_GROUP, nb, BLOCK, dh)

    def band(t):
        t = jnp.pad(t, ((0, 0), (0, 0), (BLOCK, 0), (0, 0))).reshape(b, SW_KV_HEADS, nb + 1, BLOCK, dh)
        return jnp.concatenate([t[:, :, :-1], t[:, :, 1:]], axis=3)

    kb, vb = band(k), band(v)
    logits = jnp.einsum('bkgnqd,bknsd->bkgnqs', qb, kb,
                        preferred_element_type=jnp.float32) * dh ** -0.5
    rel = (BLOCK + np.arange(BLOCK))[:, None] - np.arange(2 * BLOCK)[None, :]
    bias = rel_bias.astype(jnp.float32)[_t5_buckets(rel)]
    bias = jnp.transpose(bias, (2, 0, 1)).reshape(SW_KV_HEADS, SW_GROUP, 1, BLOCK, 2 * BLOCK)
    kpos = (np.arange(nb)[:, None, None] * BLOCK - BLOCK + np.arange(2 * BLOCK)[None, None, :])
    visible = (rel >= 0)[None] & (rel < WINDOW)[None] & (kpos >= FIRST_VALID)
    logits = jnp.where(visible, logits + bias, -jnp.inf)
    sink = sinks.astype(jnp.float32).reshape(SW_KV_HEADS, SW_GROUP, 1, 1, 1)
    m = jnp.maximum(jnp.max(logits, axis=-1, keepdims=True), sink)
    p = jnp.exp(logits - m)
    denom = jnp.sum(p, axis=-1, keepdims=True) + jnp.exp(sink - m)
    out = jnp.einsum('bkgnqs,bknsd->bkgnqd', (p / denom).astype(v.dtype), vb,
                     preferred_element_type=jnp.float32)
    return out.reshape(b, SW_HEADS, lp, dh)


def _heads(t, n):
    b, l, _ = t.shape
    return t.reshape(b, l, n, HEAD_DIM).transpose(0, 2, 1, 3)


def _merge_heads(t):
    b, h, l, d = t.shape
    return t.transpose(0, 2, 1, 3).reshape(b, l, h * d)


def _layer(x, norm_w, w_in, q_gain, k_gain, sinks, w_branch, w_out, rel_bias):
    b, lp, _ = x.shape
    xn = _rmsnorm(x, norm_w)
    proj = jnp.einsum('bld,dp->blp', xn, w_in)
    qa, ka, va, za, qbq, kbk, vbv, zb, gates = jnp.split(proj, SPLIT_POINTS, axis=-1)
    oa = _merge_heads(_stick_breaking(_heads(qa, SB_HEADS), _heads(ka, SB_HEADS),
                                      _heads(va, SB_HEADS))).astype(x.dtype)
    ob = _merge_heads(_sliding_window_gqa(_heads(qbq, SW_HEADS), _heads(kbk, SW_KV_HEADS),
                                          _heads(vbv, SW_KV_HEADS), q_gain, k_gain,
                                          sinks, rel_bias)).astype(x.dtype)
    branches = jnp.stack([oa * jax.nn.silu(za), ob * jax.nn.silu(zb)], axis=2)
    y = jnp.einsum('blgc,gcd->blgd', branches, w_branch)
    g = jax.nn.sigmoid(gates.reshape(b, lp, N_BRANCH, D_MODEL))
    merged = jnp.sum(g * y, axis=2)
    return x + jnp.einsum('bld,de->ble', merged, w_out)


def setup_inputs(seed: int = 0) -> dict:
    key = jax.random.key(seed)
    ks = jax.random.split(key, 11)
    f32 = jnp.float32
    x = jax.random.normal(ks[0], (BATCH, SEQ, D_MODEL), f32)
    meta = jax.random.normal(ks[1], (N_META, D_MODEL), f32)
    rel_bias = 0.5 * jax.random.normal(ks[2], (N_BUCKETS, SW_HEADS), f32)
    norm_w = 1.0 + 0.05 * jax.random.normal(ks[3], (DEPTH, D_MODEL), f32)
    w_in = jax.random.normal(ks[4], (DEPTH, D_MODEL, PROJ_WIDTH), f32) * D_MODEL ** -0.5
    q_gain = 1.0 + 0.05 * jax.random.normal(ks[5], (DEPTH, HEAD_DIM), f32)
    k_gain = 1.0 + 0.05 * jax.random.normal(ks[6], (DEPTH, HEAD_DIM), f32)
    sinks = 0.5 * jax.random.normal(ks[7], (DEPTH, SW_HEADS), f32)
    w_branch = jax.random.normal(ks[8], (DEPTH, N_BRANCH, BRANCH_WIDTH, D_MODEL), f32) * BRANCH_WIDTH ** -0.5
    w_out = jax.random.normal(ks[9], (DEPTH, D_MODEL, D_MODEL), f32) * D_MODEL ** -0.5
    return {"x": x, "meta": meta, "rel_bias": rel_bias, "norm_w": norm_w, "w_in": w_in,
            "q_gain": q_gain, "k_gain": k_gain, "sinks": sinks, "w_branch": w_branch,
            "w_out": w_out}


def reference(x, meta, rel_bias, norm_w, w_in, q_gain, k_gain, sinks, w_branch, w_out):
    b = x.shape[0]
    pad = jnp.zeros((b, FIRST_VALID, D_MODEL), x.dtype)
    h = jnp.concatenate([pad, jnp.broadcast_to(meta.astype(x.dtype)[None], (b, N_META, D_MODEL)), x], axis=1)
    for layer in range(DEPTH):
        h = _layer(h, norm_w[layer], w_in[layer], q_gain[layer], k_gain[layer], sinks[layer],
                   w_branch[layer], w_out[layer], rel_bias)
    return h[:, N_LEAD:].astype(x.dtype)
```

```python
import numpy as np
import ml_dtypes
import concourse.bass as bass
import concourse.mybir as mybir
from concourse.bass_utils import run_bass_kernel_spmd

F32 = mybir.dt.float32
BF16 = mybir.dt.bfloat16
AF = mybir.ActivationFunctionType
ALU = mybir.AluOpType

D = 1024
SEQ = 8192
LP = SEQ + 128
NBLK = LP // 128
NOWN = 16
TOWN = NOWN * 128
EPS = 1e-6
NEG = -30000.0
C_QA, C_KA, C_VA, C_ZA, C_QB, C_KB, C_VB, C_ZB, C_G = 0, 512, 1024, 1536, 2048, 2560, 2688, 2816, 3328


class Sched:
    def __init__(self):
        self.ops = []
        self.lastw = {}
        self.readers = {}

    def op(self, eng, fn, reads=(), writes=(), dma_key=None):
        idx = len(self.ops)
        deps = set()
        for k in reads:
            w = self.lastw.get(k)
            if w is not None:
                deps.add(w)
        for k in writes:
            w = self.lastw.get(k)
            if w is not None:
                deps.add(w)
            for r in self.readers.get(k, {}).values():
                deps.update(r)
        self.ops.append(dict(eng=eng, fn=fn, deps=deps, dma_key=dma_key))
        for k in reads:
            d = self.readers.setdefault(k, {})
            if dma_key is not None:
                d.setdefault(eng, []).append(idx)
            else:
                d[eng] = [idx]
        for k in writes:
            self.lastw[k] = idx
            self.readers[k] = {}
        return idx

    def run(self, nc, block, G):
        ops = self.ops
        needed = set()
        for o in ops:
            for d in o["deps"]:
                if not (ops[d]["eng"] == "pe" and o["eng"] == "pe"):
                    needed.add(d)
        lastop = {}
        for i, o in enumerate(ops):
            if o["dma_key"] is None:
                lastop[o["eng"]] = i
        needed.update(lastop.values())
        cnt = G.cnt
        prev_final = list(G.final.values())
        for i, o in enumerate(ops):
            if o["dma_key"] is not None:
                k = ("dma", o["dma_key"])
                if k not in G.dma_sems:
                    G.dma_sems[k] = G.pool.pop()
                cnt[k] = cnt.get(k, 0) + 16
                o["sem"] = G.dma_sems[k]
                o["val"] = cnt[k]
                G.final[k] = (o["sem"], o["val"])
            elif i in needed:
                k = o["eng"]
                cnt[k] = cnt.get(k, 0) + 1
                o["sem"] = G.esem[k]
                o["val"] = cnt[k]
                G.final[k] = (o["sem"], o["val"])
            else:
                o["sem"] = None
        end_final = list(G.final.values())

        def emit(engname, e):
            known = {}
            for s_, v in prev_final:
                e.wait_ge(s_, v)
                known[id(s_)] = v
            for i, o in enumerate(ops):
                if o["eng"] != engname:
                    continue
                for d in sorted(o["deps"]):
                    y = ops[d]
                    if y["eng"] == "pe" and engname == "pe":
                        continue
                    s_, v = y["sem"], y["val"]
                    if known.get(id(s_), 0) < v:
                        e.wait_ge(s_, v)
                        known[id(s_)] = v
                ins = o["fn"](e)
                if o["sem"] is not None:
                    ins.then_inc(o["sem"], 16 if o["dma_key"] is not None else 1)
            if engname == "sync":
                for s_, v in end_final:
                    if known.get(id(s_), 0) < v:
                        e.wait_ge(s_, v)

        @block.sync
        def _(e):
            emit("sync", e)

        @block.gpsimd
        def _(e):
            emit("pool", e)

        @block.tensor
        def _(e):
            emit("pe", e)

        @block.vector
        def _(e):
            emit("dve", e)

        @block.scalar
        def _(e):
            emit("act", e)


class Ctx:
    pass


def _mm_group(pe, out, pairs, first=True, last=True):
    n = len(pairs)
    ins = None
    for i, (l, r) in enumerate(pairs):
        ins = pe.matmul(out, lhsT=l, rhs=r, start=(first and i == 0), stop=(last and i == n - 1))
    return ins


def build_program(debug=False, phases="1sw4", mini=False):
    nc = bass.Bass("TRN2", target_bir_lowering=False)
    dt = nc.dram_tensor
    hT = dt("hT", [D, LP], F32, kind="ExternalInput").ap()
    hTo = dt("hTo", [D, TOWN], F32, kind="ExternalInput").ap()
    hTs = dt("hTs", [D, TOWN], F32, kind="ExternalInput").ap()
    xo = dt("xo", [TOWN, D], F32, kind="ExternalInput").ap()
    w_in = dt("w_in", [D, 5376], F32, kind="ExternalInput").ap()
    w_br = dt("w_br", [1024, D], F32, kind="ExternalInput").ap()
    w_out = dt("w_out", [D, D], F32, kind="ExternalInput").ap()
    nw = dt("nw", [128, 8], F32, kind="ExternalInput").ap()
    gains = dt("gains", [128, 2], F32, kind="ExternalInput").ap()
    sinkP = dt("sinkP", [128, 4], F32, kind="ExternalInput").ap()
    biasG = dt("biasG", [128, 2048], F32, kind="ExternalInput").ap()
    swm = dt("swm", [128, 512], F32, kind="ExternalInput").ap()
    cst = dt("cst", [128, 1024], BF16, kind="ExternalInput").ap()
    sbm = dt("sbm", [128, 9 * 1024], BF16, kind="ExternalInput").ap()
    out = dt("out", [TOWN, D], F32, kind="ExternalOutput").ap()
    okind = "ExternalOutput" if debug else "Internal"
    kt_d = dt("kt_d", [4, 128, LP], BF16, kind=okind).ap()
    v_d = dt("v_d", [128, NBLK * 512], BF16, kind=okind).ap()
    qt_d = dt("qt_d", [4, 128, TOWN], BF16, kind=okind).ap()
    qh_d = dt("qh_d", [4, 128, TOWN], BF16, kind=okind).ap()
    kh_d = dt("kh_d", [2, 128, 2 * TOWN], BF16, kind=okind).ap()
    vb_d = dt("vb_d", [128, 32 * 128], BF16, kind=okind).ap()
    zas_d = dt("zas_d", [4, 128, TOWN], BF16, kind=okind).ap()
    zbs_d = dt("zbs_d", [4, 128, TOWN], BF16, kind=okind).ap()
    g_d = dt("g_d", [16, 128, TOWN], BF16, kind=okind).ap()
    oa_d = dt("oa_d", [4, 128, TOWN], BF16, kind=okind).ap()
    ob_d = dt("ob_d", [4, 128, TOWN], BF16, kind=okind).ap()

    T = Ctx()
    T.__dict__.update(locals())
    import contextlib
    with contextlib.ExitStack() as gst, nc.allow_low_precision(reason="bf16 matmul operands by design"):
        G = Ctx()
        G.esem = {k: gst.enter_context(nc.semaphore("s_" + k)) for k in ("pe", "act", "dve", "pool", "sync")}
        G.pool = [gst.enter_context(nc.semaphore("dq%d" % i)) for i in range(90)]
        G.dma_sems = {}
        G.cnt = {}
        G.final = {}
        T.G = G
        T.phases = phases
        T.mini = mini
        if "1" in phases:
            phase1(nc, T)
        if "s" in phases:
            phase_sb(nc, T)
        if "w" in phases:
            phase_swa(nc, T)
        if "4" in phases:
            phase4(nc, T)
    return nc


def _load_consts(S, T, cf, cb):
    S.op("sync", lambda e: e.dma_start(out=cb[:], in_=T.cst[:, :]), writes=["cb"], dma_key="cb")


def _rstd(S, src_key, src, dst_key, dst, tmp_key, tmp, scale):
    S.op("act", lambda e: e.activation(out=tmp, in_=src, func=AF.Ln, bias=T_EPS[0], scale=scale),
         reads=[src_key, "epsb"], writes=[tmp_key])
    S.op("act", lambda e: e.activation(out=dst, in_=tmp, func=AF.Exp, scale=-0.5), reads=[tmp_key], writes=[dst_key])


T_EPS = [None]


def phase1(nc, T):
    S = Sched()
    import contextlib
    with contextlib.ExitStack() as st:
        if True:
            sb = lambda name, shape, dtype: st.enter_context(nc.sbuf_tensor(name, shape, dtype))
            pt = lambda name, shape, dtype: st.enter_context(nc.psum_tensor(name, shape, dtype))
            sm = lambda name: st.enter_context(nc.semaphore(name))
            cf = sb("cf", [128, 1024], F32)
            cb = sb("cb", [128, 1024], BF16)
            nwt = sb("nwt", [128, 8], F32)
            gt = sb("gt", [128, 2], F32)
            gq8 = sb("gq8", [128, 1], F32)
            epsb = sb("epsb", [128, 1], F32)
            xs = sb("xs", [128, 2, 8 * 512], F32)
            xw = sb("xw", [128, 2, 8 * 512], BF16)
            sq = sb("sq", [128, 2, 8 * 512], BF16)
            Rl = sb("Rl", [128, 512], F32)
            R = sb("R", [128, 2, 512], F32)
            Rtl = sb("Rtl", [128, 4], F32)
            Rt = sb("Rt", [128, 2, 4], F32)
            wst = sb("wst", [128, 2, 8 * 256], F32)
            Wk = sb("Wk", [128, 8 * 512], BF16)
            Wv = sb("Wv", [128, 8 * 512], BF16)
            wt = sb("wt", [128, 2, 8 * 128], BF16)
            kto = sb("kto", [128, 2, 4 * 512], BF16)
            vo = sb("vo", [128, 2, 4 * 512], BF16)
            xwo = sb("xwo", [128, 8 * TOWN], BF16)
            Ro = sb("Ro", [128, TOWN], F32)
            Rto = sb("Rto", [128, 16], F32)
            u = sb("u", [128, 2, 512], F32)
            ex = sb("ex", [128, 2, 512], F32)
            sq2 = sb("sq2", [128, 2, 512], BF16)
            fo = sb("fo", [128, 2, TOWN], BF16)
            vbo = sb("vbo", [128, 2, 512], BF16)
            ps = pt("ps", [128, 2, 512], F32)
            psr = pt("psr", [128, 512], F32)
            pst = pt("pst", [128, 4], F32)
            psn = pt("psn", [128, 2, 512], F32)
            ones = cb[:, 384:512]
            onescol = cb[:, 384:385]
            bdiag = cb[:, 768:896]
            _load_consts(S, T, cf, cb)
            S.op("sync", lambda e: e.dma_start(out=nwt[:], in_=T.nw[:, :]), writes=["nwt"], dma_key="nwt")
            S.op("sync", lambda e: e.dma_start(out=gt[:], in_=T.gains[:, :]), writes=["gt"], dma_key="gt")
            S.op("dve", lambda e: e.memset(epsb[:], EPS), writes=["epsb"])
            S.op("dve", lambda e: e.tensor_scalar(out=gq8[:], in0=gt[:, 0:1], scalar1=0.125, scalar2=None, op0=ALU.mult),
                 reads=["gt"], writes=["gq8"])
            T_EPS[0] = epsb[:, 0:1]

            def load_w(dst, dkey, col0, ncols, dcol0, dstride, dup_to=None, src_ap=None):
                done = 0
                nload = [0]
                while done < ncols:
                    n = min(256, ncols - done)
                    b = load_w.cnt % 2
                    load_w.cnt += 1
                    c0 = col0 + done
                    src = (T.w_in if src_ap is None else src_ap).rearrange("(kc p) c -> p kc c", p=128)[:, :, c0:c0 + n]
                    dstage = wst[:, b, 0:8 * n].rearrange("p (kc c) -> p kc c", kc=8)
                    S.op("sync", lambda e, s=src, d=dstage: e.dma_start(out=d, in_=s), writes=[("wst", b)], dma_key=("wst", b))
                    for tgt0 in ([dcol0] if dup_to is None else [dcol0, dup_to]):
                        dv = dst.rearrange("p (kc c) -> p kc c", kc=8)[:, :, tgt0 + done:tgt0 + done + n]
                        S.op("dve", lambda e, s=dstage, d=dv: e.tensor_copy(out=d, in_=s), reads=[("wst", b)], writes=[dkey])
                    done += n
            load_w.cnt = 0

            def prep_chunk(src_ap, t0, Tn, b, xw_dst=None, xw_key=None):
                xsv = xs[:, b, 0:8 * Tn].rearrange("p (kc t) -> p kc t", kc=8)
                xwv = xw[:, b, 0:8 * Tn].rearrange("p (kc t) -> p kc t", kc=8) if xw_dst is None else xw_dst
                xwk = ("xw", b) if xw_key is None else xw_key
                sqv = sq[:, b, 0:8 * Tn].rearrange("p (kc t) -> p kc t", kc=8)
                src = src_ap.rearrange("(kc p) t -> p kc t", p=128)[:, :, t0:t0 + Tn]
                S.op("sync", lambda e: e.dma_start(out=xsv, in_=src), writes=[("xs", b)], dma_key=("xs", b))
                S.op("act", lambda e: e.activation(out=sq[:, b, 0:8 * Tn], in_=xs[:, b, 0:8 * Tn], func=AF.Square),
                     reads=[("xs", b)], writes=[("sq", b)])
                S.op("pe", lambda e: _mm_group(e, psr[:, 0:Tn], [(ones, sqv[:, kc, :]) for kc in range(8)]),
                     reads=[("sq", b), "cb"], writes=["psr"])
                _rstd(S, "psr", psr[:, 0:Tn], ("R", b), R[:, b, 0:Tn], "Rl", Rl[:, 0:Tn], 1.0 / D)

                def f_xw(e):
                    ins = None
                    for kc in range(8):
                        ins = e.scalar_tensor_tensor(out=xwv[:, kc, :], in0=xsv[:, kc, :], scalar=nwt[:, kc:kc + 1], in1=R[:, b, 0:Tn],
                                                     op0=ALU.mult, op1=ALU.mult)
                    return ins
                S.op("dve", f_xw, reads=[("xs", b), "nwt", ("R", b)], writes=[xwk])
                return xwv

            xwvs = {0: prep_chunk(T.hT, 0, 512, 0)}
            load_w(Wk[:], "Wk", C_KA, 512, 0, 512)
            load_w(Wv[:], "Wv", C_VA, 512, 0, 512)
            Wkv = Wk[:].rearrange("p (kc c) -> p kc c", kc=8)
            Wvv = Wv[:].rearrange("p (kc c) -> p kc c", kc=8)
            pcnt = [0]

            def next_ps():
                b = pcnt[0] % 2
                pcnt[0] += 1
                return b
            def chunk_geom(ci):
                return ci * 512, (512 if ci < 16 else 128)
            for ci in range(17):
                t0, Tn = chunk_geom(ci)
                nb = Tn // 128
                b = ci % 2
                xwv = xwvs[ci]
                for pp in range(4):
                    pb = next_ps()
                    S.op("pe", lambda e, pb=pb, pp=pp, xwv=xwv, Tn=Tn: _mm_group(
                        e, ps[:, pb, 0:Tn], [(Wkv[:, kc, pp * 128:(pp + 1) * 128], xwv[:, kc, :]) for kc in range(8)]),
                        reads=["Wk", ("xw", b)], writes=[("ps", pb)])
                    S.op("dve", lambda e, pb=pb, pp=pp, Tn=Tn, b=b: e.tensor_copy(
                        out=kto[:, b, pp * 512:pp * 512 + Tn], in_=ps[:, pb, 0:Tn]),
                        reads=[("ps", pb)], writes=[("kto", b, pp)])
                    S.op("pool", lambda e, pp=pp, Tn=Tn, b=b, t0=t0: e.dma_start(
                        out=T.kt_d[pp, :, t0:t0 + Tn], in_=kto[:, b, pp * 512:pp * 512 + Tn]),
                        reads=[("kto", b, pp)], writes=[("kt_d", pp, ci)], dma_key=("kto", b, pp))
                if ci + 1 < 17:
                    t1, Tn1 = chunk_geom(ci + 1)
                    xwvs[ci + 1] = prep_chunk(T.hT, t1, Tn1, (ci + 1) % 2)
                for bl in range(nb):
                    pb = next_ps()
                    S.op("pe", lambda e, pb=pb, bl=bl, xwv=xwv: _mm_group(
                        e, ps[:, pb, :], [(xwv[:, kc, bl * 128:(bl + 1) * 128], Wvv[:, kc, :]) for kc in range(8)]),
                        reads=["Wv", ("xw", b)], writes=[("ps", pb)])
                    S.op("act", lambda e, pb=pb, bl=bl, b=b: e.activation(
                        out=vo[:, b, bl * 512:(bl + 1) * 512], in_=ps[:, pb, :], func=AF.Copy),
                        reads=[("ps", pb)], writes=[("vo", b)])
                blk0 = t0 // 128
                S.op("pool", lambda e, b=b, nb=nb, blk0=blk0: e.dma_start(
                    out=T.v_d[:, blk0 * 512:(blk0 + nb) * 512], in_=vo[:, b, 0:nb * 512]),
                    reads=[("vo", b)], writes=[("v_d", ci)], dma_key=("vo", b))

            xwov = xwo[:].rearrange("p (kc t) -> p kc t", kc=8)
            for tq in range(4):
                b = (17 + tq) % 2
                prep_chunk(T.hTo, tq * 512, 512, b, xw_dst=xwov[:, :, tq * 512:(tq + 1) * 512], xw_key=("xwo", tq))

            feats = []
            for i in range(4):
                feats.append(("qa", C_QA + i * 128, T.qt_d[i]))
            for i in range(4):
                feats.append(("qn", C_QB + i * 128, T.qh_d[i]))
            for i in range(4):
                feats.append(("silu", C_ZA + i * 128, T.zas_d[i]))
            for i in range(4):
                feats.append(("silu", C_ZB + i * 128, T.zbs_d[i]))
            for i in range(16):
                feats.append(("sig", C_G + i * 128, T.g_d[i]))
            wtv = [wt[:, wb, :].rearrange("p (kc c) -> p kc c", kc=8) for wb in range(2)]

            def norm_epilogue(pb, ub, gain_ap, gain_key, dst, dkey):
                S.op("dve", lambda e: e.tensor_copy(out=u[:, ub, :], in_=ps[:, pb, :]),
                     reads=[("ps", pb)], writes=[("u", ub)])
                S.op("act", lambda e: e.activation(out=sq2[:, ub, :], in_=u[:, ub, :], func=AF.Square),
                     reads=[("u", ub)], writes=[("sq2", ub)])
                S.op("pe", lambda e: e.matmul(psn[:, ub, :], lhsT=bdiag, rhs=sq2[:, ub, :], start=True, stop=True),
                     reads=[("sq2", ub), "cb"], writes=[("psn", ub)])
                _rstd(S, ("psn", ub), psn[:, ub, :], ("ex", ub), ex[:, ub, :], ("ex", ub), ex[:, ub, :], 1.0 / 64)
                S.op("dve", lambda e: e.scalar_tensor_tensor(out=dst, in0=u[:, ub, :], scalar=gain_ap, in1=ex[:, ub, :],
                                                             op0=ALU.mult, op1=ALU.mult),
                     reads=[("u", ub), ("ex", ub), gain_key], writes=[dkey])

            ucnt = [0]
            for fi, (kind, col0, dst_d) in enumerate(feats):
                wb = fi % 2
                fb = fi % 2
                if fi == 0:
                    load_w(wt[:, 0, :], ("wt", 0), col0, 128, 0, 128)
                if fi + 1 < len(feats):
                    load_w(wt[:, (fi + 1) % 2, :], ("wt", (fi + 1) % 2), feats[fi + 1][1], 128, 0, 128)
                for tq in range(4):
                    pb = next_ps()
                    S.op("pe", lambda e, pb=pb, wb=wb, tq=tq: _mm_group(
                        e, ps[:, pb, :], [(wtv[wb][:, kc, :], xwov[:, kc, tq * 512:(tq + 1) * 512]) for kc in range(8)]),
                        reads=[("wt", wb), ("xwo", tq)], writes=[("ps", pb)])
                    dst = fo[:, fb, tq * 512:(tq + 1) * 512]
                    dkey = ("fo", fb, tq)
                    ub = ucnt[0] % 2
                    ucnt[0] += 1
                    if kind == "qa":
                        S.op("dve", lambda e, pb=pb, dst=dst: e.tensor_scalar(
                            out=dst, in0=ps[:, pb, :], scalar1=0.125, scalar2=None, op0=ALU.mult),
                            reads=[("ps", pb)], writes=[dkey])
                    elif kind == "qn":
                        norm_epilogue(pb, ub, gq8[:, 0:1], "gq8", dst, dkey)
                    else:
                        fn_ = AF.Sigmoid if kind == "sig" else AF.Silu
                        S.op("act", lambda e, pb=pb, dst=dst, fn_=fn_: e.activation(out=dst, in_=ps[:, pb, :], func=fn_),
                             reads=[("ps", pb)], writes=[dkey])
                S.op("pool", lambda e, fb=fb, dst_d=dst_d: e.dma_start(out=dst_d, in_=fo[:, fb, :]),
                     reads=[("fo", fb, tq) for tq in range(4)], writes=[("fd", fi)], dma_key=("fo", fb))

            for g in range(2):
                load_w(Wk[:], "Wk", C_KB + g * 64, 64, g * 128, 512, dup_to=g * 128 + 64)
            load_w(Wk[:], "Wk", C_VB, 128, 256, 512)
            cxw = {0: prep_chunk(T.hTs, 0, 512, 0)}
            for tile in range(2):
                for tq in range(4):
                    if tile == 0:
                        b = (tq) % 2
                        xwv = cxw[tq]
                        xkey = ("xw", b)
                        if tq + 1 < 4:
                            cxw[tq + 1] = prep_chunk(T.hTs, (tq + 1) * 512, 512, (tq + 1) % 2)
                    else:
                        xwv = xwov[:, :, tq * 512:(tq + 1) * 512]
                        xkey = ("xwo", tq)
                    for g in range(2):
                        pb = next_ps()
                        ub = ucnt[0] % 2
                        ucnt[0] += 1
                        S.op("pe", lambda e, pb=pb, g=g, xwv=xwv: _mm_group(
                            e, ps[:, pb, :], [(Wkv[:, kc, g * 128:(g + 1) * 128], xwv[:, kc, :]) for kc in range(8)]),
                            reads=["Wk", xkey], writes=[("ps", pb)])
                        norm_epilogue(pb, ub, gt[:, 1:2], "gt", kto[:, ub, 0:512], ("kto", ub, 0))
                        S.op("pool", lambda e, ub=ub, g=g, tile=tile, tq=tq: e.dma_start(
                            out=T.kh_d[g, :, tile * TOWN + tq * 512: tile * TOWN + (tq + 1) * 512], in_=kto[:, ub, 0:512]),
                            reads=[("kto", ub, 0)], writes=[("kh_d", g, tile, tq)], dma_key=("kto", ub, 0))
                    vb_b = (tile * 4 + tq) % 2
                    for bl in range(4):
                        pb = next_ps()
                        S.op("pe", lambda e, pb=pb, bl=bl, xwv=xwv: _mm_group(
                            e, ps[:, pb, 0:128], [(xwv[:, kc, bl * 128:(bl + 1) * 128], Wkv[:, kc, 256:384]) for kc in range(8)]),
                            reads=["Wk", xkey], writes=[("ps", pb)])
                        S.op("dve", lambda e, pb=pb, bl=bl, vb_b=vb_b: e.tensor_copy(
                            out=vbo[:, vb_b, bl * 128:(bl + 1) * 128], in_=ps[:, pb, 0:128]),
                            reads=[("ps", pb)], writes=[("vbo", vb_b)])
                    s0 = (tile * 16 + tq * 4) * 128
                    S.op("pool", lambda e, vb_b=vb_b, s0=s0: e.dma_start(out=T.vb_d[:, s0:s0 + 512], in_=vbo[:, vb_b, :]),
                         reads=[("vbo", vb_b)], writes=[("vb_d", tile, tq)], dma_key=("vbo", vb_b))

            with nc.Block() as block:
                S.run(nc, block, T.G)


def phase_sb(nc, T):
    S = Sched()
    import contextlib
    with contextlib.ExitStack() as st:
        sb = lambda name, shape, dtype: st.enter_context(nc.sbuf_tensor(name, shape, dtype))
        pt = lambda name, shape, dtype: st.enter_context(nc.psum_tensor(name, shape, dtype))
        NB = 4
        NR = 5
        NZ = 4
        cb = sb("b_cb", [128, 1024], BF16)
        mb = sb("b_mb", [128, 9 * 1024], BF16)
        KTr = sb("b_KTr", [128, NR, 2 * NB * 128], BF16)
        Vr = sb("b_Vr", [128, NR, NB * 256], BF16)
        QT = sb("b_QT", [128, 8, TOWN], BF16)
        zs = sb("b_zs", [128, NZ, NB * 1024], F32)
        Lp = sb("b_Lp", [128, 3, NB * 1024], BF16)
        W = sb("b_W", [128, 3, NB * 1024], BF16)
        oo = sb("b_oo", [128, 2, 512], BF16)
        zA = pt("b_zA", [128, 2, 1024], F32)
        B = pt("b_B", [128, 1024], F32)
        O2 = pt("b_O", [128, 2, 512], F32)
        ident = cb[:, 0:128]
        negTri = cb[:, 128:256]
        negRest = cb[:, 256:384]
        _load_consts(S, T, None, cb)
        for mt in range(3):
            S.op("sync", lambda e, mt=mt: e.dma_start(out=mb[:, mt * 3072:(mt + 1) * 3072], in_=T.sbm[:, mt * 3072:(mt + 1) * 3072]),
                 writes=[("mb", mt)], dma_key=("mb", mt))
        S.op("dve", lambda e: e.memset(QT[:], 0.0), writes=["QTz"])
        qkeys = []
        for hf in range(2):
            for pp in range(2):
                for rh in range(2):
                    qi = (hf * 2 + pp) * 2 + rh
                    S.op("sync", lambda e, pp=pp, hf=hf, rh=rh, qi=qi: e.dma_start(
                        out=QT[rh * 64:(rh + 1) * 64, qi, :], in_=T.qt_d[2 * hf + pp, rh * 64:(rh + 1) * 64, :]),
                        reads=["QTz"], writes=[("QT", qi)], dma_key=("QT", qi))
                    qkeys.append(("QT", qi))
        vsrc = T.v_d.rearrange("p (blk c) -> p blk c", c=512)
        steps = []
        for hf in range(1 if T.mini else 2):
            for ip in range(1 if T.mini else 8):
                for kb in range(8 * ip + 8, -1, -1):
                    act = [1] if kb > 8 * ip + 4 else [0, 1]
                    kr = kb - (8 * ip + 1)
                    mt = kr if kr >= 0 else (8 if kb == 0 else None)
                    first = {c: (kb == (8 * ip + 8 if c == 1 else 8 * ip + 4)) for c in act}
                    steps.append(dict(hf=hf, ip=ip, kb=kb, act=act, mt=mt, first=first, last=(kb == 0)))
        n = len(steps)
        batches = []
        for si, stp in enumerate(steps):
            f0 = steps[batches[-1][0]] if batches else None
            if batches and len(batches[-1]) < NB and f0["hf"] == stp["hf"] and f0["ip"] == stp["ip"] and f0["act"] == stp["act"]:
                batches[-1].append(si)
            else:
                batches.append([si])
        bof = {}
        for bi, bt in enumerate(batches):
            for j, si in enumerate(bt):
                bof[si] = (bi, j)
        nbt = len(batches)

        def cols(stp):
            return (512, 1024) if stp["act"] == [1] else (0, 1024)

        def bview(t, slot, bi):
            bt = batches[bi]
            lo, hi = cols(steps[bt[0]])
            return t[:, slot, :].rearrange("p (j c) -> p j c", c=1024)[:, 0:len(bt), lo:hi]

        loaded = set()

        def e_load(bi):
            loaded.add(bi)
            bt = batches[bi]
            hf = steps[bt[0]]["hf"]
            kb_lo, kb_hi = steps[bt[-1]]["kb"], steps[bt[0]]["kb"]
            nj = kb_hi - kb_lo + 1
            slot = bi % NR
            ksrc = T.kt_d[2 * hf:2 * hf + 2, :, kb_lo * 128:(kb_hi + 1) * 128].rearrange("g p t -> p g t")
            kdst = KTr[:, slot, :].rearrange("p (g t) -> p g t", g=2)[:, :, 0:nj * 128]
            S.op("sync", lambda e: e.dma_start(out=kdst, in_=ksrc), writes=[("KTr", slot)], dma_key=("KTr", slot))
            vdst = Vr[:, slot, 0:nj * 256].rearrange("p (j c) -> p j c", c=256)
            S.op("sync", lambda e: e.dma_start(out=vdst, in_=vsrc[:, kb_lo:kb_hi + 1, hf * 256:(hf + 1) * 256]),
                 writes=[("Vr", slot)], dma_key=("Vr", slot))

        def e_zA(si):
            stp = steps[si]
            zb = si % 2
            bi, j = bof[si]
            slot = bi % NR
            jb = stp["kb"] - steps[batches[bi][-1]]["kb"]

            def f(e):
                ins = None
                for c in stp["act"]:
                    kc_ = 2 * stp["ip"] + c
                    for hl in range(4):
                        pp, rh = hl // 2, hl % 2
                        qi = (stp["hf"] * 2 + pp) * 2 + rh
                        ins = e.matmul(zA[:, zb, c * 512 + hl * 128:c * 512 + (hl + 1) * 128],
                                       lhsT=KTr[:, slot, pp * NB * 128 + jb * 128:pp * NB * 128 + (jb + 1) * 128],
                                       rhs=QT[:, qi, kc_ * 128:(kc_ + 1) * 128],
                                       start=(hl == 0), stop=True, skip_group_check=True)
                    if stp["mt"] is not None:
                        m0 = stp["mt"] * 1024 + c * 512
                        ins = e.matmul(zA[:, zb, c * 512:(c + 1) * 512], lhsT=ident, rhs=mb[:, m0:m0 + 512],
                                       start=False, stop=True, skip_group_check=True)
                return ins
            S.op("pe", f, reads=[("KTr", slot)] + qkeys + [("mb", 0), ("mb", 1), ("mb", 2), "cb"], writes=[("zA", zb)])

        def e_zs(si):
            lo, hi = cols(steps[si])
            bi, j = bof[si]
            S.op("dve", lambda e: e.tensor_copy(out=zs[:, bi % NZ, j * 1024 + lo:j * 1024 + hi], in_=zA[:, si % 2, lo:hi]),
                 reads=[("zA", si % 2)], writes=[("zs", bi % NZ, j)])

        def e_Lp(bi):
            nj = len(batches[bi])
            S.op("act", lambda e: e.activation(out=bview(Lp, bi % 3, bi), in_=bview(zs, bi % NZ, bi), func=AF.Softplus),
                 reads=[("zs", bi % NZ, j) for j in range(nj)], writes=[("Lp", bi % 3)])

        def e_cum(si, c, lhs, first_ok):
            stp = steps[si]
            bi, j = bof[si]
            S.op("pe", lambda e: e.matmul(B[:, c * 512:(c + 1) * 512], lhsT=lhs,
                                          rhs=Lp[:, bi % 3, j * 1024 + c * 512:j * 1024 + (c + 1) * 512],
                                          start=(first_ok and stp["first"][c]), stop=True, skip_group_check=True),
                 reads=[("Lp", bi % 3), "cb"], writes=[("B", c)])

        def e_arg(si, c):
            bi, j = bof[si]
            zv = zs[:, bi % NZ, j * 1024 + c * 512:j * 1024 + (c + 1) * 512]
            S.op("dve", lambda e: e.tensor_tensor(out=zv, in0=B[:, c * 512:(c + 1) * 512], in1=zv, op=ALU.add),
                 reads=[("B", c), ("zs", bi % NZ, j)], writes=[("zs", bi % NZ, j)])

        def e_W(bi):
            nj = len(batches[bi])
            S.op("act", lambda e: e.activation(out=bview(W, bi % 3, bi), in_=bview(zs, bi % NZ, bi), func=AF.Exp),
                 reads=[("zs", bi % NZ, j) for j in range(nj)], writes=[("W", bi % 3)])

        def e_AV(si):
            stp = steps[si]
            bi, j = bof[si]
            slot = bi % NR
            jb = stp["kb"] - steps[batches[bi][-1]]["kb"]
            hf = stp["hf"]

            ob_ = stp["ip"] % 2

            def f(e):
                ins = None
                for c in stp["act"]:
                    for hl in range(4):
                        pp, rh = hl // 2, hl % 2
                        ins = e.matmul(O2[rh * 64:(rh + 1) * 64, ob_, c * 256 + pp * 128:c * 256 + (pp + 1) * 128],
                                       lhsT=Vr[:, slot, jb * 256 + hl * 64:jb * 256 + (hl + 1) * 64],
                                       rhs=W[:, bi % 3, j * 1024 + c * 512 + hl * 128:j * 1024 + c * 512 + (hl + 1) * 128],
                                       start=(c == 1 and stp["first"][1] and pp == 0), stop=True,
                                       tile_position=(0, rh * 64), skip_group_check=True)
                return ins
            S.op("pe", f, reads=[("W", bi % 3), ("Vr", slot)], writes=[("O", ob_)])
            if stp["last"]:
                ob = stp["ip"] % 2
                S.op("dve", lambda e: e.tensor_copy(out=oo[:, ob, :], in_=O2[:, ob, :]), reads=[("O", ob)], writes=[("oo", ob)])
                for c in range(2):
                    kc_ = 2 * stp["ip"] + c
                    dst = T.oa_d[2 * hf:2 * hf + 2, :, kc_ * 128:(kc_ + 1) * 128].rearrange("g p t -> p g t")
                    src = oo[:, ob, c * 256:(c + 1) * 256].rearrange("p (g t) -> p g t", g=2)
                    S.op("pool", lambda e, dst=dst, src=src: e.dma_start(out=dst, in_=src),
                         reads=[("oo", ob)], writes=[("oa_d", hf, kc_)], dma_key=("oo", ob, c))

        for bi in range(min(NR, nbt)):
            e_load(bi)
        for bi in range(min(2, nbt)):
            for si in batches[bi]:
                e_zA(si)
                e_zs(si)
        e_Lp(0)
        lp_next, w_next, av_next = 1, 0, 0
        av_state = {"b": 0, "j": 0}

        def flush_av(bi, nmax):
            n_ = 0
            while n_ < nmax and av_state["b"] < w_next and av_state["b"] <= bi - 1:
                b_ = av_state["b"]
                e_AV(batches[b_][av_state["j"]])
                av_state["j"] += 1
                n_ += 1
                if av_state["j"] == len(batches[b_]):
                    if b_ + NR < nbt:
                        e_load(b_ + NR)
                    av_state["b"] += 1
                    av_state["j"] = 0

        for bi in range(nbt):
            if bi + 2 < nbt:
                while (bi + 2) not in loaded:
                    flush_av(bi + 2 - NR + 1, 1)
            nxt = list(batches[bi + 2]) if bi + 2 < nbt else []
            bt = batches[bi]
            for idx, si in enumerate(bt):
                acts = steps[si]["act"]
                if idx == 0:
                    for c in acts:
                        e_cum(si, c, negTri, True)
                if nxt:
                    e_zA(nxt[0])
                for c in acts:
                    e_arg(si, c)
                if nxt:
                    e_zs(nxt.pop(0))
                for c in acts:
                    e_cum(si, c, negRest, False)
                    if idx + 1 < len(bt):
                        e_cum(bt[idx + 1], c, negTri, True)
                flush_av(bi - 1, 1)
            for sj in nxt:
                e_zA(sj)
                e_zs(sj)
            if bi % 2 == 0:
                while lp_next < nbt and lp_next <= bi + 2:
                    e_Lp(lp_next)
                    lp_next += 1
            else:
                while w_next <= bi:
                    e_W(w_next)
                    w_next += 1
        while w_next < nbt:
            e_W(w_next)
            w_next += 1
        flush_av(nbt + 1, 10 ** 6)
        with nc.Block() as block:
            S.run(nc, block, T.G)


def phase_swa(nc, T):
    S = Sched()
    import contextlib
    with contextlib.ExitStack() as st:
        sb = lambda name, shape, dtype: st.enter_context(nc.sbuf_tensor(name, shape, dtype))
        pt = lambda name, shape, dtype: st.enter_context(nc.psum_tensor(name, shape, dtype))
        cf = sb("w_cf", [128, 1024], F32)
        cb = sb("w_cb", [128, 1024], BF16)
        QH = sb("w_QH", [128, 8, TOWN], BF16)
        KH = sb("w_KH", [128, 2, 2 * TOWN], BF16)
        VB = sb("w_VB", [128, 32 * 128], BF16)
        BM = sb("w_BM", [128, 2, 2048], F32)
        swt = sb("w_swt", [128, 512], F32)
        sk = sb("w_sk", [128, 4], F32)
        esk = sb("w_esk", [128, 4], F32)
        lgs = sb("w_lgs", [128, 2, 1024], F32)
        P = sb("w_P", [128, 2, 1024], BF16)
        dn = sb("w_dn", [128, 2, 256], F32)
        ObT = sb("w_ObT", [128, 4, TOWN], BF16)
        lg = pt("w_lg", [128, 2, 1024], F32)
        OD = pt("w_OD", [128, 2, 512], F32)
        ones64 = cb[:, 384:448]
        _load_consts(S, T, cf, cb)
        S.op("dve", lambda e: e.memset(QH[:], 0.0), writes=["QHz"])
        for pq in range(4):
            for rh in range(2):
                S.op("sync", lambda e, pq=pq, rh=rh: e.dma_start(out=QH[rh * 64:(rh + 1) * 64, pq * 2 + rh, :],
                                                                 in_=T.qh_d[pq, rh * 64:(rh + 1) * 64, :]),
                     reads=["QHz"], writes=[("QH", pq, rh)], dma_key=("QH", pq, rh))
        for g in range(2):
            S.op("sync", lambda e, g=g: e.dma_start(out=KH[:, g, :], in_=T.kh_d[g]), writes=[("KH", g)], dma_key=("KH", g))
        S.op("sync", lambda e: e.dma_start(out=VB[:], in_=T.vb_d[:, :]), writes=["VB"], dma_key="VB")
        S.op("sync", lambda e: e.dma_start(out=BM[:, 0, :], in_=T.biasG[:, :]), writes=["BM0"], dma_key="BM0")
        S.op("sync", lambda e: e.dma_start(out=BM[:, 1, :], in_=T.biasG[:, :]), writes=["BM1"], dma_key="BM1")
        S.op("sync", lambda e: e.dma_start(out=swt[:], in_=T.swm[:, :]), writes=["swt"], dma_key="swt")
        S.op("sync", lambda e: e.dma_start(out=sk[:], in_=T.sinkP[:, :]), writes=["sk"], dma_key="sk")
        S.op("act", lambda e: e.activation(out=esk[:], in_=sk[:], func=AF.Exp), reads=["sk"], writes=["esk"])
        for var in range(2):
            def f(e, var=var):
                ins = None
                for h in range(8):
                    ins = e.tensor_tensor(out=BM[:, var, h * 256:(h + 1) * 256], in0=BM[:, var, h * 256:(h + 1) * 256],
                                          in1=swt[:, var * 256:(var + 1) * 256], op=ALU.add)
                return ins
            S.op("dve", f, reads=["BM%d" % var, "swt"], writes=["BM%d" % var])
        units = [(k, g) for k in range(NOWN) for g in range(2)]

        def stage1(ui):
            k, g = units[ui]
            var = 1 if k == 0 else 0
            lb = ui % 2

            def f_qk(e, k=k, g=g, lb=lb):
                ins = None
                for hl in range(4):
                    h = 4 * g + hl
                    pq, rh = h // 2, h % 2
                    for tile in range(2):
                        ins = e.matmul(lg[:, lb, (hl * 2 + tile) * 128:(hl * 2 + tile + 1) * 128],
                                       lhsT=KH[:, g, tile * TOWN + k * 128:tile * TOWN + (k + 1) * 128],
                                       rhs=QH[:, pq * 2 + rh, k * 128:(k + 1) * 128], start=True, stop=True)
                return ins
            S.op("pe", f_qk, reads=[("KH", 0), ("KH", 1)] + [("QH", i, r) for i in range(4) for r in range(2)], writes=[("lg", lb)])
            S.op("dve", lambda e, lb=lb, g=g, var=var: e.tensor_tensor(
                out=lgs[:, lb, :], in0=lg[:, lb, :], in1=BM[:, var, g * 1024:(g + 1) * 1024], op=ALU.add),
                reads=[("lg", lb), "BM%d" % var], writes=[("lgs", lb)])
            S.op("act", lambda e, lb=lb: e.activation(out=P[:, lb, :], in_=lgs[:, lb, :], func=AF.Exp),
                 reads=[("lgs", lb)], writes=[("P", lb)])

        def stage2(ui):
            k, g = units[ui]
            lb = ui % 2

            def f_pv(e, k=k, g=g, lb=lb):
                ins = None
                for hl in range(4):
                    pql, rh = hl // 2, hl % 2
                    for tile in range(2):
                        slot = tile * 16 + k
                        rhs = P[:, lb, (hl * 2 + tile) * 128:(hl * 2 + tile + 1) * 128]
                        e.matmul(OD[rh * 64:(rh + 1) * 64, lb, pql * 128:(pql + 1) * 128],
                                 lhsT=VB[:, slot * 128 + g * 64:slot * 128 + (g + 1) * 64], rhs=rhs,
                                 start=(tile == 0 and pql == 0), stop=True, tile_position=(0, rh * 64), skip_group_check=True)
                        ins = e.matmul(OD[rh * 64:(rh + 1) * 64, lb, 256 + pql * 128:256 + (pql + 1) * 128],
                                       lhsT=ones64, rhs=rhs,
                                       start=False, stop=True, tile_position=(0, rh * 64), skip_group_check=True)
                return ins
            S.op("pe", f_pv, reads=[("P", lb), "VB", "cb"], writes=[("OD", lb)])

            def f_den(e, g=g, lb=lb):
                ins = None
                for pql in range(2):
                    pq = 2 * g + pql
                    ins = e.tensor_scalar(out=dn[:, lb, pql * 128:(pql + 1) * 128], in0=OD[:, lb, 256 + pql * 128:256 + (pql + 1) * 128],
                                          scalar1=esk[:, pq:pq + 1], scalar2=None, op0=ALU.add)
                return ins
            S.op("dve", f_den, reads=[("OD", lb), "esk"], writes=[("dn", lb)])
            S.op("act", lambda e, lb=lb: e.activation(out=dn[:, lb, :], in_=dn[:, lb, :], func=AF.Ln), reads=[("dn", lb)], writes=[("dn", lb)])
            S.op("act", lambda e, lb=lb: e.activation(out=dn[:, lb, :], in_=dn[:, lb, :], func=AF.Exp, scale=-1.0),
                 reads=[("dn", lb)], writes=[("dn", lb)])

            def f_nrm(e, k=k, g=g, lb=lb):
                ins = None
                for pql in range(2):
                    pq = 2 * g + pql
                    ins = e.tensor_tensor(out=ObT[:, pq, k * 128:(k + 1) * 128], in0=OD[:, lb, pql * 128:(pql + 1) * 128],
                                          in1=dn[:, lb, pql * 128:(pql + 1) * 128], op=ALU.mult)
                return ins
            S.op("dve", f_nrm, reads=[("OD", lb), ("dn", lb)], writes=["ObT"])

        stage1(0)
        for ui in range(len(units)):
            if ui + 1 < len(units):
                stage1(ui + 1)
            stage2(ui)
        for pq in range(4):
            S.op("sync", lambda e, pq=pq: e.dma_start(out=T.ob_d[pq], in_=ObT[:, pq, :]), reads=["ObT"], writes=[("ob_d", pq)],
                 dma_key=("ObT", pq))
        with nc.Block() as block:
            S.run(nc, block, T.G)


def phase4(nc, T):
    S = Sched()
    import contextlib
    with contextlib.ExitStack() as st:
        sb = lambda name, shape, dtype: st.enter_context(nc.sbuf_tensor(name, shape, dtype))
        pt = lambda name, shape, dtype: st.enter_context(nc.psum_tensor(name, shape, dtype))
        wst = sb("f_wst", [128, 2, 8 * 256], F32)
        Wb = sb("f_Wb", [128, 8 * 1024], BF16)
        Wo = sb("f_Wo", [128, 8 * 1024], BF16)
        oab = sb("f_oab", [128, 2, 8 * 512], BF16)
        zab = sb("f_zab", [128, 2, 8 * 512], BF16)
        br = sb("f_br", [128, 2, 8 * 512], BF16)
        gg = sb("f_gg", [128, 2, 16 * 512], BF16)
        m1 = sb("f_m1", [128, 2, 512], F32)
        m2 = sb("f_m2", [128, 2, 512], F32)
        mg = sb("f_mg", [128, 2, 8 * 512], BF16)
        xt = sb("f_xt", [128, 2, 4 * 1024], F32)
        py = pt("f_py", [128, 4, 512], F32)
        po = pt("f_po", [128, 2, 512], F32)
        cntw = [0]

        def load_w(dst, dkey, src_ap, ncols):
            done = 0
            while done < ncols:
                n = 256
                b = cntw[0] % 2
                cntw[0] += 1
                src = src_ap.rearrange("(kc p) c -> p kc c", p=128)[:, :, done:done + n]
                dstage = wst[:, b, :].rearrange("p (kc c) -> p kc c", kc=8)
                S.op("sync", lambda e, s_=src, d=dstage: e.dma_start(out=d, in_=s_), writes=[("wst", b)], dma_key=("wst", b))
                dv = dst.rearrange("p (kc c) -> p kc c", kc=8)[:, :, done:done + n]
                S.op("dve", lambda e, s_=dstage, d=dv: e.tensor_copy(out=d, in_=s_), reads=[("wst", b)], writes=[dkey])
                done += n
        load_w(Wb[:], "Wb", T.w_br, 1024)
        load_w(Wo[:], "Wo", T.w_out, 1024)
        Wbv = Wb[:].rearrange("p (kc c) -> p kc c", kc=8)
        Wov = Wo[:].rearrange("p (kc c) -> p kc c", kc=8)
        for tq in range(4):
            b = tq % 2
            tsl = slice(tq * 512, (tq + 1) * 512)
            oav = oab[:, b, :].rearrange("p (c t) -> p c t", c=8)
            zav = zab[:, b, :].rearrange("p (c t) -> p c t", c=8)
            brv = br[:, b, :].rearrange("p (c t) -> p c t", c=8)
            ggv = gg[:, b, :].rearrange("p (c t) -> p c t", c=16)
            mgv = mg[:, b, :].rearrange("p (c t) -> p c t", c=8)
            xtv = xt[:, b, :].rearrange("p (tb e) -> p tb e", tb=4)
            otv = xtv
            for (srcd, dstv, off, key) in ((T.oa_d, oav, 0, "oa"), (T.ob_d, oav, 4, "ob"), (T.zas_d, zav, 0, "za"), (T.zbs_d, zav, 4, "zb")):
                S.op("sync", lambda e, srcd=srcd, dstv=dstv, off=off, tsl=tsl: e.dma_start(
                    out=dstv[:, off:off + 4, :], in_=srcd[:, :, tsl].rearrange("f p t -> p f t")),
                    writes=[(key, b)], dma_key=(key, b))
            S.op("sync", lambda e, ggv=ggv, tsl=tsl: e.dma_start(out=ggv, in_=T.g_d[:, :, tsl].rearrange("f p t -> p f t")),
                 writes=[("gg", b)], dma_key=("gg", b))
            S.op("sync", lambda e, xtv=xtv, tq=tq: e.dma_start(out=xtv, in_=T.xo[tq * 512:(tq + 1) * 512, :].rearrange("(tb p) e -> p tb e", p=128)),
                 writes=[("xt", b)], dma_key=("xt", b))
            S.op("pool", lambda e, b=b: e.tensor_tensor(out=br[:, b, :], in0=oab[:, b, :], in1=zab[:, b, :], op=ALU.mult),
                 reads=[("oa", b), ("ob", b), ("za", b), ("zb", b)], writes=[("br", b)])
            for dc in range(8):
                yb = dc % 2
                for brn in range(2):
                    S.op("pe", lambda e, yb=yb, brn=brn, dc=dc, brv=brv: _mm_group(
                        e, py[:, yb * 2 + brn, :], [(Wbv[:, brn * 4 + cc, dc * 128:(dc + 1) * 128], brv[:, brn * 4 + cc, :]) for cc in range(4)]),
                        reads=["Wb", ("br", b)], writes=[("py", yb, brn)])
                S.op("dve", lambda e, yb=yb, dc=dc, ggv=ggv: e.tensor_tensor(out=m1[:, yb, :], in0=py[:, yb * 2, :], in1=ggv[:, dc, :], op=ALU.mult),
                     reads=[("py", yb, 0), ("gg", b)], writes=[("m1", yb)])
                S.op("dve", lambda e, yb=yb, dc=dc, ggv=ggv: e.tensor_tensor(out=m2[:, yb, :], in0=py[:, yb * 2 + 1, :], in1=ggv[:, 8 + dc, :], op=ALU.mult),
                     reads=[("py", yb, 1), ("gg", b)], writes=[("m2", yb)])
                S.op("pool", lambda e, yb=yb, dc=dc, mgv=mgv: e.tensor_tensor(out=mgv[:, dc, :], in0=m1[:, yb, :], in1=m2[:, yb, :], op=ALU.add),
                     reads=[("m1", yb), ("m2", yb)], writes=[("mg", b)])
            for tb in range(4):
                for eh in range(2):
                    pb = (tb * 2 + eh) % 2
                    S.op("pe", lambda e, pb=pb, tb=tb, eh=eh, mgv=mgv: _mm_group(
                        e, po[:, pb, :], [(mgv[:, dc, tb * 128:(tb + 1) * 128], Wov[:, dc, eh * 512:(eh + 1) * 512]) for dc in range(8)]),
                        reads=["Wo", ("mg", b)], writes=[("po", pb)])
                    S.op("dve", lambda e, pb=pb, tb=tb, eh=eh, xtv=xtv, otv=otv: e.tensor_tensor(
                        out=otv[:, tb, eh * 512:(eh + 1) * 512], in0=po[:, pb, :], in1=xtv[:, tb, eh * 512:(eh + 1) * 512], op=ALU.add),
                        reads=[("po", pb), ("xt", b)], writes=[("xt", b)])
            S.op("act", lambda e, otv=otv, tq=tq: e.dma_start(out=T.out[tq * 512:(tq + 1) * 512, :].rearrange("(tb p) e -> p tb e", p=128), in_=otv),
                 reads=[("xt", b)], writes=[("out", tq)], dma_key=("ot", b))
        with nc.Block() as block:
            S.run(nc, block, T.G)


def _t5_buckets(rel):
    n = np.maximum(rel, 0)
    max_exact = 16
    large = max_exact + (np.log(np.maximum(n, 1) / max_exact) / np.log(128 / max_exact) * (32 - max_exact)).astype(np.int32)
    large = np.minimum(large, 31)
    return np.where(n < max_exact, n, large).astype(np.int32)


def host_prep(x, meta, rel_bias, norm_w, w_in, q_gain, k_gain, sinks, w_branch, w_out):
    f32 = np.float32
    x = np.asarray(x, f32)
    B = x.shape[0]
    hTs_b = []
    for b in range(B):
        h = np.zeros((LP, D), f32)
        h[112:128] = np.asarray(meta, f32)
        h[128:] = x[b]
        hTs_b.append(np.ascontiguousarray(h.T))
    w_in0 = np.ascontiguousarray(np.asarray(w_in, f32)[0])
    w_br0 = np.ascontiguousarray(np.asarray(w_branch, f32)[0].reshape(1024, D))
    w_out0 = np.ascontiguousarray(np.asarray(w_out, f32)[0])
    nw = np.ascontiguousarray(np.asarray(norm_w, f32)[0].reshape(8, 128).T)
    gains = np.stack([np.tile(np.asarray(q_gain, f32)[0], 2), np.tile(np.asarray(k_gain, f32)[0], 2)], axis=1).astype(f32)
    sk = np.asarray(sinks, f32)[0]
    sinkP = np.zeros((128, 4), f32)
    for pq in range(4):
        sinkP[0:64, pq] = sk[2 * pq]
        sinkP[64:128, pq] = sk[2 * pq + 1]
    s_i = np.arange(128)[:, None]
    t_i = np.arange(128)[None, :]
    rel0 = 128 + t_i - s_i
    rel1 = t_i - s_i
    rb = np.asarray(rel_bias, f32)
    biasG = np.zeros((128, 8, 2, 128), f32)
    for h in range(8):
        biasG[:, h, 0, :] = rb[_t5_buckets(rel0), h]
        biasG[:, h, 1, :] = rb[_t5_buckets(rel1), h]
    vis0 = (rel0 < 128)
    vis1 = (rel1 >= 0)
    swm_gen = np.stack([np.where(vis0, 0.0, NEG), np.where(vis1, 0.0, NEG)], axis=1).astype(f32)
    cst = np.zeros((128, 1024), f32)
    jj = np.arange(128)[:, None]
    ss = np.arange(128)[None, :]
    cst[:, 0:128] = np.eye(128)
    cst[:, 128:256] = np.where(jj >= ss, -1.0, 0.0)
    cst[:, 256:384] = np.where(jj < ss, -1.0, 0.0)
    cst[:, 384:512] = 1.0
    cst[:, 512:576] = 1.0
    cst[:, 640 + 64:768] = 1.0
    cst[0:64, 768:832] = 1.0
    cst[64:128, 832:896] = 1.0
    cst_bf = cst.astype(ml_dtypes.bfloat16)
    in_maps = []
    for c in range(8):
        b, j = c // 4, c % 4
        own = [4 * k + 1 + j for k in range(NOWN)]
        hT = hTs_b[b]
        hTo = np.ascontiguousarray(np.concatenate([hT[:, n * 128:(n + 1) * 128] for n in own], axis=1))
        hTs = np.ascontiguousarray(np.concatenate([hT[:, (n - 1) * 128:n * 128] for n in own], axis=1))
        xo = np.ascontiguousarray(np.concatenate([x[b, (n - 1) * 128:n * 128] for n in own], axis=0))
        swm = np.zeros((128, 2, 2, 128), f32)
        swm[:, 0] = swm_gen
        swm[:, 1] = swm_gen
        if own[0] == 1:
            swm[0:112, 1, 0, :] = NEG
        sbm = np.zeros((128, 9, 2, 4, 128), f32)
        for kr in range(8):
            for cc in range(2):
                d = kr - 4 * cc - j
                if d > 0:
                    sbm[:, kr, cc] = NEG
                elif d == 0:
                    sbm[:, kr, cc] = np.where(s_i < t_i, 0.0, NEG)[:, None, :]
        sbm[0:112, 8] = NEG
        in_maps.append(dict(hT=hT, hTo=hTo, hTs=hTs, xo=xo, w_in=w_in0, w_br=w_br0, w_out=w_out0, nw=nw, gains=gains,
                            sinkP=sinkP, biasG=biasG.reshape(128, 2048), swm=swm.reshape(128, 512), cst=cst_bf,
                            sbm=sbm.reshape(128, 9 * 1024).astype(ml_dtypes.bfloat16)))
    return in_maps


def kernel(x, meta, rel_bias, norm_w, w_in, q_gain, k_gain, sinks, w_branch, w_out):
    in_maps = host_prep(x, meta, rel_bias, norm_w, w_in, q_gain, k_gain, sinks, w_branch, w_out)
    nc = build_program()
    res = run_bass_kernel_spmd(nc, in_maps, core_ids=list(range(8)))
    B = 2
    outp = np.zeros((B, SEQ, D), np.float32)
    for c in range(8):
        b, j = c // 4, c % 4
        o = np.asarray(res.results[c]["out"], np.float32)
        for k in range(NOWN):
            n = 4 * k + 1 + j
            outp[b, (n - 1) * 128:n * 128] = o[k * 128:(k + 1) * 128]
    return outp
```

```python
import numpy as np
import ml_dtypes
import concourse.bass as bass
import concourse.mybir as mybir
from concourse.bass_utils import run_bass_kernel_spmd

F32 = mybir.dt.float32
BF16 = mybir.dt.bfloat16
AF = mybir.ActivationFunctionType
ALU = mybir.AluOpType

D = 1024
SEQ = 8192
LP = SEQ + 128
NBLK = LP // 128
NOWN = 16
TOWN = NOWN * 128
EPS = 1e-6
NEG = -30000.0
C_QA, C_KA, C_VA, C_ZA, C_QB, C_KB, C_VB, C_ZB, C_G = 0, 512, 1024, 1536, 2048, 2560, 2688, 2816, 3328


class Sched:
    def __init__(self):
        self.ops = []
        self.lastw = {}
        self.readers = {}

    def op(self, eng, fn, reads=(), writes=(), dma_key=None):
        idx = len(self.ops)
        deps = set()
        for k in reads:
            w = self.lastw.get(k)
            if w is not None:
                deps.add(w)
        for k in writes:
            w = self.lastw.get(k)
            if w is not None:
                deps.add(w)
            for r in self.readers.get(k, {}).values():
                deps.update(r)
        self.ops.append(dict(eng=eng, fn=fn, deps=deps, dma_key=dma_key))
        for k in reads:
            d = self.readers.setdefault(k, {})
            if dma_key is not None:
                d.setdefault(eng, []).append(idx)
            else:
                d[eng] = [idx]
        for k in writes:
            self.lastw[k] = idx
            self.readers[k] = {}
        return idx

    def run(self, nc, block, G):
        ops = self.ops
        needed = set()
        for o in ops:
            for d in o["deps"]:
                if not (ops[d]["eng"] == "pe" and o["eng"] == "pe"):
                    needed.add(d)
        lastop = {}
        for i, o in enumerate(ops):
            if o["dma_key"] is None:
                lastop[o["eng"]] = i
        needed.update(lastop.values())
        cnt = G.cnt
        prev_final = list(G.final.values())
        for i, o in enumerate(ops):
            if o["dma_key"] is not None:
                k = ("dma", o["dma_key"])
                if k not in G.dma_sems:
                    G.dma_sems[k] = G.pool.pop()
                cnt[k] = cnt.get(k, 0) + 16
                o["sem"] = G.dma_sems[k]
                o["val"] = cnt[k]
                G.final[k] = (o["sem"], o["val"])
            elif i in needed:
                k = o["eng"]
                cnt[k] = cnt.get(k, 0) + 1
                o["sem"] = G.esem[k]
                o["val"] = cnt[k]
                G.final[k] = (o["sem"], o["val"])
            else:
                o["sem"] = None
        end_final = list(G.final.values())

        def emit(engname, e):
            known = {}
            for s_, v in prev_final:
                e.wait_ge(s_, v)
                known[id(s_)] = v
            for i, o in enumerate(ops):
                if o["eng"] != engname:
                    continue
                for d in sorted(o["deps"]):
                    y = ops[d]
                    if y["eng"] == "pe" and engname == "pe":
                        continue
                    s_, v = y["sem"], y["val"]
                    if known.get(id(s_), 0) < v:
                        e.wait_ge(s_, v)
                        known[id(s_)] = v
                ins = o["fn"](e)
                if o["sem"] is not None:
                    ins.then_inc(o["sem"], 16 if o["dma_key"] is not None else 1)
            if engname == "sync":
                for s_, v in end_final:
                    if known.get(id(s_), 0) < v:
                        e.wait_ge(s_, v)

        @block.sync
        def _(e):
            emit("sync", e)

        @block.gpsimd
        def _(e):
            emit("pool", e)

        @block.tensor
        def _(e):
            emit("pe", e)

        @block.vector
        def _(e):
            emit("dve", e)

        @block.scalar
        def _(e):
            emit("act", e)


class Ctx:
    pass


def _mm_group(pe, out, pairs, first=True, last=True):
    n = len(pairs)
    ins = None
    for i, (l, r) in enumerate(pairs):
        ins = pe.matmul(out, lhsT=l, rhs=r, start=(first and i == 0), stop=(last and i == n - 1))
    return ins


def build_program(debug=False, phases="1sw4", mini=False):
    nc = bass.Bass("TRN2", target_bir_lowering=False)
    dt = nc.dram_tensor
    hT = dt("hT", [D, LP], F32, kind="ExternalInput").ap()
    hTo = dt("hTo", [D, TOWN], F32, kind="ExternalInput").ap()
    hTs = dt("hTs", [D, TOWN], F32, kind="ExternalInput").ap()
    xo = dt("xo", [TOWN, D], F32, kind="ExternalInput").ap()
    w_in = dt("w_in", [D, 5376], F32, kind="ExternalInput").ap()
    w_br = dt("w_br", [1024, D], F32, kind="ExternalInput").ap()
    w_out = dt("w_out", [D, D], F32, kind="ExternalInput").ap()
    nw = dt("nw", [128, 8], F32, kind="ExternalInput").ap()
    gains = dt("gains", [128, 2], F32, kind="ExternalInput").ap()
    sinkP = dt("sinkP", [128, 4], F32, kind="ExternalInput").ap()
    biasG = dt("biasG", [128, 2048], F32, kind="ExternalInput").ap()
    swm = dt("swm", [128, 512], F32, kind="ExternalInput").ap()
    cst = dt("cst", [128, 1024], BF16, kind="ExternalInput").ap()
    sbm = dt("sbm", [128, 9 * 1024], BF16, kind="ExternalInput").ap()
    out = dt("out", [TOWN, D], F32, kind="ExternalOutput").ap()
    okind = "ExternalOutput" if debug else "Internal"
    kt_d = dt("kt_d", [4, 128, LP], BF16, kind=okind).ap()
    v_d = dt("v_d", [128, NBLK * 512], BF16, kind=okind).ap()
    qt_d = dt("qt_d", [4, 128, TOWN], BF16, kind=okind).ap()
    qh_d = dt("qh_d", [4, 128, TOWN], BF16, kind=okind).ap()
    kh_d = dt("kh_d", [2, 128, 2 * TOWN], BF16, kind=okind).ap()
    vb_d = dt("vb_d", [128, 32 * 128], BF16, kind=okind).ap()
    zas_d = dt("zas_d", [4, 128, TOWN], BF16, kind=okind).ap()
    zbs_d = dt("zbs_d", [4, 128, TOWN], BF16, kind=okind).ap()
    g_d = dt("g_d", [16, 128, TOWN], BF16, kind=okind).ap()
    oa_d = dt("oa_d", [4, 128, TOWN], BF16, kind=okind).ap()
    ob_d = dt("ob_d", [4, 128, TOWN], BF16, kind=okind).ap()

    T = Ctx()
    T.__dict__.update(locals())
    import contextlib
    with contextlib.ExitStack() as gst, nc.allow_low_precision(reason="bf16 matmul operands by design"):
        G = Ctx()
        G.esem = {k: gst.enter_context(nc.semaphore("s_" + k)) for k in ("pe", "act", "dve", "pool", "sync")}
        G.pool = [gst.enter_context(nc.semaphore("dq%d" % i)) for i in range(90)]
        G.dma_sems = {}
        G.cnt = {}
        G.final = {}
        T.G = G
        T.phases = phases
        T.mini = mini
        if "1" in phases:
            phase1(nc, T)
        if "s" in phases:
            phase_sb(nc, T)
        with contextlib.ExitStack() as wst4:
            T.f_wst = wst4.enter_context(nc.sbuf_tensor("f_wst", [128, 2, 8 * 256], F32))
            T.f_Wb = wst4.enter_context(nc.sbuf_tensor("f_Wb", [128, 8 * 1024], BF16))
            T.f_Wo = wst4.enter_context(nc.sbuf_tensor("f_Wo", [128, 8 * 1024], BF16))
            T.p4w_loaded = False
            if "w" in phases:
                phase_swa(nc, T)
            if "4" in phases:
                phase4(nc, T)
    return nc


def _load_consts(S, T, cf, cb):
    S.op("sync", lambda e: e.dma_start(out=cb[:], in_=T.cst[:, :]), writes=["cb"], dma_key="cb")


def _rstd(S, src_key, src, dst_key, dst, tmp_key, tmp, scale):
    S.op("act", lambda e: e.activation(out=tmp, in_=src, func=AF.Ln, bias=T_EPS[0], scale=scale),
         reads=[src_key, "epsb"], writes=[tmp_key])
    S.op("act", lambda e: e.activation(out=dst, in_=tmp, func=AF.Exp, scale=-0.5), reads=[tmp_key], writes=[dst_key])


T_EPS = [None]


def phase1(nc, T):
    S = Sched()
    import contextlib
    with contextlib.ExitStack() as st:
        if True:
            sb = lambda name, shape, dtype: st.enter_context(nc.sbuf_tensor(name, shape, dtype))
            pt = lambda name, shape, dtype: st.enter_context(nc.psum_tensor(name, shape, dtype))
            sm = lambda name: st.enter_context(nc.semaphore(name))
            cf = sb("cf", [128, 1024], F32)
            cb = sb("cb", [128, 1024], BF16)
            nwt = sb("nwt", [128, 8], F32)
            gt = sb("gt", [128, 2], F32)
            gq8 = sb("gq8", [128, 1], F32)
            epsb = sb("epsb", [128, 1], F32)
            xs = sb("xs", [128, 2, 8 * 512], F32)
            xw = sb("xw", [128, 2, 8 * 512], BF16)
            sq = sb("sq", [128, 2, 8 * 512], BF16)
            Rl = sb("Rl", [128, 512], F32)
            R = sb("R", [128, 2, 512], F32)
            Rtl = sb("Rtl", [128, 4], F32)
            Rt = sb("Rt", [128, 2, 4], F32)
            wst = sb("wst", [128, 2, 8 * 256], F32)
            Wk = sb("Wk", [128, 8 * 512], BF16)
            Wv = sb("Wv", [128, 8 * 512], BF16)
            wt = sb("wt", [128, 2, 8 * 128], BF16)
            kto = sb("kto", [128, 2, 4 * 512], BF16)
            vo = sb("vo", [128, 2, 4 * 512], BF16)
            xwo = sb("xwo", [128, 8 * TOWN], BF16)
            Ro = sb("Ro", [128, TOWN], F32)
            Rto = sb("Rto", [128, 16], F32)
            u = sb("u", [128, 2, 512], F32)
            ex = sb("ex", [128, 2, 512], F32)
            sq2 = sb("sq2", [128, 2, 512], BF16)
            fo = sb("fo", [128, 2, TOWN], BF16)
            vbo = sb("vbo", [128, 2, 512], BF16)
            ps = pt("ps", [128, 2, 512], F32)
            psr = pt("psr", [128, 512], F32)
            pst = pt("pst", [128, 4], F32)
            psn = pt("psn", [128, 2, 512], F32)
            ones = cb[:, 384:512]
            onescol = cb[:, 384:385]
            bdiag = cb[:, 768:896]
            _load_consts(S, T, cf, cb)
            S.op("sync", lambda e: e.dma_start(out=nwt[:], in_=T.nw[:, :]), writes=["nwt"], dma_key="nwt")
            S.op("sync", lambda e: e.dma_start(out=gt[:], in_=T.gains[:, :]), writes=["gt"], dma_key="gt")
            S.op("dve", lambda e: e.memset(epsb[:], EPS), writes=["epsb"])
            S.op("dve", lambda e: e.tensor_scalar(out=gq8[:], in0=gt[:, 0:1], scalar1=0.125, scalar2=None, op0=ALU.mult),
                 reads=["gt"], writes=["gq8"])
            T_EPS[0] = epsb[:, 0:1]

            def load_w(dst, dkey, col0, ncols, dcol0, dstride, dup_to=None, src_ap=None):
                done = 0
                nload = [0]
                while done < ncols:
                    n = min(256, ncols - done)
                    b = load_w.cnt % 2
                    load_w.cnt += 1
                    c0 = col0 + done
                    src = (T.w_in if src_ap is None else src_ap).rearrange("(kc p) c -> p kc c", p=128)[:, :, c0:c0 + n]
                    dstage = wst[:, b, 0:8 * n].rearrange("p (kc c) -> p kc c", kc=8)
                    S.op("sync", lambda e, s=src, d=dstage: e.dma_start(out=d, in_=s), writes=[("wst", b)], dma_key=("wst", b))
                    for tgt0 in ([dcol0] if dup_to is None else [dcol0, dup_to]):
                        dv = dst.rearrange("p (kc c) -> p kc c", kc=8)[:, :, tgt0 + done:tgt0 + done + n]
                        S.op("dve", lambda e, s=dstage, d=dv: e.tensor_copy(out=d, in_=s), reads=[("wst", b)], writes=[dkey])
                    done += n
            load_w.cnt = 0

            def prep_chunk(src_ap, t0, Tn, b, xw_dst=None, xw_key=None):
                xsv = xs[:, b, 0:8 * Tn].rearrange("p (kc t) -> p kc t", kc=8)
                xwv = xw[:, b, 0:8 * Tn].rearrange("p (kc t) -> p kc t", kc=8) if xw_dst is None else xw_dst
                xwk = ("xw", b) if xw_key is None else xw_key
                sqv = sq[:, b, 0:8 * Tn].rearrange("p (kc t) -> p kc t", kc=8)
                src = src_ap.rearrange("(kc p) t -> p kc t", p=128)[:, :, t0:t0 + Tn]
                S.op("sync", lambda e: e.dma_start(out=xsv, in_=src), writes=[("xs", b)], dma_key=("xs", b))
                S.op("act", lambda e: e.activation(out=sq[:, b, 0:8 * Tn], in_=xs[:, b, 0:8 * Tn], func=AF.Square),
                     reads=[("xs", b)], writes=[("sq", b)])
                S.op("pe", lambda e: _mm_group(e, psr[:, 0:Tn], [(ones, sqv[:, kc, :]) for kc in range(8)]),
                     reads=[("sq", b), "cb"], writes=["psr"])
                _rstd(S, "psr", psr[:, 0:Tn], ("R", b), R[:, b, 0:Tn], "Rl", Rl[:, 0:Tn], 1.0 / D)

                def f_xw(e):
                    ins = None
                    for kc in range(8):
                        ins = e.scalar_tensor_tensor(out=xwv[:, kc, :], in0=xsv[:, kc, :], scalar=nwt[:, kc:kc + 1], in1=R[:, b, 0:Tn],
                                                     op0=ALU.mult, op1=ALU.mult)
                    return ins
                S.op("dve", f_xw, reads=[("xs", b), "nwt", ("R", b)], writes=[xwk])
                return xwv

            xwvs = {0: prep_chunk(T.hT, 0, 512, 0)}
            load_w(Wk[:], "Wk", C_KA, 512, 0, 512)
            load_w(Wv[:], "Wv", C_VA, 512, 0, 512)
            Wkv = Wk[:].rearrange("p (kc c) -> p kc c", kc=8)
            Wvv = Wv[:].rearrange("p (kc c) -> p kc c", kc=8)
            pcnt = [0]

            def next_ps():
                b = pcnt[0] % 2
                pcnt[0] += 1
                return b
            def chunk_geom(ci):
                return ci * 512, (512 if ci < 16 else 128)
            for ci in range(17):
                t0, Tn = chunk_geom(ci)
                nb = Tn // 128
                b = ci % 2
                xwv = xwvs[ci]
                for pp in range(4):
                    pb = next_ps()
                    S.op("pe", lambda e, pb=pb, pp=pp, xwv=xwv, Tn=Tn: _mm_group(
                        e, ps[:, pb, 0:Tn], [(Wkv[:, kc, pp * 128:(pp + 1) * 128], xwv[:, kc, :]) for kc in range(8)]),
                        reads=["Wk", ("xw", b)], writes=[("ps", pb)])
                    S.op("dve", lambda e, pb=pb, pp=pp, Tn=Tn, b=b: e.tensor_copy(
                        out=kto[:, b, pp * 512:pp * 512 + Tn], in_=ps[:, pb, 0:Tn]),
                        reads=[("ps", pb)], writes=[("kto", b, pp)])
                    S.op("pool", lambda e, pp=pp, Tn=Tn, b=b, t0=t0: e.dma_start(
                        out=T.kt_d[pp, :, t0:t0 + Tn], in_=kto[:, b, pp * 512:pp * 512 + Tn]),
                        reads=[("kto", b, pp)], writes=[("kt_d", pp, ci)], dma_key=("kto", b, pp))
                if ci + 1 < 17:
                    t1, Tn1 = chunk_geom(ci + 1)
                    xwvs[ci + 1] = prep_chunk(T.hT, t1, Tn1, (ci + 1) % 2)
                for bl in range(nb):
                    pb = next_ps()
                    S.op("pe", lambda e, pb=pb, bl=bl, xwv=xwv: _mm_group(
                        e, ps[:, pb, :], [(xwv[:, kc, bl * 128:(bl + 1) * 128], Wvv[:, kc, :]) for kc in range(8)]),
                        reads=["Wv", ("xw", b)], writes=[("ps", pb)])
                    S.op("act", lambda e, pb=pb, bl=bl, b=b: e.activation(
                        out=vo[:, b, bl * 512:(bl + 1) * 512], in_=ps[:, pb, :], func=AF.Copy),
                        reads=[("ps", pb)], writes=[("vo", b)])
                blk0 = t0 // 128
                S.op("pool", lambda e, b=b, nb=nb, blk0=blk0: e.dma_start(
                    out=T.v_d[:, blk0 * 512:(blk0 + nb) * 512], in_=vo[:, b, 0:nb * 512]),
                    reads=[("vo", b)], writes=[("v_d", ci)], dma_key=("vo", b))

            xwov = xwo[:].rearrange("p (kc t) -> p kc t", kc=8)
            for tq in range(4):
                b = (17 + tq) % 2
                prep_chunk(T.hTo, tq * 512, 512, b, xw_dst=xwov[:, :, tq * 512:(tq + 1) * 512], xw_key=("xwo", tq))

            feats = []
            for i in range(4):
                feats.append(("qa", C_QA + i * 128, T.qt_d[i]))
            for i in range(4):
                feats.append(("qn", C_QB + i * 128, T.qh_d[i]))
            for i in range(4):
                feats.append(("silu", C_ZA + i * 128, T.zas_d[i]))
            for i in range(4):
                feats.append(("silu", C_ZB + i * 128, T.zbs_d[i]))
            for i in range(16):
                feats.append(("sig", C_G + i * 128, T.g_d[i]))
            wtv = [wt[:, wb, :].rearrange("p (kc c) -> p kc c", kc=8) for wb in range(2)]

            def norm_epilogue(pb, ub, gain_ap, gain_key, dst, dkey):
                S.op("dve", lambda e: e.tensor_copy(out=u[:, ub, :], in_=ps[:, pb, :]),
                     reads=[("ps", pb)], writes=[("u", ub)])
                S.op("act", lambda e: e.activation(out=sq2[:, ub, :], in_=u[:, ub, :], func=AF.Square),
                     reads=[("u", ub)], writes=[("sq2", ub)])
                S.op("pe", lambda e: e.matmul(psn[:, ub, :], lhsT=bdiag, rhs=sq2[:, ub, :], start=True, stop=True),
                     reads=[("sq2", ub), "cb"], writes=[("psn", ub)])
                _rstd(S, ("psn", ub), psn[:, ub, :], ("ex", ub), ex[:, ub, :], ("ex", ub), ex[:, ub, :], 1.0 / 64)
                S.op("dve", lambda e: e.scalar_tensor_tensor(out=dst, in0=u[:, ub, :], scalar=gain_ap, in1=ex[:, ub, :],
                                                             op0=ALU.mult, op1=ALU.mult),
                     reads=[("u", ub), ("ex", ub), gain_key], writes=[dkey])

            ucnt = [0]
            for fi, (kind, col0, dst_d) in enumerate(feats):
                wb = fi % 2
                fb = fi % 2
                if fi == 0:
                    load_w(wt[:, 0, :], ("wt", 0), col0, 128, 0, 128)
                if fi + 1 < len(feats):
                    load_w(wt[:, (fi + 1) % 2, :], ("wt", (fi + 1) % 2), feats[fi + 1][1], 128, 0, 128)
                for tq in range(4):
                    pb = next_ps()
                    S.op("pe", lambda e, pb=pb, wb=wb, tq=tq: _mm_group(
                        e, ps[:, pb, :], [(wtv[wb][:, kc, :], xwov[:, kc, tq * 512:(tq + 1) * 512]) for kc in range(8)]),
                        reads=[("wt", wb), ("xwo", tq)], writes=[("ps", pb)])
                    dst = fo[:, fb, tq * 512:(tq + 1) * 512]
                    dkey = ("fo", fb, tq)
                    ub = ucnt[0] % 2
                    ucnt[0] += 1
                    if kind == "qa":
                        S.op("dve", lambda e, pb=pb, dst=dst: e.tensor_scalar(
                            out=dst, in0=ps[:, pb, :], scalar1=0.125, scalar2=None, op0=ALU.mult),
                            reads=[("ps", pb)], writes=[dkey])
                    elif kind == "qn":
                        norm_epilogue(pb, ub, gq8[:, 0:1], "gq8", dst, dkey)
                    else:
                        fn_ = AF.Sigmoid if kind == "sig" else AF.Silu
                        S.op("act", lambda e, pb=pb, dst=dst, fn_=fn_: e.activation(out=dst, in_=ps[:, pb, :], func=fn_),
                             reads=[("ps", pb)], writes=[dkey])
                S.op("pool", lambda e, fb=fb, dst_d=dst_d: e.dma_start(out=dst_d, in_=fo[:, fb, :]),
                     reads=[("fo", fb, tq) for tq in range(4)], writes=[("fd", fi)], dma_key=("fo", fb))

            for g in range(2):
                load_w(Wk[:], "Wk", C_KB + g * 64, 64, g * 128, 512, dup_to=g * 128 + 64)
            load_w(Wk[:], "Wk", C_VB, 128, 256, 512)
            cxw = {0: prep_chunk(T.hTs, 0, 512, 0)}
            for tile in range(2):
                for tq in range(4):
                    if tile == 0:
                        b = (tq) % 2
                        xwv = cxw[tq]
                        xkey = ("xw", b)
                        if tq + 1 < 4:
                            cxw[tq + 1] = prep_chunk(T.hTs, (tq + 1) * 512, 512, (tq + 1) % 2)
                    else:
                        xwv = xwov[:, :, tq * 512:(tq + 1) * 512]
                        xkey = ("xwo", tq)
                    for g in range(2):
                        pb = next_ps()
                        ub = ucnt[0] % 2
                        ucnt[0] += 1
                        S.op("pe", lambda e, pb=pb, g=g, xwv=xwv: _mm_group(
                            e, ps[:, pb, :], [(Wkv[:, kc, g * 128:(g + 1) * 128], xwv[:, kc, :]) for kc in range(8)]),
                            reads=["Wk", xkey], writes=[("ps", pb)])
                        norm_epilogue(pb, ub, gt[:, 1:2], "gt", kto[:, ub, 0:512], ("kto", ub, 0))
                        S.op("pool", lambda e, ub=ub, g=g, tile=tile, tq=tq: e.dma_start(
                            out=T.kh_d[g, :, tile * TOWN + tq * 512: tile * TOWN + (tq + 1) * 512], in_=kto[:, ub, 0:512]),
                            reads=[("kto", ub, 0)], writes=[("kh_d", g, tile, tq)], dma_key=("kto", ub, 0))
                    vb_b = (tile * 4 + tq) % 2
                    for bl in range(4):
                        pb = next_ps()
                        S.op("pe", lambda e, pb=pb, bl=bl, xwv=xwv: _mm_group(
                            e, ps[:, pb, 0:128], [(xwv[:, kc, bl * 128:(bl + 1) * 128], Wkv[:, kc, 256:384]) for kc in range(8)]),
                            reads=["Wk", xkey], writes=[("ps", pb)])
                        S.op("dve", lambda e, pb=pb, bl=bl, vb_b=vb_b: e.tensor_copy(
                            out=vbo[:, vb_b, bl * 128:(bl + 1) * 128], in_=ps[:, pb, 0:128]),
                            reads=[("ps", pb)], writes=[("vbo", vb_b)])
                    s0 = (tile * 16 + tq * 4) * 128
                    S.op("pool", lambda e, vb_b=vb_b, s0=s0: e.dma_start(out=T.vb_d[:, s0:s0 + 512], in_=vbo[:, vb_b, :]),
                         reads=[("vbo", vb_b)], writes=[("vb_d", tile, tq)], dma_key=("vbo", vb_b))

            with nc.Block() as block:
                S.run(nc, block, T.G)


def phase_sb(nc, T):
    S = Sched()
    import contextlib
    with contextlib.ExitStack() as st:
        sb = lambda name, shape, dtype: st.enter_context(nc.sbuf_tensor(name, shape, dtype))
        pt = lambda name, shape, dtype: st.enter_context(nc.psum_tensor(name, shape, dtype))
        NB = 4
        NR = 5
        NZ = 4
        cb = sb("b_cb", [128, 1024], BF16)
        mb = sb("b_mb", [128, 9 * 1024], BF16)
        KTr = sb("b_KTr", [128, NR, 2 * NB * 128], BF16)
        Vr = sb("b_Vr", [128, NR, NB * 256], BF16)
        QT = sb("b_QT", [128, 8, TOWN], BF16)
        zs = sb("b_zs", [128, NZ, NB * 1024], F32)
        Lp = sb("b_Lp", [128, 3, NB * 1024], BF16)
        W = sb("b_W", [128, 3, NB * 1024], BF16)
        oo = sb("b_oo", [128, 2, 512], BF16)
        zA = pt("b_zA", [128, 2, 1024], F32)
        B = pt("b_B", [128, 1024], F32)
        O2 = pt("b_O", [128, 2, 512], F32)
        ident = cb[:, 0:128]
        negTri = cb[:, 128:256]
        negRest = cb[:, 256:384]
        _load_consts(S, T, None, cb)
        for mt in range(3):
            S.op("sync", lambda e, mt=mt: e.dma_start(out=mb[:, mt * 3072:(mt + 1) * 3072], in_=T.sbm[:, mt * 3072:(mt + 1) * 3072]),
                 writes=[("mb", mt)], dma_key=("mb", mt))
        S.op("dve", lambda e: e.memset(QT[:], 0.0), writes=["QTz"])
        qkeys = []
        for hf in range(2):
            for pp in range(2):
                for rh in range(2):
                    qi = (hf * 2 + pp) * 2 + rh
                    S.op("sync", lambda e, pp=pp, hf=hf, rh=rh, qi=qi: e.dma_start(
                        out=QT[rh * 64:(rh + 1) * 64, qi, :], in_=T.qt_d[2 * hf + pp, rh * 64:(rh + 1) * 64, :]),
                        reads=["QTz"], writes=[("QT", qi)], dma_key=("QT", qi))
                    qkeys.append(("QT", qi))
        vsrc = T.v_d.rearrange("p (blk c) -> p blk c", c=512)
        steps = []
        for hf in range(1 if T.mini else 2):
            for ip in range(1 if T.mini else 8):
                for kb in range(8 * ip + 8, -1, -1):
                    act = [1] if kb > 8 * ip + 4 else [0, 1]
                    kr = kb - (8 * ip + 1)
                    mt = kr if kr >= 0 else (8 if kb == 0 else None)
                    first = {c: (kb == (8 * ip + 8 if c == 1 else 8 * ip + 4)) for c in act}
                    steps.append(dict(hf=hf, ip=ip, kb=kb, act=act, mt=mt, first=first, last=(kb == 0)))
        n = len(steps)
        batches = []
        for si, stp in enumerate(steps):
            f0 = steps[batches[-1][0]] if batches else None
            if batches and len(batches[-1]) < NB and f0["hf"] == stp["hf"] and f0["ip"] == stp["ip"] and f0["act"] == stp["act"]:
                batches[-1].append(si)
            else:
                batches.append([si])
        bof = {}
        for bi, bt in enumerate(batches):
            for j, si in enumerate(bt):
                bof[si] = (bi, j)
        nbt = len(batches)

        def cols(stp):
            return (512, 1024) if stp["act"] == [1] else (0, 1024)

        def bview(t, slot, bi):
            bt = batches[bi]
            lo, hi = cols(steps[bt[0]])
            return t[:, slot, :].rearrange("p (j c) -> p j c", c=1024)[:, 0:len(bt), lo:hi]

        loaded = set()

        def e_load(bi):
            loaded.add(bi)
            bt = batches[bi]
            hf = steps[bt[0]]["hf"]
            kb_lo, kb_hi = steps[bt[-1]]["kb"], steps[bt[0]]["kb"]
            nj = kb_hi - kb_lo + 1
            slot = bi % NR
            ksrc = T.kt_d[2 * hf:2 * hf + 2, :, kb_lo * 128:(kb_hi + 1) * 128].rearrange("g p t -> p g t")
            kdst = KTr[:, slot, :].rearrange("p (g t) -> p g t", g=2)[:, :, 0:nj * 128]
            S.op("sync", lambda e: e.dma_start(out=kdst, in_=ksrc), writes=[("KTr", slot)], dma_key=("KTr", slot))
            vdst = Vr[:, slot, 0:nj * 256].rearrange("p (j c) -> p j c", c=256)
            S.op("sync", lambda e: e.dma_start(out=vdst, in_=vsrc[:, kb_lo:kb_hi + 1, hf * 256:(hf + 1) * 256]),
                 writes=[("Vr", slot)], dma_key=("Vr", slot))

        def e_zA(si):
            stp = steps[si]
            zb = si % 2
            bi, j = bof[si]
            slot = bi % NR
            jb = stp["kb"] - steps[batches[bi][-1]]["kb"]

            def f(e):
                ins = None
                for c in stp["act"]:
                    kc_ = 2 * stp["ip"] + c
                    for hl in range(4):
                        pp, rh = hl // 2, hl % 2
                        qi = (stp["hf"] * 2 + pp) * 2 + rh
                        ins = e.matmul(zA[:, zb, c * 512 + hl * 128:c * 512 + (hl + 1) * 128],
                                       lhsT=KTr[:, slot, pp * NB * 128 + jb * 128:pp * NB * 128 + (jb + 1) * 128],
                                       rhs=QT[:, qi, kc_ * 128:(kc_ + 1) * 128],
                                       start=(hl == 0), stop=True, skip_group_check=True)
                    if stp["mt"] is not None:
                        m0 = stp["mt"] * 1024 + c * 512
                        ins = e.matmul(zA[:, zb, c * 512:(c + 1) * 512], lhsT=ident, rhs=mb[:, m0:m0 + 512],
                                       start=False, stop=True, skip_group_check=True)
                return ins
            S.op("pe", f, reads=[("KTr", slot)] + qkeys + [("mb", 0), ("mb", 1), ("mb", 2), "cb"], writes=[("zA", zb)])

        def e_zs(si):
            lo, hi = cols(steps[si])
            bi, j = bof[si]
            S.op("dve", lambda e: e.tensor_copy(out=zs[:, bi % NZ, j * 1024 + lo:j * 1024 + hi], in_=zA[:, si % 2, lo:hi]),
                 reads=[("zA", si % 2)], writes=[("zs", bi % NZ, j)])

        def e_Lp(bi):
            nj = len(batches[bi])
            S.op("act", lambda e: e.activation(out=bview(Lp, bi % 3, bi), in_=bview(zs, bi % NZ, bi), func=AF.Softplus),
                 reads=[("zs", bi % NZ, j) for j in range(nj)], writes=[("Lp", bi % 3)])

        def e_cum(si, c, lhs, first_ok):
            stp = steps[si]
            bi, j = bof[si]
            S.op("pe", lambda e: e.matmul(B[:, c * 512:(c + 1) * 512], lhsT=lhs,
                                          rhs=Lp[:, bi % 3, j * 1024 + c * 512:j * 1024 + (c + 1) * 512],
                                          start=(first_ok and stp["first"][c]), stop=True, skip_group_check=True),
                 reads=[("Lp", bi % 3), "cb"], writes=[("B", c)])

        def e_arg(si, c):
            bi, j = bof[si]
            zv = zs[:, bi % NZ, j * 1024 + c * 512:j * 1024 + (c + 1) * 512]
            S.op("dve", lambda e: e.tensor_tensor(out=zv, in0=B[:, c * 512:(c + 1) * 512], in1=zv, op=ALU.add),
                 reads=[("B", c), ("zs", bi % NZ, j)], writes=[("zs", bi % NZ, j)])

        def e_W(bi):
            nj = len(batches[bi])
            S.op("act", lambda e: e.activation(out=bview(W, bi % 3, bi), in_=bview(zs, bi % NZ, bi), func=AF.Exp),
                 reads=[("zs", bi % NZ, j) for j in range(nj)], writes=[("W", bi % 3)])

        def e_AV(si):
            stp = steps[si]
            bi, j = bof[si]
            slot = bi % NR
            jb = stp["kb"] - steps[batches[bi][-1]]["kb"]
            hf = stp["hf"]

            ob_ = stp["ip"] % 2

            def f(e):
                ins = None
                for c in stp["act"]:
                    for hl in range(4):
                        pp, rh = hl // 2, hl % 2
                        ins = e.matmul(O2[rh * 64:(rh + 1) * 64, ob_, c * 256 + pp * 128:c * 256 + (pp + 1) * 128],
                                       lhsT=Vr[:, slot, jb * 256 + hl * 64:jb * 256 + (hl + 1) * 64],
                                       rhs=W[:, bi % 3, j * 1024 + c * 512 + hl * 128:j * 1024 + c * 512 + (hl + 1) * 128],
                                       start=(c == 1 and stp["first"][1] and pp == 0), stop=True,
                                       tile_position=(0, rh * 64), skip_group_check=True)
                return ins
            S.op("pe", f, reads=[("W", bi % 3), ("Vr", slot)], writes=[("O", ob_)])
            if stp["last"]:
                ob = stp["ip"] % 2
                S.op("dve", lambda e: e.tensor_copy(out=oo[:, ob, :], in_=O2[:, ob, :]), reads=[("O", ob)], writes=[("oo", ob)])
                for c in range(2):
                    kc_ = 2 * stp["ip"] + c
                    dst = T.oa_d[2 * hf:2 * hf + 2, :, kc_ * 128:(kc_ + 1) * 128].rearrange("g p t -> p g t")
                    src = oo[:, ob, c * 256:(c + 1) * 256].rearrange("p (g t) -> p g t", g=2)
                    S.op("pool", lambda e, dst=dst, src=src: e.dma_start(out=dst, in_=src),
                         reads=[("oo", ob)], writes=[("oa_d", hf, kc_)], dma_key=("oo", ob, c))

        for bi in range(min(NR, nbt)):
            e_load(bi)
        for bi in range(min(2, nbt)):
            for si in batches[bi]:
                e_zA(si)
                e_zs(si)
        e_Lp(0)
        lp_next, w_next, av_next = 1, 0, 0
        av_state = {"b": 0, "j": 0}

        def flush_av(bi, nmax):
            n_ = 0
            while n_ < nmax and av_state["b"] < w_next and av_state["b"] <= bi - 1:
                b_ = av_state["b"]
                e_AV(batches[b_][av_state["j"]])
                av_state["j"] += 1
                n_ += 1
                if av_state["j"] == len(batches[b_]):
                    if b_ + NR < nbt:
                        e_load(b_ + NR)
                    av_state["b"] += 1
                    av_state["j"] = 0

        for bi in range(nbt):
            if bi + 2 < nbt:
                while (bi + 2) not in loaded:
                    flush_av(bi + 2 - NR + 1, 1)
            nxt = list(batches[bi + 2]) if bi + 2 < nbt else []
            bt = batches[bi]
            for idx, si in enumerate(bt):
                acts = steps[si]["act"]
                if idx == 0:
                    for c in acts:
                        e_cum(si, c, negTri, True)
                if nxt:
                    e_zA(nxt[0])
                for c in acts:
                    e_arg(si, c)
                if nxt:
                    e_zs(nxt.pop(0))
                for c in acts:
                    e_cum(si, c, negRest, False)
                    if idx + 1 < len(bt):
                        e_cum(bt[idx + 1], c, negTri, True)
                flush_av(bi - 1, 2)
            for sj in nxt:
                e_zA(sj)
                e_zs(sj)
            if bi % 2 == 0:
                while lp_next < nbt and lp_next <= bi + 2:
                    e_Lp(lp_next)
                    lp_next += 1
            else:
                while w_next <= bi:
                    e_W(w_next)
                    w_next += 1
        while w_next < nbt:
            e_W(w_next)
            w_next += 1
        flush_av(nbt + 1, 10 ** 6)
        with nc.Block() as block:
            S.run(nc, block, T.G)


def phase_swa(nc, T):
    S = Sched()
    import contextlib
    with contextlib.ExitStack() as st:
        sb = lambda name, shape, dtype: st.enter_context(nc.sbuf_tensor(name, shape, dtype))
        pt = lambda name, shape, dtype: st.enter_context(nc.psum_tensor(name, shape, dtype))
        cf = sb("w_cf", [128, 1024], F32)
        cb = sb("w_cb", [128, 1024], BF16)
        QH = sb("w_QH", [128, 8, TOWN], BF16)
        KH = sb("w_KH", [128, 2, 2 * TOWN], BF16)
        VB = sb("w_VB", [128, 32 * 128], BF16)
        BM = sb("w_BM", [128, 2, 2048], F32)
        swt = sb("w_swt", [128, 512], F32)
        sk = sb("w_sk", [128, 4], F32)
        esk = sb("w_esk", [128, 4], F32)
        lgs = sb("w_lgs", [128, 2, 1024], F32)
        P = sb("w_P", [128, 2, 1024], BF16)
        dn = sb("w_dn", [128, 2, 256], F32)
        ObT = sb("w_ObT", [128, 4, TOWN], BF16)
        lg = pt("w_lg", [128, 2, 1024], F32)
        OD = pt("w_OD", [128, 2, 512], F32)
        ones64 = cb[:, 384:448]
        _load_consts(S, T, cf, cb)
        S.op("dve", lambda e: e.memset(QH[:], 0.0), writes=["QHz"])
        p4_jobs = [(T.f_Wb, "Wb", T.w_br, d0) for d0 in range(0, 1024, 256)] + [(T.f_Wo, "Wo", T.w_out, d0) for d0 in range(0, 1024, 256)]

        def emit_p4_weight(i):
            wdst, wkey, wsrc, d0 = p4_jobs[i]
            b = i % 2
            src = wsrc.rearrange("(kc p) c -> p kc c", p=128)[:, :, d0:d0 + 256]
            dstage = T.f_wst[:, b, :].rearrange("p (kc c) -> p kc c", kc=8)
            S.op("sync", lambda e: e.dma_start(out=dstage, in_=src), writes=[("f_wst", b)], dma_key=("f_wst", b))
            dv = wdst[:].rearrange("p (kc c) -> p kc c", kc=8)[:, :, d0:d0 + 256]
            S.op("pool", lambda e: e.tensor_copy(out=dv, in_=dstage), reads=[("f_wst", b)], writes=[wkey])
        for pq in range(4):
            for rh in range(2):
                S.op("sync", lambda e, pq=pq, rh=rh: e.dma_start(out=QH[rh * 64:(rh + 1) * 64, pq * 2 + rh, :],
                                                                 in_=T.qh_d[pq, rh * 64:(rh + 1) * 64, :]),
                     reads=["QHz"], writes=[("QH", pq, rh)], dma_key=("QH", pq, rh))
        for g in range(2):
            S.op("sync", lambda e, g=g: e.dma_start(out=KH[:, g, :], in_=T.kh_d[g]), writes=[("KH", g)], dma_key=("KH", g))
        S.op("sync", lambda e: e.dma_start(out=VB[:], in_=T.vb_d[:, :]), writes=["VB"], dma_key="VB")
        S.op("sync", lambda e: e.dma_start(out=BM[:, 0, :], in_=T.biasG[:, :]), writes=["BM0"], dma_key="BM0")
        S.op("sync", lambda e: e.dma_start(out=BM[:, 1, :], in_=T.biasG[:, :]), writes=["BM1"], dma_key="BM1")
        S.op("sync", lambda e: e.dma_start(out=swt[:], in_=T.swm[:, :]), writes=["swt"], dma_key="swt")
        S.op("sync", lambda e: e.dma_start(out=sk[:], in_=T.sinkP[:, :]), writes=["sk"], dma_key="sk")
        S.op("act", lambda e: e.activation(out=esk[:], in_=sk[:], func=AF.Exp), reads=["sk"], writes=["esk"])
        for var in range(2):
            def f(e, var=var):
                ins = None
                for h in range(8):
                    ins = e.tensor_tensor(out=BM[:, var, h * 256:(h + 1) * 256], in0=BM[:, var, h * 256:(h + 1) * 256],
                                          in1=swt[:, var * 256:(var + 1) * 256], op=ALU.add)
                return ins
            S.op("dve", f, reads=["BM%d" % var, "swt"], writes=["BM%d" % var])
        units = [(k, g) for k in range(NOWN) for g in range(2)]

        def stage1(ui):
            k, g = units[ui]
            var = 1 if k == 0 else 0
            lb = ui % 2

            def f_qk(e, k=k, g=g, lb=lb):
                ins = None
                for hl in range(4):
                    h = 4 * g + hl
                    pq, rh = h // 2, h % 2
                    for tile in range(2):
                        ins = e.matmul(lg[:, lb, (hl * 2 + tile) * 128:(hl * 2 + tile + 1) * 128],
                                       lhsT=KH[:, g, tile * TOWN + k * 128:tile * TOWN + (k + 1) * 128],
                                       rhs=QH[:, pq * 2 + rh, k * 128:(k + 1) * 128], start=True, stop=True)
                return ins
            S.op("pe", f_qk, reads=[("KH", 0), ("KH", 1)] + [("QH", i, r) for i in range(4) for r in range(2)], writes=[("lg", lb)])
            S.op("dve", lambda e, lb=lb, g=g, var=var: e.tensor_tensor(
                out=lgs[:, lb, :], in0=lg[:, lb, :], in1=BM[:, var, g * 1024:(g + 1) * 1024], op=ALU.add),
                reads=[("lg", lb), "BM%d" % var], writes=[("lgs", lb)])
            S.op("act", lambda e, lb=lb: e.activation(out=P[:, lb, :], in_=lgs[:, lb, :], func=AF.Exp),
                 reads=[("lgs", lb)], writes=[("P", lb)])

        def stage2(ui):
            k, g = units[ui]
            lb = ui % 2

            def f_pv(e, k=k, g=g, lb=lb):
                ins = None
                for hl in range(4):
                    pql, rh = hl // 2, hl % 2
                    for tile in range(2):
                        slot = tile * 16 + k
                        rhs = P[:, lb, (hl * 2 + tile) * 128:(hl * 2 + tile + 1) * 128]
                        e.matmul(OD[rh * 64:(rh + 1) * 64, lb, pql * 128:(pql + 1) * 128],
                                 lhsT=VB[:, slot * 128 + g * 64:slot * 128 + (g + 1) * 64], rhs=rhs,
                                 start=(tile == 0 and pql == 0), stop=True, tile_position=(0, rh * 64), skip_group_check=True)
                        ins = e.matmul(OD[rh * 64:(rh + 1) * 64, lb, 256 + pql * 128:256 + (pql + 1) * 128],
                                       lhsT=ones64, rhs=rhs,
                                       start=False, stop=True, tile_position=(0, rh * 64), skip_group_check=True)
                return ins
            S.op("pe", f_pv, reads=[("P", lb), "VB", "cb"], writes=[("OD", lb)])

            def f_den(e, g=g, lb=lb):
                ins = None
                for pql in range(2):
                    pq = 2 * g + pql
                    ins = e.tensor_scalar(out=dn[:, lb, pql * 128:(pql + 1) * 128], in0=OD[:, lb, 256 + pql * 128:256 + (pql + 1) * 128],
                                          scalar1=esk[:, pq:pq + 1], scalar2=None, op0=ALU.add)
                return ins
            S.op("dve", f_den, reads=[("OD", lb), "esk"], writes=[("dn", lb)])
            S.op("act", lambda e, lb=lb: e.activation(out=dn[:, lb, :], in_=dn[:, lb, :], func=AF.Ln), reads=[("dn", lb)], writes=[("dn", lb)])
            S.op("act", lambda e, lb=lb: e.activation(out=dn[:, lb, :], in_=dn[:, lb, :], func=AF.Exp, scale=-1.0),
                 reads=[("dn", lb)], writes=[("dn", lb)])

            def f_nrm(e, k=k, g=g, lb=lb):
                ins = None
                for pql in range(2):
                    pq = 2 * g + pql
                    ins = e.tensor_tensor(out=ObT[:, pq, k * 128:(k + 1) * 128], in0=OD[:, lb, pql * 128:(pql + 1) * 128],
                                          in1=dn[:, lb, pql * 128:(pql + 1) * 128], op=ALU.mult)
                return ins
            S.op("dve", f_nrm, reads=[("OD", lb), ("dn", lb)], writes=["ObT"])

        for i in range(len(p4_jobs)):
            emit_p4_weight(i)
        T.p4w_loaded = True
        stage1(0)
        for ui in range(len(units)):
            if ui + 1 < len(units):
                stage1(ui + 1)
            stage2(ui)
        for pq in range(4):
            S.op("sync", lambda e, pq=pq: e.dma_start(out=T.ob_d[pq], in_=ObT[:, pq, :]), reads=["ObT"], writes=[("ob_d", pq)],
                 dma_key=("ObT", pq))
        with nc.Block() as block:
            S.run(nc, block, T.G)


def phase4(nc, T):
    S = Sched()
    import contextlib
    with contextlib.ExitStack() as st:
        sb = lambda name, shape, dtype: st.enter_context(nc.sbuf_tensor(name, shape, dtype))
        pt = lambda name, shape, dtype: st.enter_context(nc.psum_tensor(name, shape, dtype))
        wst, Wb, Wo = T.f_wst, T.f_Wb, T.f_Wo
        oab = sb("f_oab", [128, 2, 8 * 512], BF16)
        zab = sb("f_zab", [128, 2, 8 * 512], BF16)
        br = sb("f_br", [128, 2, 8 * 512], BF16)
        gg = sb("f_gg", [128, 2, 16 * 512], BF16)
        m1 = sb("f_m1", [128, 2, 512], F32)
        m2 = sb("f_m2", [128, 2, 512], F32)
        mg = sb("f_mg", [128, 2, 8 * 512], BF16)
        xt = sb("f_xt", [128, 2, 4 * 1024], F32)
        py = pt("f_py", [128, 4, 512], F32)
        po = pt("f_po", [128, 2, 512], F32)
        cntw = [0]

        def load_w(dst, dkey, src_ap, ncols):
            done = 0
            while done < ncols:
                n = 256
                b = cntw[0] % 2
                cntw[0] += 1
                src = src_ap.rearrange("(kc p) c -> p kc c", p=128)[:, :, done:done + n]
                dstage = wst[:, b, :].rearrange("p (kc c) -> p kc c", kc=8)
                S.op("sync", lambda e, s_=src, d=dstage: e.dma_start(out=d, in_=s_), writes=[("wst", b)], dma_key=("wst", b))
                dv = dst.rearrange("p (kc c) -> p kc c", kc=8)[:, :, done:done + n]
                S.op("dve", lambda e, s_=dstage, d=dv: e.tensor_copy(out=d, in_=s_), reads=[("wst", b)], writes=[dkey])
                done += n
        if not T.p4w_loaded:
            load_w(Wb[:], "Wb", T.w_br, 1024)
            load_w(Wo[:], "Wo", T.w_out, 1024)
        Wbv = Wb[:].rearrange("p (kc c) -> p kc c", kc=8)
        Wov = Wo[:].rearrange("p (kc c) -> p kc c", kc=8)
        for tq in range(4):
            b = tq % 2
            tsl = slice(tq * 512, (tq + 1) * 512)
            oav = oab[:, b, :].rearrange("p (c t) -> p c t", c=8)
            zav = zab[:, b, :].rearrange("p (c t) -> p c t", c=8)
            brv = br[:, b, :].rearrange("p (c t) -> p c t", c=8)
            ggv = gg[:, b, :].rearrange("p (c t) -> p c t", c=16)
            mgv = mg[:, b, :].rearrange("p (c t) -> p c t", c=8)
            xtv = xt[:, b, :].rearrange("p (tb e) -> p tb e", tb=4)
            otv = xtv
            for (srcd, dstv, off, key) in ((T.oa_d, oav, 0, "oa"), (T.ob_d, oav, 4, "ob"), (T.zas_d, zav, 0, "za"), (T.zbs_d, zav, 4, "zb")):
                S.op("sync", lambda e, srcd=srcd, dstv=dstv, off=off, tsl=tsl: e.dma_start(
                    out=dstv[:, off:off + 4, :], in_=srcd[:, :, tsl].rearrange("f p t -> p f t")),
                    writes=[(key, b)], dma_key=(key, b))
            S.op("sync", lambda e, ggv=ggv, tsl=tsl: e.dma_start(out=ggv, in_=T.g_d[:, :, tsl].rearrange("f p t -> p f t")),
                 writes=[("gg", b)], dma_key=("gg", b))
            S.op("sync", lambda e, xtv=xtv, tq=tq: e.dma_start(out=xtv, in_=T.xo[tq * 512:(tq + 1) * 512, :].rearrange("(tb p) e -> p tb e", p=128)),
                 writes=[("xt", b)], dma_key=("xt", b))
            S.op("pool", lambda e, b=b: e.tensor_tensor(out=br[:, b, :], in0=oab[:, b, :], in1=zab[:, b, :], op=ALU.mult),
                 reads=[("oa", b), ("ob", b), ("za", b), ("zb", b)], writes=[("br", b)])
            for dc in range(8):
                yb = dc % 2
                for brn in range(2):
                    S.op("pe", lambda e, yb=yb, brn=brn, dc=dc, brv=brv: _mm_group(
                        e, py[:, yb * 2 + brn, :], [(Wbv[:, brn * 4 + cc, dc * 128:(dc + 1) * 128], brv[:, brn * 4 + cc, :]) for cc in range(4)]),
                        reads=["Wb", ("br", b)], writes=[("py", yb, brn)])
                S.op("dve", lambda e, yb=yb, dc=dc, ggv=ggv: e.tensor_tensor(out=m1[:, yb, :], in0=py[:, yb * 2, :], in1=ggv[:, dc, :], op=ALU.mult),
                     reads=[("py", yb, 0), ("gg", b)], writes=[("m1", yb)])
                S.op("dve", lambda e, yb=yb, dc=dc, ggv=ggv: e.tensor_tensor(out=m2[:, yb, :], in0=py[:, yb * 2 + 1, :], in1=ggv[:, 8 + dc, :], op=ALU.mult),
                     reads=[("py", yb, 1), ("gg", b)], writes=[("m2", yb)])
                S.op("pool", lambda e, yb=yb, dc=dc, mgv=mgv: e.tensor_tensor(out=mgv[:, dc, :], in0=m1[:, yb, :], in1=m2[:, yb, :], op=ALU.add),
                     reads=[("m1", yb), ("m2", yb)], writes=[("mg", b)])
            for tb in range(4):
                for eh in range(2):
                    pb = (tb * 2 + eh) % 2
                    S.op("pe", lambda e, pb=pb, tb=tb, eh=eh, mgv=mgv: _mm_group(
                        e, po[:, pb, :], [(mgv[:, dc, tb * 128:(tb + 1) * 128], Wov[:, dc, eh * 512:(eh + 1) * 512]) for dc in range(8)]),
                        reads=["Wo", ("mg", b)], writes=[("po", pb)])
                    S.op("dve", lambda e, pb=pb, tb=tb, eh=eh, xtv=xtv, otv=otv: e.tensor_tensor(
                        out=otv[:, tb, eh * 512:(eh + 1) * 512], in0=po[:, pb, :], in1=xtv[:, tb, eh * 512:(eh + 1) * 512], op=ALU.add),
                        reads=[("po", pb), ("xt", b)], writes=[("xt", b)])
            S.op("act", lambda e, otv=otv, tq=tq: e.dma_start(out=T.out[tq * 512:(tq + 1) * 512, :].rearrange("(tb p) e -> p tb e", p=128), in_=otv),
                 reads=[("xt", b)], writes=[("out", tq)], dma_key=("ot", b))
        with nc.Block() as block:
            S.run(nc, block, T.G)


def _t5_buckets(rel):
    n = np.maximum(rel, 0)
    max_exact = 16
    large = max_exact + (np.log(np.maximum(n, 1) / max_exact) / np.log(128 / max_exact) * (32 - max_exact)).astype(np.int32)
    large = np.minimum(large, 31)
    return np.where(n < max_exact, n, large).astype(np.int32)


def host_prep(x, meta, rel_bias, norm_w, w_in, q_gain, k_gain, sinks, w_branch, w_out):
    f32 = np.float32
    x = np.asarray(x, f32)
    B = x.shape[0]
    hTs_b = []
    for b in range(B):
        h = np.zeros((LP, D), f32)
        h[112:128] = np.asarray(meta, f32)
        h[128:] = x[b]
        hTs_b.append(np.ascontiguousarray(h.T))
    w_in0 = np.ascontiguousarray(np.asarray(w_in, f32)[0])
    w_br0 = np.ascontiguousarray(np.asarray(w_branch, f32)[0].reshape(1024, D))
    w_out0 = np.ascontiguousarray(np.asarray(w_out, f32)[0])
    nw = np.ascontiguousarray(np.asarray(norm_w, f32)[0].reshape(8, 128).T)
    gains = np.stack([np.tile(np.asarray(q_gain, f32)[0], 2), np.tile(np.asarray(k_gain, f32)[0], 2)], axis=1).astype(f32)
    sk = np.asarray(sinks, f32)[0]
    sinkP = np.zeros((128, 4), f32)
    for pq in range(4):
        sinkP[0:64, pq] = sk[2 * pq]
        sinkP[64:128, pq] = sk[2 * pq + 1]
    s_i = np.arange(128)[:, None]
    t_i = np.arange(128)[None, :]
    rel0 = 128 + t_i - s_i
    rel1 = t_i - s_i
    rb = np.asarray(rel_bias, f32)
    biasG = np.zeros((128, 8, 2, 128), f32)
    for h in range(8):
        biasG[:, h, 0, :] = rb[_t5_buckets(rel0), h]
        biasG[:, h, 1, :] = rb[_t5_buckets(rel1), h]
    vis0 = (rel0 < 128)
    vis1 = (rel1 >= 0)
    swm_gen = np.stack([np.where(vis0, 0.0, NEG), np.where(vis1, 0.0, NEG)], axis=1).astype(f32)
    cst = np.zeros((128, 1024), f32)
    jj = np.arange(128)[:, None]
    ss = np.arange(128)[None, :]
    cst[:, 0:128] = np.eye(128)
    cst[:, 128:256] = np.where(jj >= ss, -1.0, 0.0)
    cst[:, 256:384] = np.where(jj < ss, -1.0, 0.0)
    cst[:, 384:512] = 1.0
    cst[:, 512:576] = 1.0
    cst[:, 640 + 64:768] = 1.0
    cst[0:64, 768:832] = 1.0
    cst[64:128, 832:896] = 1.0
    cst_bf = cst.astype(ml_dtypes.bfloat16)
    in_maps = []
    for c in range(8):
        b, j = c // 4, c % 4
        own = [4 * k + 1 + j for k in range(NOWN)]
        hT = hTs_b[b]
        hTo = np.ascontiguousarray(np.concatenate([hT[:, n * 128:(n + 1) * 128] for n in own], axis=1))
        hTs = np.ascontiguousarray(np.concatenate([hT[:, (n - 1) * 128:n * 128] for n in own], axis=1))
        xo = np.ascontiguousarray(np.concatenate([x[b, (n - 1) * 128:n * 128] for n in own], axis=0))
        swm = np.zeros((128, 2, 2, 128), f32)
        swm[:, 0] = swm_gen
        swm[:, 1] = swm_gen
        if own[0] == 1:
            swm[0:112, 1, 0, :] = NEG
        sbm = np.zeros((128, 9, 2, 4, 128), f32)
        for kr in range(8):
            for cc in range(2):
                d = kr - 4 * cc - j
                if d > 0:
                    sbm[:, kr, cc] = NEG
                elif d == 0:
                    sbm[:, kr, cc] = np.where(s_i < t_i, 0.0, NEG)[:, None, :]
        sbm[0:112, 8] = NEG
        in_maps.append(dict(hT=hT, hTo=hTo, hTs=hTs, xo=xo, w_in=w_in0, w_br=w_br0, w_out=w_out0, nw=nw, gains=gains,
                            sinkP=sinkP, biasG=biasG.reshape(128, 2048), swm=swm.reshape(128, 512), cst=cst_bf,
                            sbm=sbm.reshape(128, 9 * 1024).astype(ml_dtypes.bfloat16)))
    return in_maps


def kernel(x, meta, rel_bias, norm_w, w_in, q_gain, k_gain, sinks, w_branch, w_out):
    in_maps = host_prep(x, meta, rel_bias, norm_w, w_in, q_gain, k_gain, sinks, w_branch, w_out)
    nc = build_program()
    res = run_bass_kernel_spmd(nc, in_maps, core_ids=list(range(8)))
    B = 2
    outp = np.zeros((B, SEQ, D), np.float32)
    for c in range(8):
        b, j = c // 4, c % 4
        o = np.asarray(res.results[c]["out"], np.float32)
        for k in range(NOWN):
            n = 4 * k + 1 + j
            outp[b, (n - 1) * 128:n * 128] = o[k * 128:(k + 1) * 128]
    return outp
```

```python
import numpy as np
import ml_dtypes
import concourse.bass as bass
import concourse.mybir as mybir
from concourse.bass_utils import run_bass_kernel_spmd

F32 = mybir.dt.float32
BF16 = mybir.dt.bfloat16
AF = mybir.ActivationFunctionType
ALU = mybir.AluOpType

D = 1024
SEQ = 8192
LP = SEQ + 128
NBLK = LP // 128
NOWN = 16
TOWN = NOWN * 128
EPS = 1e-6
NEG = -30000.0
C_QA, C_KA, C_VA, C_ZA, C_QB, C_KB, C_VB, C_ZB, C_G = 0, 512, 1024, 1536, 2048, 2560, 2688, 2816, 3328


class Sched:
    def __init__(self):
        self.ops = []
        self.lastw = {}
        self.readers = {}

    def op(self, eng, fn, reads=(), writes=(), dma_key=None):
        idx = len(self.ops)
        deps = set()
        for k in reads:
            w = self.lastw.get(k)
            if w is not None:
                deps.add(w)
        for k in writes:
            w = self.lastw.get(k)
            if w is not None:
                deps.add(w)
            for r in self.readers.get(k, {}).values():
                deps.update(r)
        self.ops.append(dict(eng=eng, fn=fn, deps=deps, dma_key=dma_key))
        for k in reads:
            d = self.readers.setdefault(k, {})
            if dma_key is not None:
                d.setdefault(eng, []).append(idx)
            else:
                d[eng] = [idx]
        for k in writes:
            self.lastw[k] = idx
            self.readers[k] = {}
        return idx

    def run(self, nc, block, G):
        ops = self.ops
        needed = set()
        for o in ops:
            for d in o["deps"]:
                if not (ops[d]["eng"] == "pe" and o["eng"] == "pe"):
                    needed.add(d)
        lastop = {}
        for i, o in enumerate(ops):
            if o["dma_key"] is None:
                lastop[o["eng"]] = i
        needed.update(lastop.values())
        cnt = G.cnt
        prev_final = list(G.final.values())
        for i, o in enumerate(ops):
            if o["dma_key"] is not None:
                k = ("dma", o["dma_key"])
                if k not in G.dma_sems:
                    G.dma_sems[k] = G.pool.pop()
                cnt[k] = cnt.get(k, 0) + 16
                o["sem"] = G.dma_sems[k]
                o["val"] = cnt[k]
                G.final[k] = (o["sem"], o["val"])
            elif i in needed:
                k = o["eng"]
                cnt[k] = cnt.get(k, 0) + 1
                o["sem"] = G.esem[k]
                o["val"] = cnt[k]
                G.final[k] = (o["sem"], o["val"])
            else:
                o["sem"] = None
        end_final = list(G.final.values())

        def emit(engname, e):
            known = {}
            for s_, v in prev_final:
                e.wait_ge(s_, v)
                known[id(s_)] = v
            for i, o in enumerate(ops):
                if o["eng"] != engname:
                    continue
                for d in sorted(o["deps"]):
                    y = ops[d]
                    if y["eng"] == "pe" and engname == "pe":
                        continue
                    s_, v = y["sem"], y["val"]
                    if known.get(id(s_), 0) < v:
                        e.wait_ge(s_, v)
                        known[id(s_)] = v
                ins = o["fn"](e)
                if o["sem"] is not None:
                    ins.then_inc(o["sem"], 16 if o["dma_key"] is not None else 1)
            if engname == "sync":
                for s_, v in end_final:
                    if known.get(id(s_), 0) < v:
                        e.wait_ge(s_, v)

        @block.sync
        def _(e):
            emit("sync", e)

        @block.gpsimd
        def _(e):
            emit("pool", e)

        @block.tensor
        def _(e):
            emit("pe", e)

        @block.vector
        def _(e):
            emit("dve", e)

        @block.scalar
        def _(e):
            emit("act", e)


class Ctx:
    pass


def _mm_group(pe, out, pairs, first=True, last=True):
    n = len(pairs)
    ins = None
    for i, (l, r) in enumerate(pairs):
        ins = pe.matmul(out, lhsT=l, rhs=r, start=(first and i == 0), stop=(last and i == n - 1))
    return ins


def build_program(debug=False, phases="1sw4", mini=False):
    nc = bass.Bass("TRN2", target_bir_lowering=False)
    dt = nc.dram_tensor
    hT = dt("hT", [D, LP], F32, kind="ExternalInput").ap()
    hTo = dt("hTo", [D, TOWN], F32, kind="ExternalInput").ap()
    hTs = dt("hTs", [D, TOWN], F32, kind="ExternalInput").ap()
    xo = dt("xo", [TOWN, D], F32, kind="ExternalInput").ap()
    w_in = dt("w_in", [D, 5376], F32, kind="ExternalInput").ap()
    w_br = dt("w_br", [1024, D], F32, kind="ExternalInput").ap()
    w_out = dt("w_out", [D, D], F32, kind="ExternalInput").ap()
    nw = dt("nw", [128, 8], F32, kind="ExternalInput").ap()
    gains = dt("gains", [128, 2], F32, kind="ExternalInput").ap()
    sinkP = dt("sinkP", [128, 4], F32, kind="ExternalInput").ap()
    biasG = dt("biasG", [128, 2048], F32, kind="ExternalInput").ap()
    swm = dt("swm", [128, 512], F32, kind="ExternalInput").ap()
    cst = dt("cst", [128, 1024], BF16, kind="ExternalInput").ap()
    sbm = dt("sbm", [128, 9 * 1024], BF16, kind="ExternalInput").ap()
    out = dt("out", [TOWN, D], F32, kind="ExternalOutput").ap()
    okind = "ExternalOutput" if debug else "Internal"
    kt_d = dt("kt_d", [4, 128, LP], BF16, kind=okind).ap()
    v_d = dt("v_d", [128, NBLK * 512], BF16, kind=okind).ap()
    qt_d = dt("qt_d", [4, 128, TOWN], BF16, kind=okind).ap()
    qh_d = dt("qh_d", [4, 128, TOWN], BF16, kind=okind).ap()
    kh_d = dt("kh_d", [2, 128, 2 * TOWN], BF16, kind=okind).ap()
    vb_d = dt("vb_d", [128, 32 * 128], BF16, kind=okind).ap()
    zas_d = dt("zas_d", [4, 128, TOWN], BF16, kind=okind).ap()
    zbs_d = dt("zbs_d", [4, 128, TOWN], BF16, kind=okind).ap()
    g_d = dt("g_d", [16, 128, TOWN], BF16, kind=okind).ap()
    oa_d = dt("oa_d", [4, 128, TOWN], BF16, kind=okind).ap()
    ob_d = dt("ob_d", [4, 128, TOWN], BF16, kind=okind).ap()

    T = Ctx()
    T.__dict__.update(locals())
    import contextlib
    with contextlib.ExitStack() as gst, nc.allow_low_precision(reason="bf16 matmul operands by design"):
        G = Ctx()
        G.esem = {k: gst.enter_context(nc.semaphore("s_" + k)) for k in ("pe", "act", "dve", "pool", "sync")}
        G.pool = [gst.enter_context(nc.semaphore("dq%d" % i)) for i in range(90)]
        G.dma_sems = {}
        G.cnt = {}
        G.final = {}
        T.G = G
        T.phases = phases
        T.mini = mini
        if "1" in phases:
            phase1(nc, T)
        if "s" in phases:
            phase_sb(nc, T)
        with contextlib.ExitStack() as wst4:
            T.f_wst = wst4.enter_context(nc.sbuf_tensor("f_wst", [128, 2, 8 * 256], F32))
            T.f_Wb = wst4.enter_context(nc.sbuf_tensor("f_Wb", [128, 8 * 1024], BF16))
            T.f_Wo = wst4.enter_context(nc.sbuf_tensor("f_Wo", [128, 8 * 1024], BF16))
            T.p4w_loaded = False
            if "w" in phases:
                phase_swa(nc, T)
            if "4" in phases:
                phase4(nc, T)
    return nc


def _load_consts(S, T, cf, cb):
    S.op("sync", lambda e: e.dma_start(out=cb[:], in_=T.cst[:, :]), writes=["cb"], dma_key="cb")


def _rstd(S, src_key, src, dst_key, dst, tmp_key, tmp, scale):
    S.op("act", lambda e: e.activation(out=tmp, in_=src, func=AF.Ln, bias=T_EPS[0], scale=scale),
         reads=[src_key, "epsb"], writes=[tmp_key])
    S.op("act", lambda e: e.activation(out=dst, in_=tmp, func=AF.Exp, scale=-0.5), reads=[tmp_key], writes=[dst_key])


T_EPS = [None]


def phase1(nc, T):
    S = Sched()
    import contextlib
    with contextlib.ExitStack() as st:
        if True:
            sb = lambda name, shape, dtype: st.enter_context(nc.sbuf_tensor(name, shape, dtype))
            pt = lambda name, shape, dtype: st.enter_context(nc.psum_tensor(name, shape, dtype))
            sm = lambda name: st.enter_context(nc.semaphore(name))
            cf = sb("cf", [128, 1024], F32)
            cb = sb("cb", [128, 1024], BF16)
            nwt = sb("nwt", [128, 8], F32)
            gt = sb("gt", [128, 2], F32)
            gq8 = sb("gq8", [128, 1], F32)
            epsb = sb("epsb", [128, 1], F32)
            xs = sb("xs", [128, 2, 8 * 512], F32)
            xw = sb("xw", [128, 2, 8 * 512], BF16)
            sq = sb("sq", [128, 2, 8 * 512], BF16)
            Rl = sb("Rl", [128, 512], F32)
            R = sb("R", [128, 2, 512], F32)
            Rtl = sb("Rtl", [128, 4], F32)
            Rt = sb("Rt", [128, 2, 4], F32)
            wst = sb("wst", [128, 2, 8 * 256], F32)
            Wk = sb("Wk", [128, 8 * 512], BF16)
            Wv = sb("Wv", [128, 8 * 512], BF16)
            wt = sb("wt", [128, 2, 8 * 128], BF16)
            kto = sb("kto", [128, 2, 4 * 512], BF16)
            vo = sb("vo", [128, 2, 4 * 512], BF16)
            xwo = sb("xwo", [128, 8 * TOWN], BF16)
            Ro = sb("Ro", [128, TOWN], F32)
            Rto = sb("Rto", [128, 16], F32)
            u = sb("u", [128, 2, 512], F32)
            ex = sb("ex", [128, 2, 512], F32)
            sq2 = sb("sq2", [128, 2, 512], BF16)
            fo = sb("fo", [128, 2, TOWN], BF16)
            vbo = sb("vbo", [128, 2, 512], BF16)
            ps = pt("ps", [128, 2, 512], F32)
            psr = pt("psr", [128, 512], F32)
            pst = pt("pst", [128, 4], F32)
            psn = pt("psn", [128, 2, 512], F32)
            ones = cb[:, 384:512]
            onescol = cb[:, 384:385]
            bdiag = cb[:, 768:896]
            _load_consts(S, T, cf, cb)
            S.op("sync", lambda e: e.dma_start(out=nwt[:], in_=T.nw[:, :]), writes=["nwt"], dma_key="nwt")
            S.op("sync", lambda e: e.dma_start(out=gt[:], in_=T.gains[:, :]), writes=["gt"], dma_key="gt")
            S.op("dve", lambda e: e.memset(epsb[:], EPS), writes=["epsb"])
            S.op("dve", lambda e: e.tensor_scalar(out=gq8[:], in0=gt[:, 0:1], scalar1=0.125, scalar2=None, op0=ALU.mult),
                 reads=["gt"], writes=["gq8"])
            T_EPS[0] = epsb[:, 0:1]

            def load_w(dst, dkey, col0, ncols, dcol0, dstride, dup_to=None, src_ap=None):
                done = 0
                nload = [0]
                while done < ncols:
                    n = min(256, ncols - done)
                    b = load_w.cnt % 2
                    load_w.cnt += 1
                    c0 = col0 + done
                    src = (T.w_in if src_ap is None else src_ap).rearrange("(kc p) c -> p kc c", p=128)[:, :, c0:c0 + n]
                    dstage = wst[:, b, 0:8 * n].rearrange("p (kc c) -> p kc c", kc=8)
                    S.op("sync", lambda e, s=src, d=dstage: e.dma_start(out=d, in_=s), writes=[("wst", b)], dma_key=("wst", b))
                    for tgt0 in ([dcol0] if dup_to is None else [dcol0, dup_to]):
                        dv = dst.rearrange("p (kc c) -> p kc c", kc=8)[:, :, tgt0 + done:tgt0 + done + n]
                        S.op("dve", lambda e, s=dstage, d=dv: e.tensor_copy(out=d, in_=s), reads=[("wst", b)], writes=[dkey])
                    done += n
            load_w.cnt = 0

            def prep_chunk(src_ap, t0, Tn, b, xw_dst=None, xw_key=None):
                xsv = xs[:, b, 0:8 * Tn].rearrange("p (kc t) -> p kc t", kc=8)
                xwv = xw[:, b, 0:8 * Tn].rearrange("p (kc t) -> p kc t", kc=8) if xw_dst is None else xw_dst
                xwk = ("xw", b) if xw_key is None else xw_key
                sqv = sq[:, b, 0:8 * Tn].rearrange("p (kc t) -> p kc t", kc=8)
                src = src_ap.rearrange("(kc p) t -> p kc t", p=128)[:, :, t0:t0 + Tn]
                S.op("sync", lambda e: e.dma_start(out=xsv, in_=src), writes=[("xs", b)], dma_key=("xs", b))
                S.op("act", lambda e: e.activation(out=sq[:, b, 0:8 * Tn], in_=xs[:, b, 0:8 * Tn], func=AF.Square),
                     reads=[("xs", b)], writes=[("sq", b)])
                S.op("pe", lambda e: _mm_group(e, psr[:, 0:Tn], [(ones, sqv[:, kc, :]) for kc in range(8)]),
                     reads=[("sq", b), "cb"], writes=["psr"])
                _rstd(S, "psr", psr[:, 0:Tn], ("R", b), R[:, b, 0:Tn], "Rl", Rl[:, 0:Tn], 1.0 / D)

                def f_xw(e):
                    ins = None
                    for kc in range(8):
                        ins = e.scalar_tensor_tensor(out=xwv[:, kc, :], in0=xsv[:, kc, :], scalar=nwt[:, kc:kc + 1], in1=R[:, b, 0:Tn],
                                                     op0=ALU.mult, op1=ALU.mult)
                    return ins
                S.op("dve", f_xw, reads=[("xs", b), "nwt", ("R", b)], writes=[xwk])
                return xwv

            xwvs = {0: prep_chunk(T.hT, 0, 512, 0)}
            load_w(Wk[:], "Wk", C_KA, 512, 0, 512)
            load_w(Wv[:], "Wv", C_VA, 512, 0, 512)
            Wkv = Wk[:].rearrange("p (kc c) -> p kc c", kc=8)
            Wvv = Wv[:].rearrange("p (kc c) -> p kc c", kc=8)
            pcnt = [0]

            def next_ps():
                b = pcnt[0] % 2
                pcnt[0] += 1
                return b
            def chunk_geom(ci):
                return ci * 512, (512 if ci < 16 else 128)
            for ci in range(17):
                t0, Tn = chunk_geom(ci)
                nb = Tn // 128
                b = ci % 2
                xwv = xwvs[ci]
                for pp in range(4):
                    pb = next_ps()
                    S.op("pe", lambda e, pb=pb, pp=pp, xwv=xwv, Tn=Tn: _mm_group(
                        e, ps[:, pb, 0:Tn], [(Wkv[:, kc, pp * 128:(pp + 1) * 128], xwv[:, kc, :]) for kc in range(8)]),
                        reads=["Wk", ("xw", b)], writes=[("ps", pb)])
                    S.op("dve", lambda e, pb=pb, pp=pp, Tn=Tn, b=b: e.tensor_copy(
                        out=kto[:, b, pp * 512:pp * 512 + Tn], in_=ps[:, pb, 0:Tn]),
                        reads=[("ps", pb)], writes=[("kto", b, pp)])
                    S.op("pool", lambda e, pp=pp, Tn=Tn, b=b, t0=t0: e.dma_start(
                        out=T.kt_d[pp, :, t0:t0 + Tn], in_=kto[:, b, pp * 512:pp * 512 + Tn]),
                        reads=[("kto", b, pp)], writes=[("kt_d", pp, ci)], dma_key=("kto", b, pp))
                if ci + 1 < 17:
                    t1, Tn1 = chunk_geom(ci + 1)
                    xwvs[ci + 1] = prep_chunk(T.hT, t1, Tn1, (ci + 1) % 2)
                for bl in range(nb):
                    pb = next_ps()
                    S.op("pe", lambda e, pb=pb, bl=bl, xwv=xwv: _mm_group(
                        e, ps[:, pb, :], [(xwv[:, kc, bl * 128:(bl + 1) * 128], Wvv[:, kc, :]) for kc in range(8)]),
                        reads=["Wv", ("xw", b)], writes=[("ps", pb)])
                    S.op("act", lambda e, pb=pb, bl=bl, b=b: e.activation(
                        out=vo[:, b, bl * 512:(bl + 1) * 512], in_=ps[:, pb, :], func=AF.Copy),
                        reads=[("ps", pb)], writes=[("vo", b)])
                blk0 = t0 // 128
                S.op("pool", lambda e, b=b, nb=nb, blk0=blk0: e.dma_start(
                    out=T.v_d[:, blk0 * 512:(blk0 + nb) * 512], in_=vo[:, b, 0:nb * 512]),
                    reads=[("vo", b)], writes=[("v_d", ci)], dma_key=("vo", b))

            xwov = xwo[:].rearrange("p (kc t) -> p kc t", kc=8)
            for tq in range(4):
                b = (17 + tq) % 2
                prep_chunk(T.hTo, tq * 512, 512, b, xw_dst=xwov[:, :, tq * 512:(tq + 1) * 512], xw_key=("xwo", tq))

            feats = []
            for i in range(4):
                feats.append(("qa", C_QA + i * 128, T.qt_d[i]))
            for i in range(4):
                feats.append(("qn", C_QB + i * 128, T.qh_d[i]))
            for i in range(4):
                feats.append(("silu", C_ZA + i * 128, T.zas_d[i]))
            for i in range(4):
                feats.append(("silu", C_ZB + i * 128, T.zbs_d[i]))
            for i in range(16):
                feats.append(("sig", C_G + i * 128, T.g_d[i]))
            wtv = [wt[:, wb, :].rearrange("p (kc c) -> p kc c", kc=8) for wb in range(2)]

            def norm_epilogue(pb, ub, gain_ap, gain_key, dst, dkey):
                S.op("dve", lambda e: e.tensor_copy(out=u[:, ub, :], in_=ps[:, pb, :]),
                     reads=[("ps", pb)], writes=[("u", ub)])
                S.op("act", lambda e: e.activation(out=sq2[:, ub, :], in_=u[:, ub, :], func=AF.Square),
                     reads=[("u", ub)], writes=[("sq2", ub)])
                S.op("pe", lambda e: e.matmul(psn[:, ub, :], lhsT=bdiag, rhs=sq2[:, ub, :], start=True, stop=True),
                     reads=[("sq2", ub), "cb"], writes=[("psn", ub)])
                _rstd(S, ("psn", ub), psn[:, ub, :], ("ex", ub), ex[:, ub, :], ("ex", ub), ex[:, ub, :], 1.0 / 64)
                S.op("dve", lambda e: e.scalar_tensor_tensor(out=dst, in0=u[:, ub, :], scalar=gain_ap, in1=ex[:, ub, :],
                                                             op0=ALU.mult, op1=ALU.mult),
                     reads=[("u", ub), ("ex", ub), gain_key], writes=[dkey])

            ucnt = [0]
            for fi, (kind, col0, dst_d) in enumerate(feats):
                wb = fi % 2
                fb = fi % 2
                if fi == 0:
                    load_w(wt[:, 0, :], ("wt", 0), col0, 128, 0, 128)
                if fi + 1 < len(feats):
                    load_w(wt[:, (fi + 1) % 2, :], ("wt", (fi + 1) % 2), feats[fi + 1][1], 128, 0, 128)
                for tq in range(4):
                    pb = next_ps()
                    S.op("pe", lambda e, pb=pb, wb=wb, tq=tq: _mm_group(
                        e, ps[:, pb, :], [(wtv[wb][:, kc, :], xwov[:, kc, tq * 512:(tq + 1) * 512]) for kc in range(8)]),
                        reads=[("wt", wb), ("xwo", tq)], writes=[("ps", pb)])
                    dst = fo[:, fb, tq * 512:(tq + 1) * 512]
                    dkey = ("fo", fb, tq)
                    ub = ucnt[0] % 2
                    ucnt[0] += 1
                    if kind == "qa":
                        S.op("dve", lambda e, pb=pb, dst=dst: e.tensor_scalar(
                            out=dst, in0=ps[:, pb, :], scalar1=0.125, scalar2=None, op0=ALU.mult),
                            reads=[("ps", pb)], writes=[dkey])
                    elif kind == "qn":
                        norm_epilogue(pb, ub, gq8[:, 0:1], "gq8", dst, dkey)
                    else:
                        fn_ = AF.Sigmoid if kind == "sig" else AF.Silu
                        S.op("act", lambda e, pb=pb, dst=dst, fn_=fn_: e.activation(out=dst, in_=ps[:, pb, :], func=fn_),
                             reads=[("ps", pb)], writes=[dkey])
                S.op("pool", lambda e, fb=fb, dst_d=dst_d: e.dma_start(out=dst_d, in_=fo[:, fb, :]),
                     reads=[("fo", fb, tq) for tq in range(4)], writes=[("fd", fi)], dma_key=("fo", fb))

            for g in range(2):
                load_w(Wk[:], "Wk", C_KB + g * 64, 64, g * 128, 512, dup_to=g * 128 + 64)
            load_w(Wk[:], "Wk", C_VB, 128, 256, 512)
            cxw = {0: prep_chunk(T.hTs, 0, 512, 0)}
            for tile in range(2):
                for tq in range(4):
                    if tile == 0:
                        b = (tq) % 2
                        xwv = cxw[tq]
                        xkey = ("xw", b)
                        if tq + 1 < 4:
                            cxw[tq + 1] = prep_chunk(T.hTs, (tq + 1) * 512, 512, (tq + 1) % 2)
                    else:
                        xwv = xwov[:, :, tq * 512:(tq + 1) * 512]
                        xkey = ("xwo", tq)
                    for g in range(2):
                        pb = next_ps()
                        ub = ucnt[0] % 2
                        ucnt[0] += 1
                        S.op("pe", lambda e, pb=pb, g=g, xwv=xwv: _mm_group(
                            e, ps[:, pb, :], [(Wkv[:, kc, g * 128:(g + 1) * 128], xwv[:, kc, :]) for kc in range(8)]),
                            reads=["Wk", xkey], writes=[("ps", pb)])
                        norm_epilogue(pb, ub, gt[:, 1:2], "gt", kto[:, ub, 0:512], ("kto", ub, 0))
                        S.op("pool", lambda e, ub=ub, g=g, tile=tile, tq=tq: e.dma_start(
                            out=T.kh_d[g, :, tile * TOWN + tq * 512: tile * TOWN + (tq + 1) * 512], in_=kto[:, ub, 0:512]),
                            reads=[("kto", ub, 0)], writes=[("kh_d", g, tile, tq)], dma_key=("kto", ub, 0))
                    vb_b = (tile * 4 + tq) % 2
                    for bl in range(4):
                        pb = next_ps()
                        S.op("pe", lambda e, pb=pb, bl=bl, xwv=xwv: _mm_group(
                            e, ps[:, pb, 0:128], [(xwv[:, kc, bl * 128:(bl + 1) * 128], Wkv[:, kc, 256:384]) for kc in range(8)]),
                            reads=["Wk", xkey], writes=[("ps", pb)])
                        S.op("dve", lambda e, pb=pb, bl=bl, vb_b=vb_b: e.tensor_copy(
                            out=vbo[:, vb_b, bl * 128:(bl + 1) * 128], in_=ps[:, pb, 0:128]),
                            reads=[("ps", pb)], writes=[("vbo", vb_b)])
                    s0 = (tile * 16 + tq * 4) * 128
                    S.op("pool", lambda e, vb_b=vb_b, s0=s0: e.dma_start(out=T.vb_d[:, s0:s0 + 512], in_=vbo[:, vb_b, :]),
                         reads=[("vbo", vb_b)], writes=[("vb_d", tile, tq)], dma_key=("vbo", vb_b))

            with nc.Block() as block:
                S.run(nc, block, T.G)


def phase_sb(nc, T):
    S = Sched()
    import contextlib
    with contextlib.ExitStack() as st:
        sb = lambda name, shape, dtype: st.enter_context(nc.sbuf_tensor(name, shape, dtype))
        pt = lambda name, shape, dtype: st.enter_context(nc.psum_tensor(name, shape, dtype))
        NB = 4
        NR = 5
        NZ = 4
        cb = sb("b_cb", [128, 1024], BF16)
        mb = sb("b_mb", [128, 9 * 1024], BF16)
        KTr = sb("b_KTr", [128, NR, 2 * NB * 128], BF16)
        Vr = sb("b_Vr", [128, NR, NB * 256], BF16)
        QT = sb("b_QT", [128, 8, TOWN], BF16)
        zs = sb("b_zs", [128, NZ, NB * 1024], F32)
        Lp = sb("b_Lp", [128, 3, NB * 1024], BF16)
        W = sb("b_W", [128, 3, NB * 1024], BF16)
        oo = sb("b_oo", [128, 2, 512], BF16)
        zA = pt("b_zA", [128, 2, 1024], F32)
        B = pt("b_B", [128, 1024], F32)
        O2 = pt("b_O", [128, 2, 512], F32)
        ident = cb[:, 0:128]
        negTri = cb[:, 128:256]
        negRest = cb[:, 256:384]
        _load_consts(S, T, None, cb)
        for mt in range(3):
            S.op("sync", lambda e, mt=mt: e.dma_start(out=mb[:, mt * 3072:(mt + 1) * 3072], in_=T.sbm[:, mt * 3072:(mt + 1) * 3072]),
                 writes=[("mb", mt)], dma_key=("mb", mt))
        S.op("dve", lambda e: e.memset(QT[:], 0.0), writes=["QTz"])
        qkeys = []
        for hf in range(2):
            for pp in range(2):
                for rh in range(2):
                    qi = (hf * 2 + pp) * 2 + rh
                    S.op("sync", lambda e, pp=pp, hf=hf, rh=rh, qi=qi: e.dma_start(
                        out=QT[rh * 64:(rh + 1) * 64, qi, :], in_=T.qt_d[2 * hf + pp, rh * 64:(rh + 1) * 64, :]),
                        reads=["QTz"], writes=[("QT", qi)], dma_key=("QT", qi))
                    qkeys.append(("QT", qi))
        vsrc = T.v_d.rearrange("p (blk c) -> p blk c", c=512)
        steps = []
        for hf in range(1 if T.mini else 2):
            for ip in range(1 if T.mini else 8):
                for kb in range(8 * ip + 8, -1, -1):
                    act = [1] if kb > 8 * ip + 4 else [0, 1]
                    kr = kb - (8 * ip + 1)
                    mt = kr if kr >= 0 else (8 if kb == 0 else None)
                    first = {c: (kb == (8 * ip + 8 if c == 1 else 8 * ip + 4)) for c in act}
                    steps.append(dict(hf=hf, ip=ip, kb=kb, act=act, mt=mt, first=first, last=(kb == 0)))
        n = len(steps)
        batches = []
        for si, stp in enumerate(steps):
            f0 = steps[batches[-1][0]] if batches else None
            if batches and len(batches[-1]) < NB and f0["hf"] == stp["hf"] and f0["ip"] == stp["ip"] and f0["act"] == stp["act"]:
                batches[-1].append(si)
            else:
                batches.append([si])
        bof = {}
        for bi, bt in enumerate(batches):
            for j, si in enumerate(bt):
                bof[si] = (bi, j)
        nbt = len(batches)

        def cols(stp):
            return (512, 1024) if stp["act"] == [1] else (0, 1024)

        def bview(t, slot, bi):
            bt = batches[bi]
            lo, hi = cols(steps[bt[0]])
            return t[:, slot, :].rearrange("p (j c) -> p j c", c=1024)[:, 0:len(bt), lo:hi]

        loaded = set()

        def e_load(bi):
            loaded.add(bi)
            bt = batches[bi]
            hf = steps[bt[0]]["hf"]
            kb_lo, kb_hi = steps[bt[-1]]["kb"], steps[bt[0]]["kb"]
            nj = kb_hi - kb_lo + 1
            slot = bi % NR
            ksrc = T.kt_d[2 * hf:2 * hf + 2, :, kb_lo * 128:(kb_hi + 1) * 128].rearrange("g p t -> p g t")
            kdst = KTr[:, slot, :].rearrange("p (g t) -> p g t", g=2)[:, :, 0:nj * 128]
            S.op("sync", lambda e: e.dma_start(out=kdst, in_=ksrc), writes=[("KTr", slot)], dma_key=("KTr", slot))
            vdst = Vr[:, slot, 0:nj * 256].rearrange("p (j c) -> p j c", c=256)
            S.op("sync", lambda e: e.dma_start(out=vdst, in_=vsrc[:, kb_lo:kb_hi + 1, hf * 256:(hf + 1) * 256]),
                 writes=[("Vr", slot)], dma_key=("Vr", slot))

        def e_zA(si):
            stp = steps[si]
            zb = si % 2
            bi, j = bof[si]
            slot = bi % NR
            jb = stp["kb"] - steps[batches[bi][-1]]["kb"]

            def f(e):
                ins = None
                for c in stp["act"]:
                    kc_ = 2 * stp["ip"] + c
                    for hl in range(4):
                        pp, rh = hl // 2, hl % 2
                        qi = (stp["hf"] * 2 + pp) * 2 + rh
                        ins = e.matmul(zA[:, zb, c * 512 + hl * 128:c * 512 + (hl + 1) * 128],
                                       lhsT=KTr[:, slot, pp * NB * 128 + jb * 128:pp * NB * 128 + (jb + 1) * 128],
                                       rhs=QT[:, qi, kc_ * 128:(kc_ + 1) * 128],
                                       start=(hl == 0), stop=True, skip_group_check=True)
                    if stp["mt"] is not None:
                        m0 = stp["mt"] * 1024 + c * 512
                        ins = e.matmul(zA[:, zb, c * 512:(c + 1) * 512], lhsT=ident, rhs=mb[:, m0:m0 + 512],
                                       start=False, stop=True, skip_group_check=True)
                return ins
            S.op("pe", f, reads=[("KTr", slot)] + qkeys + [("mb", 0), ("mb", 1), ("mb", 2), "cb"], writes=[("zA", zb)])

        def e_zs(si):
            lo, hi = cols(steps[si])
            bi, j = bof[si]
            S.op("dve", lambda e: e.tensor_copy(out=zs[:, bi % NZ, j * 1024 + lo:j * 1024 + hi], in_=zA[:, si % 2, lo:hi]),
                 reads=[("zA", si % 2)], writes=[("zs", bi % NZ, j)])

        def e_Lp(bi):
            nj = len(batches[bi])
            S.op("act", lambda e: e.activation(out=bview(Lp, bi % 3, bi), in_=bview(zs, bi % NZ, bi), func=AF.Softplus),
                 reads=[("zs", bi % NZ, j) for j in range(nj)], writes=[("Lp", bi % 3)])

        def e_cum(si, c, lhs, first_ok):
            stp = steps[si]
            bi, j = bof[si]
            S.op("pe", lambda e: e.matmul(B[:, c * 512:(c + 1) * 512], lhsT=lhs,
                                          rhs=Lp[:, bi % 3, j * 1024 + c * 512:j * 1024 + (c + 1) * 512],
                                          start=(first_ok and stp["first"][c]), stop=True, skip_group_check=True),
                 reads=[("Lp", bi % 3), "cb"], writes=[("B", c)])

        def e_arg(si, c):
            bi, j = bof[si]
            zv = zs[:, bi % NZ, j * 1024 + c * 512:j * 1024 + (c + 1) * 512]
            S.op("dve", lambda e: e.tensor_tensor(out=zv, in0=B[:, c * 512:(c + 1) * 512], in1=zv, op=ALU.add),
                 reads=[("B", c), ("zs", bi % NZ, j)], writes=[("zs", bi % NZ, j)])

        def e_W(bi):
            nj = len(batches[bi])
            S.op("act", lambda e: e.activation(out=bview(W, bi % 3, bi), in_=bview(zs, bi % NZ, bi), func=AF.Exp),
                 reads=[("zs", bi % NZ, j) for j in range(nj)], writes=[("W", bi % 3)])

        def e_AV(si):
            stp = steps[si]
            bi, j = bof[si]
            slot = bi % NR
            jb = stp["kb"] - steps[batches[bi][-1]]["kb"]
            hf = stp["hf"]

            ob_ = stp["ip"] % 2

            def f(e):
                ins = None
                for c in stp["act"]:
                    for hl in range(4):
                        pp, rh = hl // 2, hl % 2
                        ins = e.matmul(O2[rh * 64:(rh + 1) * 64, ob_, c * 256 + pp * 128:c * 256 + (pp + 1) * 128],
                                       lhsT=Vr[:, slot, jb * 256 + hl * 64:jb * 256 + (hl + 1) * 64],
                                       rhs=W[:, bi % 3, j * 1024 + c * 512 + hl * 128:j * 1024 + c * 512 + (hl + 1) * 128],
                                       start=(c == 1 and stp["first"][1] and pp == 0), stop=True,
                                       tile_position=(0, rh * 64), skip_group_check=True)
                return ins
            S.op("pe", f, reads=[("W", bi % 3), ("Vr", slot)], writes=[("O", ob_)])
            if stp["last"]:
                ob = stp["ip"] % 2
                S.op("dve", lambda e: e.tensor_copy(out=oo[:, ob, :], in_=O2[:, ob, :]), reads=[("O", ob)], writes=[("oo", ob)])
                for c in range(2):
                    kc_ = 2 * stp["ip"] + c
                    dst = T.oa_d[2 * hf:2 * hf + 2, :, kc_ * 128:(kc_ + 1) * 128].rearrange("g p t -> p g t")
                    src = oo[:, ob, c * 256:(c + 1) * 256].rearrange("p (g t) -> p g t", g=2)
                    S.op("pool", lambda e, dst=dst, src=src: e.dma_start(out=dst, in_=src),
                         reads=[("oo", ob)], writes=[("oa_d", hf, kc_)], dma_key=("oo", ob, c))

        for bi in range(min(NR, nbt)):
            e_load(bi)
        for bi in range(min(2, nbt)):
            for si in batches[bi]:
                e_zA(si)
                e_zs(si)
        e_Lp(0)
        lp_next, w_next, av_next = 1, 0, 0
        av_state = {"b": 0, "j": 0}

        def flush_av(bi, nmax):
            n_ = 0
            while n_ < nmax and av_state["b"] < w_next and av_state["b"] <= bi - 1:
                b_ = av_state["b"]
                e_AV(batches[b_][av_state["j"]])
                av_state["j"] += 1
                n_ += 1
                if av_state["j"] == len(batches[b_]):
                    if b_ + NR < nbt:
                        e_load(b_ + NR)
                    av_state["b"] += 1
                    av_state["j"] = 0

        for bi in range(nbt):
            if bi + 2 < nbt:
                while (bi + 2) not in loaded:
                    flush_av(bi + 2 - NR + 1, 1)
            nxt = list(batches[bi + 2]) if bi + 2 < nbt else []
            bt = batches[bi]
            for idx, si in enumerate(bt):
                acts = steps[si]["act"]
                if idx == 0:
                    for c in acts:
                        e_cum(si, c, negTri, True)
                if nxt:
                    e_zA(nxt[0])
                for c in acts:
                    e_arg(si, c)
                if nxt:
                    e_zs(nxt.pop(0))
                for c in acts:
                    e_cum(si, c, negRest, False)
                    if idx + 1 < len(bt):
                        e_cum(bt[idx + 1], c, negTri, True)
                flush_av(bi - 1, 2)
            for sj in nxt:
                e_zA(sj)
                e_zs(sj)
            if bi % 2 == 0:
                while lp_next < nbt and lp_next <= bi + 2:
                    e_Lp(lp_next)
                    lp_next += 1
            else:
                while w_next <= bi:
                    e_W(w_next)
                    w_next += 1
        while w_next < nbt:
            e_W(w_next)
            w_next += 1
        flush_av(nbt + 1, 10 ** 6)
        with nc.Block() as block:
            S.run(nc, block, T.G)


def phase_swa(nc, T):
    S = Sched()
    import contextlib
    with contextlib.ExitStack() as st:
        sb = lambda name, shape, dtype: st.enter_context(nc.sbuf_tensor(name, shape, dtype))
        pt = lambda name, shape, dtype: st.enter_context(nc.psum_tensor(name, shape, dtype))
        cf = sb("w_cf", [128, 1024], F32)
        cb = sb("w_cb", [128, 1024], BF16)
        QH = sb("w_QH", [128, 8, TOWN], BF16)
        KH = sb("w_KH", [128, 2, 2 * TOWN], BF16)
        VB = sb("w_VB", [128, 32 * 128], BF16)
        BM = sb("w_BM", [128, 2, 2048], F32)
        swt = sb("w_swt", [128, 512], F32)
        sk = sb("w_sk", [128, 4], F32)
        esk = sb("w_esk", [128, 4], F32)
        lgs = sb("w_lgs", [128, 3, 1024], F32)
        P = sb("w_P", [128, 3, 1024], BF16)
        dn = sb("w_dn", [128, 2, 256], F32)
        ObT = sb("w_ObT", [128, 4, TOWN], BF16)
        lg = pt("w_lg", [128, 3, 1024], F32)
        OD = pt("w_OD", [128, 2, 512], F32)
        ones64 = cb[:, 384:448]
        _load_consts(S, T, cf, cb)
        S.op("dve", lambda e: e.memset(QH[:], 0.0), writes=["QHz"])
        p4_jobs = [(T.f_Wb, "Wb", T.w_br, d0) for d0 in range(0, 1024, 256)] + [(T.f_Wo, "Wo", T.w_out, d0) for d0 in range(0, 1024, 256)]

        def emit_p4_weight(i):
            wdst, wkey, wsrc, d0 = p4_jobs[i]
            b = i % 2
            src = wsrc.rearrange("(kc p) c -> p kc c", p=128)[:, :, d0:d0 + 256]
            dstage = T.f_wst[:, b, :].rearrange("p (kc c) -> p kc c", kc=8)
            S.op("sync", lambda e: e.dma_start(out=dstage, in_=src), writes=[("f_wst", b)], dma_key=("f_wst", b))
            dv = wdst[:].rearrange("p (kc c) -> p kc c", kc=8)[:, :, d0:d0 + 256]
            S.op("pool", lambda e: e.tensor_copy(out=dv, in_=dstage), reads=[("f_wst", b)], writes=[wkey])
        for pq in range(4):
            for rh in range(2):
                S.op("sync", lambda e, pq=pq, rh=rh: e.dma_start(out=QH[rh * 64:(rh + 1) * 64, pq * 2 + rh, :],
                                                                 in_=T.qh_d[pq, rh * 64:(rh + 1) * 64, :]),
                     reads=["QHz"], writes=[("QH", pq, rh)], dma_key=("QH", pq, rh))
        for g in range(2):
            S.op("sync", lambda e, g=g: e.dma_start(out=KH[:, g, :], in_=T.kh_d[g]), writes=[("KH", g)], dma_key=("KH", g))
        S.op("sync", lambda e: e.dma_start(out=VB[:], in_=T.vb_d[:, :]), writes=["VB"], dma_key="VB")
        S.op("sync", lambda e: e.dma_start(out=BM[:, 0, :], in_=T.biasG[:, :]), writes=["BM0"], dma_key="BM0")
        S.op("sync", lambda e: e.dma_start(out=BM[:, 1, :], in_=T.biasG[:, :]), writes=["BM1"], dma_key="BM1")
        S.op("sync", lambda e: e.dma_start(out=swt[:], in_=T.swm[:, :]), writes=["swt"], dma_key="swt")
        S.op("sync", lambda e: e.dma_start(out=sk[:], in_=T.sinkP[:, :]), writes=["sk"], dma_key="sk")
        S.op("act", lambda e: e.activation(out=esk[:], in_=sk[:], func=AF.Exp), reads=["sk"], writes=["esk"])
        for var in range(2):
            def f(e, var=var):
                ins = None
                for h in range(8):
                    ins = e.tensor_tensor(out=BM[:, var, h * 256:(h + 1) * 256], in0=BM[:, var, h * 256:(h + 1) * 256],
                                          in1=swt[:, var * 256:(var + 1) * 256], op=ALU.add)
                return ins
            S.op("dve", f, reads=["BM%d" % var, "swt"], writes=["BM%d" % var])
        units = [(k, g) for k in range(NOWN) for g in range(2)]

        def stage1(ui):
            k, g = units[ui]
            var = 1 if k == 0 else 0
            lb = ui % 3

            def f_qk(e, k=k, g=g, lb=lb):
                ins = None
                for hl in range(4):
                    h = 4 * g + hl
                    pq, rh = h // 2, h % 2
                    for tile in range(2):
                        ins = e.matmul(lg[:, lb, (hl * 2 + tile) * 128:(hl * 2 + tile + 1) * 128],
                                       lhsT=KH[:, g, tile * TOWN + k * 128:tile * TOWN + (k + 1) * 128],
                                       rhs=QH[:, pq * 2 + rh, k * 128:(k + 1) * 128], start=True, stop=True)
                return ins
            S.op("pe", f_qk, reads=[("KH", 0), ("KH", 1)] + [("QH", i, r) for i in range(4) for r in range(2)], writes=[("lg", lb)])
            S.op("dve", lambda e, lb=lb, g=g, var=var: e.tensor_tensor(
                out=lgs[:, lb, :], in0=lg[:, lb, :], in1=BM[:, var, g * 1024:(g + 1) * 1024], op=ALU.add),
                reads=[("lg", lb), "BM%d" % var], writes=[("lgs", lb)])
            S.op("act", lambda e, lb=lb: e.activation(out=P[:, lb, :], in_=lgs[:, lb, :], func=AF.Exp),
                 reads=[("lgs", lb)], writes=[("P", lb)])

        def stage2(ui):
            k, g = units[ui]
            lb = ui % 2
            l3 = ui % 3

            def f_pv(e, k=k, g=g, lb=lb, l3=l3):
                ins = None
                for hl in range(4):
                    pql, rh = hl // 2, hl % 2
                    for tile in range(2):
                        slot = tile * 16 + k
                        rhs = P[:, l3, (hl * 2 + tile) * 128:(hl * 2 + tile + 1) * 128]
                        e.matmul(OD[rh * 64:(rh + 1) * 64, lb, pql * 128:(pql + 1) * 128],
                                 lhsT=VB[:, slot * 128 + g * 64:slot * 128 + (g + 1) * 64], rhs=rhs,
                                 start=(tile == 0 and pql == 0), stop=True, tile_position=(0, rh * 64), skip_group_check=True)
                        ins = e.matmul(OD[rh * 64:(rh + 1) * 64, lb, 256 + pql * 128:256 + (pql + 1) * 128],
                                       lhsT=ones64, rhs=rhs,
                                       start=False, stop=True, tile_position=(0, rh * 64), skip_group_check=True)
                return ins
            S.op("pe", f_pv, reads=[("P", l3), "VB", "cb"], writes=[("OD", lb)])

            def f_den(e, g=g, lb=lb):
                ins = None
                for pql in range(2):
                    pq = 2 * g + pql
                    ins = e.tensor_scalar(out=dn[:, lb, pql * 128:(pql + 1) * 128], in0=OD[:, lb, 256 + pql * 128:256 + (pql + 1) * 128],
                                          scalar1=esk[:, pq:pq + 1], scalar2=None, op0=ALU.add)
                return ins
            S.op("dve", f_den, reads=[("OD", lb), "esk"], writes=[("dn", lb)])
            S.op("act", lambda e, lb=lb: e.activation(out=dn[:, lb, :], in_=dn[:, lb, :], func=AF.Ln), reads=[("dn", lb)], writes=[("dn", lb)])
            S.op("act", lambda e, lb=lb: e.activation(out=dn[:, lb, :], in_=dn[:, lb, :], func=AF.Exp, scale=-1.0),
                 reads=[("dn", lb)], writes=[("dn", lb)])

            def f_nrm(e, k=k, g=g, lb=lb):
                ins = None
                for pql in range(2):
                    pq = 2 * g + pql
                    ins = e.tensor_tensor(out=ObT[:, pq, k * 128:(k + 1) * 128], in0=OD[:, lb, pql * 128:(pql + 1) * 128],
                                          in1=dn[:, lb, pql * 128:(pql + 1) * 128], op=ALU.mult)
                return ins
            S.op("dve", f_nrm, reads=[("OD", lb), ("dn", lb)], writes=["ObT"])

        for i in range(len(p4_jobs)):
            emit_p4_weight(i)
        T.p4w_loaded = True
        stage1(0)
        stage1(1)
        for ui in range(len(units)):
            if ui + 2 < len(units):
                stage1(ui + 2)
            stage2(ui)
        for pq in range(4):
            S.op("sync", lambda e, pq=pq: e.dma_start(out=T.ob_d[pq], in_=ObT[:, pq, :]), reads=["ObT"], writes=[("ob_d", pq)],
                 dma_key=("ObT", pq))
        with nc.Block() as block:
            S.run(nc, block, T.G)


def phase4(nc, T):
    S = Sched()
    import contextlib
    with contextlib.ExitStack() as st:
        sb = lambda name, shape, dtype: st.enter_context(nc.sbuf_tensor(name, shape, dtype))
        pt = lambda name, shape, dtype: st.enter_context(nc.psum_tensor(name, shape, dtype))
        wst, Wb, Wo = T.f_wst, T.f_Wb, T.f_Wo
        oab = sb("f_oab", [128, 2, 8 * 512], BF16)
        zab = sb("f_zab", [128, 2, 8 * 512], BF16)
        br = sb("f_br", [128, 2, 8 * 512], BF16)
        gg = sb("f_gg", [128, 2, 16 * 512], BF16)
        m1 = sb("f_m1", [128, 2, 512], F32)
        m2 = sb("f_m2", [128, 2, 512], F32)
        mg = sb("f_mg", [128, 2, 8 * 512], BF16)
        xt = sb("f_xt", [128, 2, 4 * 1024], F32)
        py = pt("f_py", [128, 4, 512], F32)
        po = pt("f_po", [128, 2, 512], F32)
        cntw = [0]

        def load_w(dst, dkey, src_ap, ncols):
            done = 0
            while done < ncols:
                n = 256
                b = cntw[0] % 2
                cntw[0] += 1
                src = src_ap.rearrange("(kc p) c -> p kc c", p=128)[:, :, done:done + n]
                dstage = wst[:, b, :].rearrange("p (kc c) -> p kc c", kc=8)
                S.op("sync", lambda e, s_=src, d=dstage: e.dma_start(out=d, in_=s_), writes=[("wst", b)], dma_key=("wst", b))
                dv = dst.rearrange("p (kc c) -> p kc c", kc=8)[:, :, done:done + n]
                S.op("dve", lambda e, s_=dstage, d=dv: e.tensor_copy(out=d, in_=s_), reads=[("wst", b)], writes=[dkey])
                done += n
        if not T.p4w_loaded:
            load_w(Wb[:], "Wb", T.w_br, 1024)
            load_w(Wo[:], "Wo", T.w_out, 1024)
        Wbv = Wb[:].rearrange("p (kc c) -> p kc c", kc=8)
        Wov = Wo[:].rearrange("p (kc c) -> p kc c", kc=8)
        for tq in range(4):
            b = tq % 2
            tsl = slice(tq * 512, (tq + 1) * 512)
            oav = oab[:, b, :].rearrange("p (c t) -> p c t", c=8)
            zav = zab[:, b, :].rearrange("p (c t) -> p c t", c=8)
            brv = br[:, b, :].rearrange("p (c t) -> p c t", c=8)
            ggv = gg[:, b, :].rearrange("p (c t) -> p c t", c=16)
            mgv = mg[:, b, :].rearrange("p (c t) -> p c t", c=8)
            xtv = xt[:, b, :].rearrange("p (tb e) -> p tb e", tb=4)
            otv = xtv
            for (srcd, dstv, off, key) in ((T.oa_d, oav, 0, "oa"), (T.ob_d, oav, 4, "ob"), (T.zas_d, zav, 0, "za"), (T.zbs_d, zav, 4, "zb")):
                S.op("sync", lambda e, srcd=srcd, dstv=dstv, off=off, tsl=tsl: e.dma_start(
                    out=dstv[:, off:off + 4, :], in_=srcd[:, :, tsl].rearrange("f p t -> p f t")),
                    writes=[(key, b)], dma_key=(key, b))
            S.op("sync", lambda e, ggv=ggv, tsl=tsl: e.dma_start(out=ggv, in_=T.g_d[:, :, tsl].rearrange("f p t -> p f t")),
                 writes=[("gg", b)], dma_key=("gg", b))
            S.op("sync", lambda e, xtv=xtv, tq=tq: e.dma_start(out=xtv, in_=T.xo[tq * 512:(tq + 1) * 512, :].rearrange("(tb p) e -> p tb e", p=128)),
                 writes=[("xt", b)], dma_key=("xt", b))
            S.op("pool", lambda e, b=b: e.tensor_tensor(out=br[:, b, :], in0=oab[:, b, :], in1=zab[:, b, :], op=ALU.mult),
                 reads=[("oa", b), ("ob", b), ("za", b), ("zb", b)], writes=[("br", b)])
            for dc in range(8):
                yb = dc % 2
                for brn in range(2):
                    S.op("pe", lambda e, yb=yb, brn=brn, dc=dc, brv=brv: _mm_group(
                        e, py[:, yb * 2 + brn, :], [(Wbv[:, brn * 4 + cc, dc * 128:(dc + 1) * 128], brv[:, brn * 4 + cc, :]) for cc in range(4)]),
                        reads=["Wb", ("br", b)], writes=[("py", yb, brn)])
                S.op("dve", lambda e, yb=yb, dc=dc, ggv=ggv: e.tensor_tensor(out=m1[:, yb, :], in0=py[:, yb * 2, :], in1=ggv[:, dc, :], op=ALU.mult),
                     reads=[("py", yb, 0), ("gg", b)], writes=[("m1", yb)])
                S.op("dve", lambda e, yb=yb, dc=dc, ggv=ggv: e.tensor_tensor(out=m2[:, yb, :], in0=py[:, yb * 2 + 1, :], in1=ggv[:, 8 + dc, :], op=ALU.mult),
                     reads=[("py", yb, 1), ("gg", b)], writes=[("m2", yb)])
                S.op("pool", lambda e, yb=yb, dc=dc, mgv=mgv: e.tensor_tensor(out=mgv[:, dc, :], in0=m1[:, yb, :], in1=m2[:, yb, :], op=ALU.add),
                     reads=[("m1", yb), ("m2", yb)], writes=[("mg", b)])
            for tb in range(4):
                for eh in range(2):
                    pb = (tb * 2 + eh) % 2
                    S.op("pe", lambda e, pb=pb, tb=tb, eh=eh, mgv=mgv: _mm_group(
                        e, po[:, pb, :], [(mgv[:, dc, tb * 128:(tb + 1) * 128], Wov[:, dc, eh * 512:(eh + 1) * 512]) for dc in range(8)]),
                        reads=["Wo", ("mg", b)], writes=[("po", pb)])
                    S.op("dve", lambda e, pb=pb, tb=tb, eh=eh, xtv=xtv, otv=otv: e.tensor_tensor(
                        out=otv[:, tb, eh * 512:(eh + 1) * 512], in0=po[:, pb, :], in1=xtv[:, tb, eh * 512:(eh + 1) * 512], op=ALU.add),
                        reads=[("po", pb), ("xt", b)], writes=[("xt", b)])
            S.op("act", lambda e, otv=otv, tq=tq: e.dma_start(out=T.out[tq * 512:(tq + 1) * 512, :].rearrange("(tb p) e -> p tb e", p=128), in_=otv),
                 reads=[("xt", b)], writes=[("out", tq)], dma_key=("ot", b))
        with nc.Block() as block:
            S.run(nc, block, T.G)


def _t5_buckets(rel):
    n = np.maximum(rel, 0)
    max_exact = 16
    large = max_exact + (np.log(np.maximum(n, 1) / max_exact) / np.log(128 / max_exact) * (32 - max_exact)).astype(np.int32)
    large = np.minimum(large, 31)
    return np.where(n < max_exact, n, large).astype(np.int32)


def host_prep(x, meta, rel_bias, norm_w, w_in, q_gain, k_gain, sinks, w_branch, w_out):
    f32 = np.float32
    x = np.asarray(x, f32)
    B = x.shape[0]
    hTs_b = []
    for b in range(B):
        h = np.zeros((LP, D), f32)
        h[112:128] = np.asarray(meta, f32)
        h[128:] = x[b]
        hTs_b.append(np.ascontiguousarray(h.T))
    w_in0 = np.ascontiguousarray(np.asarray(w_in, f32)[0])
    w_br0 = np.ascontiguousarray(np.asarray(w_branch, f32)[0].reshape(1024, D))
    w_out0 = np.ascontiguousarray(np.asarray(w_out, f32)[0])
    nw = np.ascontiguousarray(np.asarray(norm_w, f32)[0].reshape(8, 128).T)
    gains = np.stack([np.tile(np.asarray(q_gain, f32)[0], 2), np.tile(np.asarray(k_gain, f32)[0], 2)], axis=1).astype(f32)
    sk = np.asarray(sinks, f32)[0]
    sinkP = np.zeros((128, 4), f32)
    for pq in range(4):
        sinkP[0:64, pq] = sk[2 * pq]
        sinkP[64:128, pq] = sk[2 * pq + 1]
    s_i = np.arange(128)[:, None]
    t_i = np.arange(128)[None, :]
    rel0 = 128 + t_i - s_i
    rel1 = t_i - s_i
    rb = np.asarray(rel_bias, f32)
    biasG = np.zeros((128, 8, 2, 128), f32)
    for h in range(8):
        biasG[:, h, 0, :] = rb[_t5_buckets(rel0), h]
        biasG[:, h, 1, :] = rb[_t5_buckets(rel1), h]
    vis0 = (rel0 < 128)
    vis1 = (rel1 >= 0)
    swm_gen = np.stack([np.where(vis0, 0.0, NEG), np.where(vis1, 0.0, NEG)], axis=1).astype(f32)
    cst = np.zeros((128, 1024), f32)
    jj = np.arange(128)[:, None]
    ss = np.arange(128)[None, :]
    cst[:, 0:128] = np.eye(128)
    cst[:, 128:256] = np.where(jj >= ss, -1.0, 0.0)
    cst[:, 256:384] = np.where(jj < ss, -1.0, 0.0)
    cst[:, 384:512] = 1.0
    cst[:, 512:576] = 1.0
    cst[:, 640 + 64:768] = 1.0
    cst[0:64, 768:832] = 1.0
    cst[64:128, 832:896] = 1.0
    cst_bf = cst.astype(ml_dtypes.bfloat16)
    in_maps = []
    for c in range(8):
        b, j = c // 4, c % 4
        own = [4 * k + 1 + j for k in range(NOWN)]
        hT = hTs_b[b]
        hTo = np.ascontiguousarray(np.concatenate([hT[:, n * 128:(n + 1) * 128] for n in own], axis=1))
        hTs = np.ascontiguousarray(np.concatenate([hT[:, (n - 1) * 128:n * 128] for n in own], axis=1))
        xo = np.ascontiguousarray(np.concatenate([x[b, (n - 1) * 128:n * 128] for n in own], axis=0))
        swm = np.zeros((128, 2, 2, 128), f32)
        swm[:, 0] = swm_gen
        swm[:, 1] = swm_gen
        if own[0] == 1:
            swm[0:112, 1, 0, :] = NEG
        sbm = np.zeros((128, 9, 2, 4, 128), f32)
        for kr in range(8):
            for cc in range(2):
                d = kr - 4 * cc - j
                if d > 0:
                    sbm[:, kr, cc] = NEG
                elif d == 0:
                    sbm[:, kr, cc] = np.where(s_i < t_i, 0.0, NEG)[:, None, :]
        sbm[0:112, 8] = NEG
        in_maps.append(dict(hT=hT, hTo=hTo, hTs=hTs, xo=xo, w_in=w_in0, w_br=w_br0, w_out=w_out0, nw=nw, gains=gains,
                            sinkP=sinkP, biasG=biasG.reshape(128, 2048), swm=swm.reshape(128, 512), cst=cst_bf,
                            sbm=sbm.reshape(128, 9 * 1024).astype(ml_dtypes.bfloat16)))
    return in_maps


def kernel(x, meta, rel_bias, norm_w, w_in, q_gain, k_gain, sinks, w_branch, w_out):
    in_maps = host_prep(x, meta, rel_bias, norm_w, w_in, q_gain, k_gain, sinks, w_branch, w_out)
    nc = build_program()
    res = run_bass_kernel_spmd(nc, in_maps, core_ids=list(range(8)))
    B = 2
    outp = np.zeros((B, SEQ, D), np.float32)
    for c in range(8):
        b, j = c // 4, c % 4
        o = np.asarray(res.results[c]["out"], np.float32)
        for k in range(NOWN):
            n = 4 * k + 1 + j
            outp[b, (n - 1) * 128:n * 128] = o[k * 128:(k + 1) * 128]
    return outp
```

```python
import numpy as np
import ml_dtypes
import concourse.bass as bass
import concourse.mybir as mybir
from concourse.bass_utils import run_bass_kernel_spmd

F32 = mybir.dt.float32
BF16 = mybir.dt.bfloat16
AF = mybir.ActivationFunctionType
ALU = mybir.AluOpType

D = 1024
SEQ = 8192
LP = SEQ + 128
NBLK = LP // 128
NOWN = 16
TOWN = NOWN * 128
EPS = 1e-6
NEG = -30000.0
C_QA, C_KA, C_VA, C_ZA, C_QB, C_KB, C_VB, C_ZB, C_G = 0, 512, 1024, 1536, 2048, 2560, 2688, 2816, 3328


class Sched:
    def __init__(self):
        self.ops = []
        self.lastw = {}
        self.readers = {}

    def op(self, eng, fn, reads=(), writes=(), dma_key=None):
        idx = len(self.ops)
        deps = set()
        for k in reads:
            w = self.lastw.get(k)
            if w is not None:
                deps.add(w)
        for k in writes:
            w = self.lastw.get(k)
            if w is not None:
                deps.add(w)
            for r in self.readers.get(k, {}).values():
                deps.update(r)
        self.ops.append(dict(eng=eng, fn=fn, deps=deps, dma_key=dma_key))
        for k in reads:
            d = self.readers.setdefault(k, {})
            if dma_key is not None:
                d.setdefault(eng, []).append(idx)
            else:
                d[eng] = [idx]
        for k in writes:
            self.lastw[k] = idx
            self.readers[k] = {}
        return idx

    def run(self, nc, block, G):
        ops = self.ops
        needed = set()
        for o in ops:
            for d in o["deps"]:
                if not (ops[d]["eng"] == "pe" and o["eng"] == "pe"):
                    needed.add(d)
        lastop = {}
        for i, o in enumerate(ops):
            if o["dma_key"] is None:
                lastop[o["eng"]] = i
        needed.update(lastop.values())
        cnt = G.cnt
        prev_final = list(G.final.values())
        for i, o in enumerate(ops):
            if o["dma_key"] is not None:
                k = ("dma", o["dma_key"])
                if k not in G.dma_sems:
                    G.dma_sems[k] = G.pool.pop()
                cnt[k] = cnt.get(k, 0) + 16
                o["sem"] = G.dma_sems[k]
                o["val"] = cnt[k]
                G.final[k] = (o["sem"], o["val"])
            elif i in needed:
                k = o["eng"]
                cnt[k] = cnt.get(k, 0) + 1
                o["sem"] = G.esem[k]
                o["val"] = cnt[k]
                G.final[k] = (o["sem"], o["val"])
            else:
                o["sem"] = None
        end_final = list(G.final.values())

        def emit(engname, e):
            known = {}
            for s_, v in prev_final:
                e.wait_ge(s_, v)
                known[id(s_)] = v
            for i, o in enumerate(ops):
                if o["eng"] != engname:
                    continue
                for d in sorted(o["deps"]):
                    y = ops[d]
                    if y["eng"] == "pe" and engname == "pe":
                        continue
                    s_, v = y["sem"], y["val"]
                    if known.get(id(s_), 0) < v:
                        e.wait_ge(s_, v)
                        known[id(s_)] = v
                ins = o["fn"](e)
                if o["sem"] is not None:
                    ins.then_inc(o["sem"], 16 if o["dma_key"] is not None else 1)
            if engname == "sync":
                for s_, v in end_final:
                    if known.get(id(s_), 0) < v:
                        e.wait_ge(s_, v)

        @block.sync
        def _(e):
            emit("sync", e)

        @block.gpsimd
        def _(e):
            emit("pool", e)

        @block.tensor
        def _(e):
            emit("pe", e)

        @block.vector
        def _(e):
            emit("dve", e)

        @block.scalar
        def _(e):
            emit("act", e)


class Ctx:
    pass


def _mm_group(pe, out, pairs, first=True, last=True):
    n = len(pairs)
    ins = None
    for i, (l, r) in enumerate(pairs):
        ins = pe.matmul(out, lhsT=l, rhs=r, start=(first and i == 0), stop=(last and i == n - 1))
    return ins


def build_program(debug=False, phases="1sw4", mini=False):
    nc = bass.Bass("TRN2", target_bir_lowering=False)
    dt = nc.dram_tensor
    hT = dt("hT", [D, LP], F32, kind="ExternalInput").ap()
    hTo = dt("hTo", [D, TOWN], F32, kind="ExternalInput").ap()
    hTs = dt("hTs", [D, TOWN], F32, kind="ExternalInput").ap()
    xo = dt("xo", [TOWN, D], F32, kind="ExternalInput").ap()
    w_in = dt("w_in", [D, 5376], F32, kind="ExternalInput").ap()
    w_br = dt("w_br", [1024, D], F32, kind="ExternalInput").ap()
    w_out = dt("w_out", [D, D], F32, kind="ExternalInput").ap()
    nw = dt("nw", [128, 8], F32, kind="ExternalInput").ap()
    gains = dt("gains", [128, 2], F32, kind="ExternalInput").ap()
    sinkP = dt("sinkP", [128, 4], F32, kind="ExternalInput").ap()
    biasG = dt("biasG", [128, 2048], F32, kind="ExternalInput").ap()
    swm = dt("swm", [128, 512], F32, kind="ExternalInput").ap()
    cst = dt("cst", [128, 1024], BF16, kind="ExternalInput").ap()
    sbm = dt("sbm", [128, 9 * 1024], BF16, kind="ExternalInput").ap()
    out = dt("out", [TOWN, D], F32, kind="ExternalOutput").ap()
    okind = "ExternalOutput" if debug else "Internal"
    kt_d = dt("kt_d", [4, 128, LP], BF16, kind=okind).ap()
    v_d = dt("v_d", [128, NBLK * 512], BF16, kind=okind).ap()
    qt_d = dt("qt_d", [4, 128, TOWN], BF16, kind=okind).ap()
    qh_d = dt("qh_d", [4, 128, TOWN], BF16, kind=okind).ap()
    kh_d = dt("kh_d", [2, 128, 2 * TOWN], BF16, kind=okind).ap()
    vb_d = dt("vb_d", [128, 32 * 128], BF16, kind=okind).ap()
    zas_d = dt("zas_d", [4, 128, TOWN], BF16, kind=okind).ap()
    zbs_d = dt("zbs_d", [4, 128, TOWN], BF16, kind=okind).ap()
    g_d = dt("g_d", [16, 128, TOWN], BF16, kind=okind).ap()
    oa_d = dt("oa_d", [4, 128, TOWN], BF16, kind=okind).ap()
    ob_d = dt("ob_d", [4, 128, TOWN], BF16, kind=okind).ap()

    T = Ctx()
    T.__dict__.update(locals())
    import contextlib
    with contextlib.ExitStack() as gst, nc.allow_low_precision(reason="bf16 matmul operands by design"):
        G = Ctx()
        G.esem = {k: gst.enter_context(nc.semaphore("s_" + k)) for k in ("pe", "act", "dve", "pool", "sync")}
        G.pool = [gst.enter_context(nc.semaphore("dq%d" % i)) for i in range(90)]
        G.dma_sems = {}
        G.cnt = {}
        G.final = {}
        T.G = G
        T.phases = phases
        T.mini = mini
        if "1" in phases:
            phase1(nc, T)
        if "s" in phases:
            phase_sb(nc, T)
        with contextlib.ExitStack() as wst4:
            T.f_wst = wst4.enter_context(nc.sbuf_tensor("f_wst", [128, 2, 8 * 256], F32))
            T.f_Wb = wst4.enter_context(nc.sbuf_tensor("f_Wb", [128, 8 * 1024], BF16))
            T.f_Wo = wst4.enter_context(nc.sbuf_tensor("f_Wo", [128, 8 * 1024], BF16))
            T.p4w_loaded = False
            if "w" in phases:
                phase_swa(nc, T)
            if "4" in phases:
                phase4(nc, T)
    return nc


def _load_consts(S, T, cf, cb):
    S.op("sync", lambda e: e.dma_start(out=cb[:], in_=T.cst[:, :]), writes=["cb"], dma_key="cb")


def _rstd(S, src_key, src, dst_key, dst, tmp_key, tmp, scale):
    S.op("act", lambda e: e.activation(out=tmp, in_=src, func=AF.Ln, bias=T_EPS[0], scale=scale),
         reads=[src_key, "epsb"], writes=[tmp_key])
    S.op("act", lambda e: e.activation(out=dst, in_=tmp, func=AF.Exp, scale=-0.5), reads=[tmp_key], writes=[dst_key])


T_EPS = [None]


def phase1(nc, T):
    S = Sched()
    import contextlib
    with contextlib.ExitStack() as st:
        if True:
            sb = lambda name, shape, dtype: st.enter_context(nc.sbuf_tensor(name, shape, dtype))
            pt = lambda name, shape, dtype: st.enter_context(nc.psum_tensor(name, shape, dtype))
            sm = lambda name: st.enter_context(nc.semaphore(name))
            cf = None
            cb = sb("cb", [128, 1024], BF16)
            nwt = sb("nwt", [128, 8], F32)
            gt = sb("gt", [128, 2], F32)
            gq8 = sb("gq8", [128, 1], F32)
            epsb = sb("epsb", [128, 1], F32)
            xs = sb("xs", [128, 2, 8 * 512], F32)
            xw = sb("xw", [128, 2, 8 * 512], BF16)
            sq = sb("sq", [128, 2, 8 * 512], BF16)
            Rl = sb("Rl", [128, 512], F32)
            R = sb("R", [128, 2, 512], F32)
            wst = sb("wst", [128, 2, 8 * 256], F32)
            Wk = sb("Wk", [128, 8 * 512], BF16)
            Wv = sb("Wv", [128, 8 * 512], BF16)
            wt = sb("wt", [128, 2, 8 * 128], BF16)
            kto = sb("kto", [128, 2, 4 * 512], BF16)
            vo = sb("vo", [128, 2, 4 * 512], BF16)
            xwo = sb("xwo", [128, 8 * TOWN], BF16)
            u = sb("u", [128, 4, 512], F32)
            ex = sb("ex", [128, 4, 512], F32)
            sq2 = sb("sq2", [128, 4, 512], BF16)
            fo = sb("fo", [128, 2, TOWN], BF16)
            vbo = sb("vbo", [128, 2, 512], BF16)
            ps = pt("ps", [128, 2, 512], F32)
            psr = pt("psr", [128, 512], F32)
            psn = pt("psn", [128, 4, 512], F32)
            ones = cb[:, 384:512]
            onescol = cb[:, 384:385]
            bdiag = cb[:, 768:896]
            _load_consts(S, T, cf, cb)
            S.op("sync", lambda e: e.dma_start(out=nwt[:], in_=T.nw[:, :]), writes=["nwt"], dma_key="nwt")
            S.op("sync", lambda e: e.dma_start(out=gt[:], in_=T.gains[:, :]), writes=["gt"], dma_key="gt")
            S.op("dve", lambda e: e.memset(epsb[:], EPS), writes=["epsb"])
            S.op("dve", lambda e: e.tensor_scalar(out=gq8[:], in0=gt[:, 0:1], scalar1=0.125, scalar2=None, op0=ALU.mult),
                 reads=["gt"], writes=["gq8"])
            T_EPS[0] = epsb[:, 0:1]

            def load_w(dst, dkey, col0, ncols, dcol0, dstride, dup_to=None, src_ap=None):
                done = 0
                nload = [0]
                while done < ncols:
                    n = min(256, ncols - done)
                    b = load_w.cnt % 2
                    load_w.cnt += 1
                    c0 = col0 + done
                    src = (T.w_in if src_ap is None else src_ap).rearrange("(kc p) c -> p kc c", p=128)[:, :, c0:c0 + n]
                    dstage = wst[:, b, 0:8 * n].rearrange("p (kc c) -> p kc c", kc=8)
                    S.op("sync", lambda e, s=src, d=dstage: e.dma_start(out=d, in_=s), writes=[("wst", b)], dma_key=("wst", b))
                    for tgt0 in ([dcol0] if dup_to is None else [dcol0, dup_to]):
                        dv = dst.rearrange("p (kc c) -> p kc c", kc=8)[:, :, tgt0 + done:tgt0 + done + n]
                        S.op("dve", lambda e, s=dstage, d=dv: e.tensor_copy(out=d, in_=s), reads=[("wst", b)], writes=[dkey])
                    done += n
            load_w.cnt = 0

            def prep_chunk(src_ap, t0, Tn, b, xw_dst=None, xw_key=None):
                xsv = xs[:, b, 0:8 * Tn].rearrange("p (kc t) -> p kc t", kc=8)
                xwv = xw[:, b, 0:8 * Tn].rearrange("p (kc t) -> p kc t", kc=8) if xw_dst is None else xw_dst
                xwk = ("xw", b) if xw_key is None else xw_key
                sqv = sq[:, b, 0:8 * Tn].rearrange("p (kc t) -> p kc t", kc=8)
                src = src_ap.rearrange("(kc p) t -> p kc t", p=128)[:, :, t0:t0 + Tn]
                S.op("sync", lambda e: e.dma_start(out=xsv, in_=src), writes=[("xs", b)], dma_key=("xs", b))
                S.op("act", lambda e: e.activation(out=sq[:, b, 0:8 * Tn], in_=xs[:, b, 0:8 * Tn], func=AF.Square),
                     reads=[("xs", b)], writes=[("sq", b)])
                S.op("pe", lambda e: _mm_group(e, psr[:, 0:Tn], [(ones, sqv[:, kc, :]) for kc in range(8)]),
                     reads=[("sq", b), "cb"], writes=["psr"])
                _rstd(S, "psr", psr[:, 0:Tn], ("R", b), R[:, b, 0:Tn], "Rl", Rl[:, 0:Tn], 1.0 / D)

                def f_xw(e):
                    ins = None
                    for kc in range(8):
                        ins = e.scalar_tensor_tensor(out=xwv[:, kc, :], in0=xsv[:, kc, :], scalar=nwt[:, kc:kc + 1], in1=R[:, b, 0:Tn],
                                                     op0=ALU.mult, op1=ALU.mult)
                    return ins
                S.op("dve", f_xw, reads=[("xs", b), "nwt", ("R", b)], writes=[xwk])
                return xwv

            xwvs = {0: prep_chunk(T.hT, 0, 512, 0)}
            load_w(Wk[:], "Wk", C_KA, 512, 0, 512)
            load_w(Wv[:], "Wv", C_VA, 512, 0, 512)
            Wkv = Wk[:].rearrange("p (kc c) -> p kc c", kc=8)
            Wvv = Wv[:].rearrange("p (kc c) -> p kc c", kc=8)
            pcnt = [0]

            def next_ps():
                b = pcnt[0] % 2
                pcnt[0] += 1
                return b
            def chunk_geom(ci):
                return ci * 512, (512 if ci < 16 else 128)
            for ci in range(17):
                t0, Tn = chunk_geom(ci)
                nb = Tn // 128
                b = ci % 2
                xwv = xwvs[ci]
                for pp in range(4):
                    pb = next_ps()
                    S.op("pe", lambda e, pb=pb, pp=pp, xwv=xwv, Tn=Tn: _mm_group(
                        e, ps[:, pb, 0:Tn], [(Wkv[:, kc, pp * 128:(pp + 1) * 128], xwv[:, kc, :]) for kc in range(8)]),
                        reads=["Wk", ("xw", b)], writes=[("ps", pb)])
                    S.op("dve", lambda e, pb=pb, pp=pp, Tn=Tn, b=b: e.tensor_copy(
                        out=kto[:, b, pp * 512:pp * 512 + Tn], in_=ps[:, pb, 0:Tn]),
                        reads=[("ps", pb)], writes=[("kto", b, pp)])
                    S.op("pool", lambda e, pp=pp, Tn=Tn, b=b, t0=t0: e.dma_start(
                        out=T.kt_d[pp, :, t0:t0 + Tn], in_=kto[:, b, pp * 512:pp * 512 + Tn]),
                        reads=[("kto", b, pp)], writes=[("kt_d", pp, ci)], dma_key=("kto", b, pp))
                if ci + 1 < 17:
                    t1, Tn1 = chunk_geom(ci + 1)
                    xwvs[ci + 1] = prep_chunk(T.hT, t1, Tn1, (ci + 1) % 2)
                for bl in range(nb):
                    pb = next_ps()
                    S.op("pe", lambda e, pb=pb, bl=bl, xwv=xwv: _mm_group(
                        e, ps[:, pb, :], [(xwv[:, kc, bl * 128:(bl + 1) * 128], Wvv[:, kc, :]) for kc in range(8)]),
                        reads=["Wv", ("xw", b)], writes=[("ps", pb)])
                    S.op("act", lambda e, pb=pb, bl=bl, b=b: e.activation(
                        out=vo[:, b, bl * 512:(bl + 1) * 512], in_=ps[:, pb, :], func=AF.Copy),
                        reads=[("ps", pb)], writes=[("vo", b)])
                blk0 = t0 // 128
                S.op("pool", lambda e, b=b, nb=nb, blk0=blk0: e.dma_start(
                    out=T.v_d[:, blk0 * 512:(blk0 + nb) * 512], in_=vo[:, b, 0:nb * 512]),
                    reads=[("vo", b)], writes=[("v_d", ci)], dma_key=("vo", b))

            xwov = xwo[:].rearrange("p (kc t) -> p kc t", kc=8)
            for tq in range(4):
                b = (17 + tq) % 2
                prep_chunk(T.hTo, tq * 512, 512, b, xw_dst=xwov[:, :, tq * 512:(tq + 1) * 512], xw_key=("xwo", tq))

            feats = []
            for i in range(4):
                feats.append(("silu", C_ZA + i * 128, T.zas_d[i]))
            for i in range(4):
                feats.append(("silu", C_ZB + i * 128, T.zbs_d[i]))
            for i in range(16):
                feats.append(("sig", C_G + i * 128, T.g_d[i]))
            for i in range(4):
                feats.append(("qa", C_QA + i * 128, T.qt_d[i]))
            for i in range(4):
                feats.append(("qn", C_QB + i * 128, T.qh_d[i]))
            wtv = [wt[:, wb, :].rearrange("p (kc c) -> p kc c", kc=8) for wb in range(2)]

            def norm_epilogue(pb, ub, gain_ap, gain_key, dst, dkey):
                S.op("dve", lambda e: e.tensor_copy(out=u[:, ub, :], in_=ps[:, pb, :]),
                     reads=[("ps", pb)], writes=[("u", ub)])
                S.op("act", lambda e: e.activation(out=sq2[:, ub, :], in_=u[:, ub, :], func=AF.Square),
                     reads=[("u", ub)], writes=[("sq2", ub)])
                S.op("pe", lambda e: e.matmul(psn[:, ub, :], lhsT=bdiag, rhs=sq2[:, ub, :], start=True, stop=True),
                     reads=[("sq2", ub), "cb"], writes=[("psn", ub)])
                _rstd(S, ("psn", ub), psn[:, ub, :], ("ex", ub), ex[:, ub, :], ("ex", ub), ex[:, ub, :], 1.0 / 64)
                S.op("dve", lambda e: e.scalar_tensor_tensor(out=dst, in0=u[:, ub, :], scalar=gain_ap, in1=ex[:, ub, :],
                                                             op0=ALU.mult, op1=ALU.mult),
                     reads=[("u", ub), ("ex", ub), gain_key], writes=[dkey])

            def gen_C():
                for g in range(2):
                    load_w(Wk[:], "Wk", C_KB + g * 64, 64, g * 128, 512, dup_to=g * 128 + 64)
                load_w(Wk[:], "Wk", C_VB, 128, 256, 512)
                yield
                cxw = {0: prep_chunk(T.hTs, 0, 512, 0)}
                yield
                ccnt = 0
                for tile in range(2):
                    for tq in range(4):
                        if tile == 0:
                            b = (tq) % 2
                            xwv = cxw[tq]
                            xkey = ("xw", b)
                            if tq + 1 < 4:
                                cxw[tq + 1] = prep_chunk(T.hTs, (tq + 1) * 512, 512, (tq + 1) % 2)
                                yield
                        else:
                            xwv = xwov[:, :, tq * 512:(tq + 1) * 512]
                            xkey = ("xwo", tq)
                        pend = []
                        for g in range(2):
                            pb = next_ps()
                            ks = ccnt % 2
                            ub = 2 + ks
                            ccnt += 1
                            S.op("pe", lambda e, pb=pb, g=g, xwv=xwv: _mm_group(
                                e, ps[:, pb, :], [(Wkv[:, kc, g * 128:(g + 1) * 128], xwv[:, kc, :]) for kc in range(8)]),
                                reads=["Wk", xkey], writes=[("ps", pb)])
                            S.op("dve", lambda e, ub=ub, pb=pb: e.tensor_copy(out=u[:, ub, :], in_=ps[:, pb, :]),
                                 reads=[("ps", pb)], writes=[("u", ub)])
                            S.op("act", lambda e, ub=ub: e.activation(out=sq2[:, ub, :], in_=u[:, ub, :], func=AF.Square),
                                 reads=[("u", ub)], writes=[("sq2", ub)])
                            pend.append((ub, ks, g))
                            yield
                        for ub, ks, g in pend:
                            S.op("pe", lambda e, ub=ub: e.matmul(psn[:, ub, :], lhsT=bdiag, rhs=sq2[:, ub, :], start=True, stop=True),
                                 reads=[("sq2", ub), "cb"], writes=[("psn", ub)])
                            _rstd(S, ("psn", ub), psn[:, ub, :], ("ex", ub), ex[:, ub, :], ("ex", ub), ex[:, ub, :], 1.0 / 64)
                            yield
                            S.op("dve", lambda e, ub=ub, ks=ks: e.scalar_tensor_tensor(
                                out=kto[:, ks, 0:512], in0=u[:, ub, :], scalar=gt[:, 1:2], in1=ex[:, ub, :], op0=ALU.mult, op1=ALU.mult),
                                reads=[("u", ub), ("ex", ub), "gt"], writes=[("kto", ks, 0)])
                            S.op("pool", lambda e, ks=ks, g=g, tile=tile, tq=tq: e.dma_start(
                                out=T.kh_d[g, :, tile * TOWN + tq * 512: tile * TOWN + (tq + 1) * 512], in_=kto[:, ks, 0:512]),
                                reads=[("kto", ks, 0)], writes=[("kh_d", g, tile, tq)], dma_key=("kto", ks, 0))
                            yield
                        vb_b = (tile * 4 + tq) % 2
                        for bl in range(4):
                            pb = next_ps()
                            S.op("pe", lambda e, pb=pb, bl=bl, xwv=xwv: _mm_group(
                                e, ps[:, pb, 0:128], [(xwv[:, kc, bl * 128:(bl + 1) * 128], Wkv[:, kc, 256:384]) for kc in range(8)]),
                                reads=["Wk", xkey], writes=[("ps", pb)])
                            S.op("dve", lambda e, pb=pb, bl=bl, vb_b=vb_b: e.tensor_copy(
                                out=vbo[:, vb_b, bl * 128:(bl + 1) * 128], in_=ps[:, pb, 0:128]),
                                reads=[("ps", pb)], writes=[("vbo", vb_b)])
                            yield
                        s0 = (tile * 16 + tq * 4) * 128
                        S.op("pool", lambda e, vb_b=vb_b, s0=s0: e.dma_start(out=T.vb_d[:, s0:s0 + 512], in_=vbo[:, vb_b, :]),
                             reads=[("vbo", vb_b)], writes=[("vb_d", tile, tq)], dma_key=("vbo", vb_b))
                        yield
            bgC = gen_C()

            def pumpC(nq):
                for _ in range(nq):
                    if next(bgC, "end") == "end":
                        return

            ucnt = [0]
            for fi, (kind, col0, dst_d) in enumerate(feats):
                wb = fi % 2
                fb = fi % 2
                if fi == 0:
                    load_w(wt[:, 0, :], ("wt", 0), col0, 128, 0, 128)
                if fi + 1 < len(feats):
                    load_w(wt[:, (fi + 1) % 2, :], ("wt", (fi + 1) % 2), feats[fi + 1][1], 128, 0, 128)
                for tq in range(4):
                    pb = next_ps()
                    S.op("pe", lambda e, pb=pb, wb=wb, tq=tq: _mm_group(
                        e, ps[:, pb, :], [(wtv[wb][:, kc, :], xwov[:, kc, tq * 512:(tq + 1) * 512]) for kc in range(8)]),
                        reads=[("wt", wb), ("xwo", tq)], writes=[("ps", pb)])
                    dst = fo[:, fb, tq * 512:(tq + 1) * 512]
                    dkey = ("fo", fb, tq)
                    ub = ucnt[0] % 2
                    ucnt[0] += 1
                    if kind == "qa":
                        S.op("dve", lambda e, pb=pb, dst=dst: e.tensor_scalar(
                            out=dst, in0=ps[:, pb, :], scalar1=0.125, scalar2=None, op0=ALU.mult),
                            reads=[("ps", pb)], writes=[dkey])
                    elif kind == "qn":
                        norm_epilogue(pb, ub, gq8[:, 0:1], "gq8", dst, dkey)
                    if kind in ("qa", "qn"):
                        pumpC(3)
                    else:
                        fn_ = AF.Sigmoid if kind == "sig" else AF.Silu
                        S.op("act", lambda e, pb=pb, dst=dst, fn_=fn_: e.activation(out=dst, in_=ps[:, pb, :], func=fn_),
                             reads=[("ps", pb)], writes=[dkey])
                S.op("pool", lambda e, fb=fb, dst_d=dst_d: e.dma_start(out=dst_d, in_=fo[:, fb, :]),
                     reads=[("fo", fb, tq) for tq in range(4)], writes=[("fd", fi)], dma_key=("fo", fb))
            pumpC(10 ** 6)

            with nc.Block() as block:
                S.run(nc, block, T.G)


def phase_sb(nc, T):
    S = Sched()
    import contextlib
    with contextlib.ExitStack() as st:
        sb = lambda name, shape, dtype: st.enter_context(nc.sbuf_tensor(name, shape, dtype))
        pt = lambda name, shape, dtype: st.enter_context(nc.psum_tensor(name, shape, dtype))
        NB = 4
        NR = 5
        NZ = 4
        cb = sb("b_cb", [128, 1024], BF16)
        mb = sb("b_mb", [128, 9 * 1024], BF16)
        KTr = sb("b_KTr", [128, NR, 2 * NB * 128], BF16)
        Vr = sb("b_Vr", [128, NR, NB * 256], BF16)
        QT = sb("b_QT", [128, 8, TOWN], BF16)
        zs = sb("b_zs", [128, NZ, NB * 1024], F32)
        Lp = sb("b_Lp", [128, 3, NB * 1024], BF16)
        W = sb("b_W", [128, 3, NB * 1024], BF16)
        oo = sb("b_oo", [128, 2, 512], BF16)
        zA = pt("b_zA", [128, 2, 1024], F32)
        B = pt("b_B", [128, 1024], F32)
        O2 = pt("b_O", [128, 2, 512], F32)
        ident = cb[:, 0:128]
        negTri = cb[:, 128:256]
        negRest = cb[:, 256:384]
        _load_consts(S, T, None, cb)
        for mt in range(3):
            S.op("sync", lambda e, mt=mt: e.dma_start(out=mb[:, mt * 3072:(mt + 1) * 3072], in_=T.sbm[:, mt * 3072:(mt + 1) * 3072]),
                 writes=[("mb", mt)], dma_key=("mb", mt))
        S.op("dve", lambda e: e.memset(QT[:], 0.0), writes=["QTz"])
        qkeys = []
        for hf in range(2):
            for pp in range(2):
                for rh in range(2):
                    qi = (hf * 2 + pp) * 2 + rh
                    S.op("sync", lambda e, pp=pp, hf=hf, rh=rh, qi=qi: e.dma_start(
                        out=QT[rh * 64:(rh + 1) * 64, qi, :], in_=T.qt_d[2 * hf + pp, rh * 64:(rh + 1) * 64, :]),
                        reads=["QTz"], writes=[("QT", qi)], dma_key=("QT", qi))
                    qkeys.append(("QT", qi))
        vsrc = T.v_d.rearrange("p (blk c) -> p blk c", c=512)
        steps = []
        for hf in range(1 if T.mini else 2):
            for ip in range(1 if T.mini else 8):
                for kb in range(8 * ip + 8, -1, -1):
                    act = [1] if kb > 8 * ip + 4 else [0, 1]
                    kr = kb - (8 * ip + 1)
                    mt = kr if kr >= 0 else (8 if kb == 0 else None)
                    first = {c: (kb == (8 * ip + 8 if c == 1 else 8 * ip + 4)) for c in act}
                    steps.append(dict(hf=hf, ip=ip, kb=kb, act=act, mt=mt, first=first, last=(kb == 0)))
        n = len(steps)
        batches = []
        for si, stp in enumerate(steps):
            f0 = steps[batches[-1][0]] if batches else None
            if batches and len(batches[-1]) < NB and f0["hf"] == stp["hf"] and f0["ip"] == stp["ip"] and f0["act"] == stp["act"]:
                batches[-1].append(si)
            else:
                batches.append([si])
        bof = {}
        for bi, bt in enumerate(batches):
            for j, si in enumerate(bt):
                bof[si] = (bi, j)
        nbt = len(batches)

        def cols(stp):
            return (512, 1024) if stp["act"] == [1] else (0, 1024)

        def bview(t, slot, bi):
            bt = batches[bi]
            lo, hi = cols(steps[bt[0]])
            return t[:, slot, :].rearrange("p (j c) -> p j c", c=1024)[:, 0:len(bt), lo:hi]

        loaded = set()

        def e_load(bi):
            loaded.add(bi)
            bt = batches[bi]
            hf = steps[bt[0]]["hf"]
            kb_lo, kb_hi = steps[bt[-1]]["kb"], steps[bt[0]]["kb"]
            nj = kb_hi - kb_lo + 1
            slot = bi % NR
            ksrc = T.kt_d[2 * hf:2 * hf + 2, :, kb_lo * 128:(kb_hi + 1) * 128].rearrange("g p t -> p g t")
            kdst = KTr[:, slot, :].rearrange("p (g t) -> p g t", g=2)[:, :, 0:nj * 128]
            S.op("sync", lambda e: e.dma_start(out=kdst, in_=ksrc), writes=[("KTr", slot)], dma_key=("KTr", slot))
            vdst = Vr[:, slot, 0:nj * 256].rearrange("p (j c) -> p j c", c=256)
            S.op("sync", lambda e: e.dma_start(out=vdst, in_=vsrc[:, kb_lo:kb_hi + 1, hf * 256:(hf + 1) * 256]),
                 writes=[("Vr", slot)], dma_key=("Vr", slot))

        def e_zA(si):
            stp = steps[si]
            zb = si % 2
            bi, j = bof[si]
            slot = bi % NR
            jb = stp["kb"] - steps[batches[bi][-1]]["kb"]

            def f(e):
                ins = None
                for c in stp["act"]:
                    kc_ = 2 * stp["ip"] + c
                    for hl in range(4):
                        pp, rh = hl // 2, hl % 2
                        qi = (stp["hf"] * 2 + pp) * 2 + rh
                        ins = e.matmul(zA[:, zb, c * 512 + hl * 128:c * 512 + (hl + 1) * 128],
                                       lhsT=KTr[:, slot, pp * NB * 128 + jb * 128:pp * NB * 128 + (jb + 1) * 128],
                                       rhs=QT[:, qi, kc_ * 128:(kc_ + 1) * 128],
                                       start=(hl == 0), stop=True, skip_group_check=True)
                    if stp["mt"] is not None:
                        m0 = stp["mt"] * 1024 + c * 512
                        ins = e.matmul(zA[:, zb, c * 512:(c + 1) * 512], lhsT=ident, rhs=mb[:, m0:m0 + 512],
                                       start=False, stop=True, skip_group_check=True)
                return ins
            S.op("pe", f, reads=[("KTr", slot)] + qkeys + [("mb", 0), ("mb", 1), ("mb", 2), "cb"], writes=[("zA", zb)])

        def e_zs(si):
            lo, hi = cols(steps[si])
            bi, j = bof[si]
            S.op("dve", lambda e: e.tensor_copy(out=zs[:, bi % NZ, j * 1024 + lo:j * 1024 + hi], in_=zA[:, si % 2, lo:hi]),
                 reads=[("zA", si % 2)], writes=[("zs", bi % NZ, j)])

        def e_Lp(bi):
            nj = len(batches[bi])
            S.op("act", lambda e: e.activation(out=bview(Lp, bi % 3, bi), in_=bview(zs, bi % NZ, bi), func=AF.Softplus),
                 reads=[("zs", bi % NZ, j) for j in range(nj)], writes=[("Lp", bi % 3)])

        def e_cum(si, c, lhs, first_ok):
            stp = steps[si]
            bi, j = bof[si]
            S.op("pe", lambda e: e.matmul(B[:, c * 512:(c + 1) * 512], lhsT=lhs,
                                          rhs=Lp[:, bi % 3, j * 1024 + c * 512:j * 1024 + (c + 1) * 512],
                                          start=(first_ok and stp["first"][c]), stop=True, skip_group_check=True),
                 reads=[("Lp", bi % 3), "cb"], writes=[("B", c)])

        def e_arg(si, c):
            bi, j = bof[si]
            zv = zs[:, bi % NZ, j * 1024 + c * 512:j * 1024 + (c + 1) * 512]
            S.op("dve", lambda e: e.tensor_tensor(out=zv, in0=B[:, c * 512:(c + 1) * 512], in1=zv, op=ALU.add),
                 reads=[("B", c), ("zs", bi % NZ, j)], writes=[("zs", bi % NZ, j)])

        def e_W(bi):
            nj = len(batches[bi])
            S.op("act", lambda e: e.activation(out=bview(W, bi % 3, bi), in_=bview(zs, bi % NZ, bi), func=AF.Exp),
                 reads=[("zs", bi % NZ, j) for j in range(nj)], writes=[("W", bi % 3)])

        def e_AV(si):
            stp = steps[si]
            bi, j = bof[si]
            slot = bi % NR
            jb = stp["kb"] - steps[batches[bi][-1]]["kb"]
            hf = stp["hf"]

            ob_ = stp["ip"] % 2

            def f(e):
                ins = None
                for c in stp["act"]:
                    for hl in range(4):
                        pp, rh = hl // 2, hl % 2
                        ins = e.matmul(O2[rh * 64:(rh + 1) * 64, ob_, c * 256 + pp * 128:c * 256 + (pp + 1) * 128],
                                       lhsT=Vr[:, slot, jb * 256 + hl * 64:jb * 256 + (hl + 1) * 64],
                                       rhs=W[:, bi % 3, j * 1024 + c * 512 + hl * 128:j * 1024 + c * 512 + (hl + 1) * 128],
                                       start=(c == 1 and stp["first"][1] and pp == 0), stop=True,
                                       tile_position=(0, rh * 64), skip_group_check=True)
                return ins
            S.op("pe", f, reads=[("W", bi % 3), ("Vr", slot)], writes=[("O", ob_)])
            if stp["last"]:
                ob = stp["ip"] % 2
                S.op("dve", lambda e: e.tensor_copy(out=oo[:, ob, :], in_=O2[:, ob, :]), reads=[("O", ob)], writes=[("oo", ob)])
                for c in range(2):
                    kc_ = 2 * stp["ip"] + c
                    dst = T.oa_d[2 * hf:2 * hf + 2, :, kc_ * 128:(kc_ + 1) * 128].rearrange("g p t -> p g t")
                    src = oo[:, ob, c * 256:(c + 1) * 256].rearrange("p (g t) -> p g t", g=2)
                    S.op("pool", lambda e, dst=dst, src=src: e.dma_start(out=dst, in_=src),
                         reads=[("oo", ob)], writes=[("oa_d", hf, kc_)], dma_key=("oo", ob, c))

        for bi in range(min(NR, nbt)):
            e_load(bi)
        for bi in range(min(2, nbt)):
            for si in batches[bi]:
                e_zA(si)
                e_zs(si)
        e_Lp(0)
        lp_next, w_next, av_next = 1, 0, 0
        av_state = {"b": 0, "j": 0}

        def flush_av(bi, nmax):
            n_ = 0
            while n_ < nmax and av_state["b"] < w_next and av_state["b"] <= bi - 1:
                b_ = av_state["b"]
                e_AV(batches[b_][av_state["j"]])
                av_state["j"] += 1
                n_ += 1
                if av_state["j"] == len(batches[b_]):
                    if b_ + NR < nbt:
                        e_load(b_ + NR)
                    av_state["b"] += 1
                    av_state["j"] = 0

        for bi in range(nbt):
            if bi + 2 < nbt:
                while (bi + 2) not in loaded:
                    flush_av(bi + 2 - NR + 1, 1)
            nxt = list(batches[bi + 2]) if bi + 2 < nbt else []
            bt = batches[bi]
            for idx, si in enumerate(bt):
                acts = steps[si]["act"]
                if idx == 0:
                    for c in acts:
                        e_cum(si, c, negTri, True)
                if nxt:
                    e_zA(nxt[0])
                for c in acts:
                    e_arg(si, c)
                if nxt:
                    e_zs(nxt.pop(0))
                for c in acts:
                    e_cum(si, c, negRest, False)
                    if idx + 1 < len(bt):
                        e_cum(bt[idx + 1], c, negTri, True)
                flush_av(bi - 1, 2)
            for sj in nxt:
                e_zA(sj)
                e_zs(sj)
            if bi % 2 == 0:
                while lp_next < nbt and lp_next <= bi + 2:
                    e_Lp(lp_next)
                    lp_next += 1
            else:
                while w_next <= bi:
                    e_W(w_next)
                    w_next += 1
        while w_next < nbt:
            e_W(w_next)
            w_next += 1
        flush_av(nbt + 1, 10 ** 6)
        with nc.Block() as block:
            S.run(nc, block, T.G)


def phase_swa(nc, T):
    S = Sched()
    import contextlib
    with contextlib.ExitStack() as st:
        sb = lambda name, shape, dtype: st.enter_context(nc.sbuf_tensor(name, shape, dtype))
        pt = lambda name, shape, dtype: st.enter_context(nc.psum_tensor(name, shape, dtype))
        cf = sb("w_cf", [128, 1024], F32)
        cb = sb("w_cb", [128, 1024], BF16)
        QH = sb("w_QH", [128, 8, TOWN], BF16)
        KH = sb("w_KH", [128, 2, 2 * TOWN], BF16)
        VB = sb("w_VB", [128, 32 * 128], BF16)
        BM = sb("w_BM", [128, 2, 2048], F32)
        swt = sb("w_swt", [128, 512], F32)
        sk = sb("w_sk", [128, 4], F32)
        esk = sb("w_esk", [128, 4], F32)
        lgs = sb("w_lgs", [128, 2, 1024], F32)
        P = sb("w_P", [128, 2, 1024], BF16)
        dn = sb("w_dn", [128, 2, 256], F32)
        ObT = sb("w_ObT", [128, 4, TOWN], BF16)
        lg = pt("w_lg", [128, 2, 1024], F32)
        OD = pt("w_OD", [128, 2, 512], F32)
        ones64 = cb[:, 384:448]
        _load_consts(S, T, cf, cb)
        S.op("dve", lambda e: e.memset(QH[:], 0.0), writes=["QHz"])
        p4_jobs = [(T.f_Wb, "Wb", T.w_br, d0) for d0 in range(0, 1024, 256)] + [(T.f_Wo, "Wo", T.w_out, d0) for d0 in range(0, 1024, 256)]

        def emit_p4_weight(i):
            wdst, wkey, wsrc, d0 = p4_jobs[i]
            b = i % 2
            src = wsrc.rearrange("(kc p) c -> p kc c", p=128)[:, :, d0:d0 + 256]
            dstage = T.f_wst[:, b, :].rearrange("p (kc c) -> p kc c", kc=8)
            S.op("sync", lambda e: e.dma_start(out=dstage, in_=src), writes=[("f_wst", b)], dma_key=("f_wst", b))
            dv = wdst[:].rearrange("p (kc c) -> p kc c", kc=8)[:, :, d0:d0 + 256]
            S.op("pool", lambda e: e.tensor_copy(out=dv, in_=dstage), reads=[("f_wst", b)], writes=[wkey])
        for pq in range(4):
            for rh in range(2):
                S.op("sync", lambda e, pq=pq, rh=rh: e.dma_start(out=QH[rh * 64:(rh + 1) * 64, pq * 2 + rh, :],
                                                                 in_=T.qh_d[pq, rh * 64:(rh + 1) * 64, :]),
                     reads=["QHz"], writes=[("QH", pq, rh)], dma_key=("QH", pq, rh))
        for g in range(2):
            S.op("sync", lambda e, g=g: e.dma_start(out=KH[:, g, :], in_=T.kh_d[g]), writes=[("KH", g)], dma_key=("KH", g))
        S.op("sync", lambda e: e.dma_start(out=VB[:], in_=T.vb_d[:, :]), writes=["VB"], dma_key="VB")
        S.op("sync", lambda e: e.dma_start(out=BM[:, 0, :], in_=T.biasG[:, :]), writes=["BM0"], dma_key="BM0")
        S.op("sync", lambda e: e.dma_start(out=BM[:, 1, :], in_=T.biasG[:, :]), writes=["BM1"], dma_key="BM1")
        S.op("sync", lambda e: e.dma_start(out=swt[:], in_=T.swm[:, :]), writes=["swt"], dma_key="swt")
        S.op("sync", lambda e: e.dma_start(out=sk[:], in_=T.sinkP[:, :]), writes=["sk"], dma_key="sk")
        S.op("act", lambda e: e.activation(out=esk[:], in_=sk[:], func=AF.Exp), reads=["sk"], writes=["esk"])
        for var in range(2):
            def f(e, var=var):
                ins = None
                for h in range(8):
                    ins = e.tensor_tensor(out=BM[:, var, h * 256:(h + 1) * 256], in0=BM[:, var, h * 256:(h + 1) * 256],
                                          in1=swt[:, var * 256:(var + 1) * 256], op=ALU.add)
                return ins
            S.op("dve", f, reads=["BM%d" % var, "swt"], writes=["BM%d" % var])
        units = [(k, g) for k in range(NOWN) for g in range(2)]

        def stage1(ui):
            k, g = units[ui]
            var = 1 if k == 0 else 0
            lb = ui % 2

            def f_qk(e, k=k, g=g, lb=lb):
                ins = None
                for hl in range(4):
                    h = 4 * g + hl
                    pq, rh = h // 2, h % 2
                    for tile in range(2):
                        ins = e.matmul(lg[:, lb, (hl * 2 + tile) * 128:(hl * 2 + tile + 1) * 128],
                                       lhsT=KH[:, g, tile * TOWN + k * 128:tile * TOWN + (k + 1) * 128],
                                       rhs=QH[:, pq * 2 + rh, k * 128:(k + 1) * 128], start=True, stop=True)
                return ins
            S.op("pe", f_qk, reads=[("KH", 0), ("KH", 1)] + [("QH", i, r) for i in range(4) for r in range(2)], writes=[("lg", lb)])
            S.op("dve", lambda e, lb=lb, g=g, var=var: e.tensor_tensor(
                out=lgs[:, lb, :], in0=lg[:, lb, :], in1=BM[:, var, g * 1024:(g + 1) * 1024], op=ALU.add),
                reads=[("lg", lb), "BM%d" % var], writes=[("lgs", lb)])
            S.op("act", lambda e, lb=lb: e.activation(out=P[:, lb, :], in_=lgs[:, lb, :], func=AF.Exp),
                 reads=[("lgs", lb)], writes=[("P", lb)])

        def stage2(ui):
            k, g = units[ui]
            lb = ui % 2

            def f_pv(e, k=k, g=g, lb=lb):
                ins = None
                for hl in range(4):
                    pql, rh = hl // 2, hl % 2
                    for tile in range(2):
                        slot = tile * 16 + k
                        rhs = P[:, lb, (hl * 2 + tile) * 128:(hl * 2 + tile + 1) * 128]
                        e.matmul(OD[rh * 64:(rh + 1) * 64, lb, pql * 128:(pql + 1) * 128],
                                 lhsT=VB[:, slot * 128 + g * 64:slot * 128 + (g + 1) * 64], rhs=rhs,
                                 start=(tile == 0 and pql == 0), stop=True, tile_position=(0, rh * 64), skip_group_check=True)
                        ins = e.matmul(OD[rh * 64:(rh + 1) * 64, lb, 256 + pql * 128:256 + (pql + 1) * 128],
                                       lhsT=ones64, rhs=rhs,
                                       start=False, stop=True, tile_position=(0, rh * 64), skip_group_check=True)
                return ins
            S.op("pe", f_pv, reads=[("P", lb), "VB", "cb"], writes=[("OD", lb)])

            def f_den(e, g=g, lb=lb):
                ins = None
                for pql in range(2):
                    pq = 2 * g + pql
                    ins = e.tensor_scalar(out=dn[:, lb, pql * 128:(pql + 1) * 128], in0=OD[:, lb, 256 + pql * 128:256 + (pql + 1) * 128],
                                          scalar1=esk[:, pq:pq + 1], scalar2=None, op0=ALU.add)
                return ins
            S.op("dve", f_den, reads=[("OD", lb), "esk"], writes=[("dn", lb)])
            S.op("act", lambda e, lb=lb: e.activation(out=dn[:, lb, :], in_=dn[:, lb, :], func=AF.Ln), reads=[("dn", lb)], writes=[("dn", lb)])
            S.op("act", lambda e, lb=lb: e.activation(out=dn[:, lb, :], in_=dn[:, lb, :], func=AF.Exp, scale=-1.0),
                 reads=[("dn", lb)], writes=[("dn", lb)])

            def f_nrm(e, k=k, g=g, lb=lb):
                ins = None
                for pql in range(2):
                    pq = 2 * g + pql
                    ins = e.tensor_tensor(out=ObT[:, pq, k * 128:(k + 1) * 128], in0=OD[:, lb, pql * 128:(pql + 1) * 128],
                                          in1=dn[:, lb, pql * 128:(pql + 1) * 128], op=ALU.mult)
                return ins
            S.op("dve", f_nrm, reads=[("OD", lb), ("dn", lb)], writes=["ObT"])

        for i in range(len(p4_jobs)):
            emit_p4_weight(i)
        T.p4w_loaded = True
        stage1(0)
        for ui in range(len(units)):
            if ui + 1 < len(units):
                stage1(ui + 1)
            stage2(ui)
        for pq in range(4):
            S.op("sync", lambda e, pq=pq: e.dma_start(out=T.ob_d[pq], in_=ObT[:, pq, :]), reads=["ObT"], writes=[("ob_d", pq)],
                 dma_key=("ObT", pq))
        with nc.Block() as block:
            S.run(nc, block, T.G)


def phase4(nc, T):
    S = Sched()
    import contextlib
    with contextlib.ExitStack() as st:
        sb = lambda name, shape, dtype: st.enter_context(nc.sbuf_tensor(name, shape, dtype))
        pt = lambda name, shape, dtype: st.enter_context(nc.psum_tensor(name, shape, dtype))
        wst, Wb, Wo = T.f_wst, T.f_Wb, T.f_Wo
        oab = sb("f_oab", [128, 2, 8 * 512], BF16)
        zab = sb("f_zab", [128, 2, 8 * 512], BF16)
        br = sb("f_br", [128, 2, 8 * 512], BF16)
        gg = sb("f_gg", [128, 2, 16 * 512], BF16)
        m1 = sb("f_m1", [128, 2, 512], F32)
        m2 = sb("f_m2", [128, 2, 512], F32)
        mg = sb("f_mg", [128, 2, 8 * 512], BF16)
        xt = sb("f_xt", [128, 2, 4 * 1024], F32)
        py = pt("f_py", [128, 4, 512], F32)
        po = pt("f_po", [128, 2, 512], F32)
        cntw = [0]

        def load_w(dst, dkey, src_ap, ncols):
            done = 0
            while done < ncols:
                n = 256
                b = cntw[0] % 2
                cntw[0] += 1
                src = src_ap.rearrange("(kc p) c -> p kc c", p=128)[:, :, done:done + n]
                dstage = wst[:, b, :].rearrange("p (kc c) -> p kc c", kc=8)
                S.op("sync", lambda e, s_=src, d=dstage: e.dma_start(out=d, in_=s_), writes=[("wst", b)], dma_key=("wst", b))
                dv = dst.rearrange("p (kc c) -> p kc c", kc=8)[:, :, done:done + n]
                S.op("dve", lambda e, s_=dstage, d=dv: e.tensor_copy(out=d, in_=s_), reads=[("wst", b)], writes=[dkey])
                done += n
        if not T.p4w_loaded:
            load_w(Wb[:], "Wb", T.w_br, 1024)
            load_w(Wo[:], "Wo", T.w_out, 1024)
        Wbv = Wb[:].rearrange("p (kc c) -> p kc c", kc=8)
        Wov = Wo[:].rearrange("p (kc c) -> p kc c", kc=8)
        for tq in range(4):
            b = tq % 2
            tsl = slice(tq * 512, (tq + 1) * 512)
            oav = oab[:, b, :].rearrange("p (c t) -> p c t", c=8)
            zav = zab[:, b, :].rearrange("p (c t) -> p c t", c=8)
            brv = br[:, b, :].rearrange("p (c t) -> p c t", c=8)
            ggv = gg[:, b, :].rearrange("p (c t) -> p c t", c=16)
            mgv = mg[:, b, :].rearrange("p (c t) -> p c t", c=8)
            xtv = xt[:, b, :].rearrange("p (tb e) -> p tb e", tb=4)
            otv = xtv
            for (srcd, dstv, off, key) in ((T.oa_d, oav, 0, "oa"), (T.ob_d, oav, 4, "ob"), (T.zas_d, zav, 0, "za"), (T.zbs_d, zav, 4, "zb")):
                S.op("sync", lambda e, srcd=srcd, dstv=dstv, off=off, tsl=tsl: e.dma_start(
                    out=dstv[:, off:off + 4, :], in_=srcd[:, :, tsl].rearrange("f p t -> p f t")),
                    writes=[(key, b)], dma_key=(key, b))
            S.op("sync", lambda e, ggv=ggv, tsl=tsl: e.dma_start(out=ggv, in_=T.g_d[:, :, tsl].rearrange("f p t -> p f t")),
                 writes=[("gg", b)], dma_key=("gg", b))
            S.op("sync", lambda e, xtv=xtv, tq=tq: e.dma_start(out=xtv, in_=T.xo[tq * 512:(tq + 1) * 512, :].rearrange("(tb p) e -> p tb e", p=128)),
                 writes=[("xt", b)], dma_key=("xt", b))
            S.op("pool", lambda e, b=b: e.tensor_tensor(out=br[:, b, :], in0=oab[:, b, :], in1=zab[:, b, :], op=ALU.mult),
                 reads=[("oa", b), ("ob", b), ("za", b), ("zb", b)], writes=[("br", b)])
            for dc in range(8):
                yb = dc % 2
                for brn in range(2):
                    S.op("pe", lambda e, yb=yb, brn=brn, dc=dc, brv=brv: _mm_group(
                        e, py[:, yb * 2 + brn, :], [(Wbv[:, brn * 4 + cc, dc * 128:(dc + 1) * 128], brv[:, brn * 4 + cc, :]) for cc in range(4)]),
                        reads=["Wb", ("br", b)], writes=[("py", yb, brn)])
                S.op("dve", lambda e, yb=yb, dc=dc, ggv=ggv: e.tensor_tensor(out=m1[:, yb, :], in0=py[:, yb * 2, :], in1=ggv[:, dc, :], op=ALU.mult),
                     reads=[("py", yb, 0), ("gg", b)], writes=[("m1", yb)])
                S.op("dve", lambda e, yb=yb, dc=dc, ggv=ggv: e.tensor_tensor(out=m2[:, yb, :], in0=py[:, yb * 2 + 1, :], in1=ggv[:, 8 + dc, :], op=ALU.mult),
                     reads=[("py", yb, 1), ("gg", b)], writes=[("m2", yb)])
                S.op("pool", lambda e, yb=yb, dc=dc, mgv=mgv: e.tensor_tensor(out=mgv[:, dc, :], in0=m1[:, yb, :], in1=m2[:, yb, :], op=ALU.add),
                     reads=[("m1", yb), ("m2", yb)], writes=[("mg", b)])
            for tb in range(4):
                for eh in range(2):
                    pb = (tb * 2 + eh) % 2
                    S.op("pe", lambda e, pb=pb, tb=tb, eh=eh, mgv=mgv: _mm_group(
                        e, po[:, pb, :], [(mgv[:, dc, tb * 128:(tb + 1) * 128], Wov[:, dc, eh * 512:(eh + 1) * 512]) for dc in range(8)]),
                        reads=["Wo", ("mg", b)], writes=[("po", pb)])
                    S.op("dve", lambda e, pb=pb, tb=tb, eh=eh, xtv=xtv, otv=otv: e.tensor_tensor(
                        out=otv[:, tb, eh * 512:(eh + 1) * 512], in0=po[:, pb, :], in1=xtv[:, tb, eh * 512:(eh + 1) * 512], op=ALU.add),
                        reads=[("po", pb), ("xt", b)], writes=[("xt", b)])
            S.op("act", lambda e, otv=otv, tq=tq: e.dma_start(out=T.out[tq * 512:(tq + 1) * 512, :].rearrange("(tb p) e -> p tb e", p=128), in_=otv),
                 reads=[("xt", b)], writes=[("out", tq)], dma_key=("ot", b))
        with nc.Block() as block:
            S.run(nc, block, T.G)


def _t5_buckets(rel):
    n = np.maximum(rel, 0)
    max_exact = 16
    large = max_exact + (np.log(np.maximum(n, 1) / max_exact) / np.log(128 / max_exact) * (32 - max_exact)).astype(np.int32)
    large = np.minimum(large, 31)
    return np.where(n < max_exact, n, large).astype(np.int32)


def host_prep(x, meta, rel_bias, norm_w, w_in, q_gain, k_gain, sinks, w_branch, w_out):
    f32 = np.float32
    x = np.asarray(x, f32)
    B = x.shape[0]
    hTs_b = []
    for b in range(B):
        h = np.zeros((LP, D), f32)
        h[112:128] = np.asarray(meta, f32)
        h[128:] = x[b]
        hTs_b.append(np.ascontiguousarray(h.T))
    w_in0 = np.ascontiguousarray(np.asarray(w_in, f32)[0])
    w_br0 = np.ascontiguousarray(np.asarray(w_branch, f32)[0].reshape(1024, D))
    w_out0 = np.ascontiguousarray(np.asarray(w_out, f32)[0])
    nw = np.ascontiguousarray(np.asarray(norm_w, f32)[0].reshape(8, 128).T)
    gains = np.stack([np.tile(np.asarray(q_gain, f32)[0], 2), np.tile(np.asarray(k_gain, f32)[0], 2)], axis=1).astype(f32)
    sk = np.asarray(sinks, f32)[0]
    sinkP = np.zeros((128, 4), f32)
    for pq in range(4):
        sinkP[0:64, pq] = sk[2 * pq]
        sinkP[64:128, pq] = sk[2 * pq + 1]
    s_i = np.arange(128)[:, None]
    t_i = np.arange(128)[None, :]
    rel0 = 128 + t_i - s_i
    rel1 = t_i - s_i
    rb = np.asarray(rel_bias, f32)
    biasG = np.zeros((128, 8, 2, 128), f32)
    for h in range(8):
        biasG[:, h, 0, :] = rb[_t5_buckets(rel0), h]
        biasG[:, h, 1, :] = rb[_t5_buckets(rel1), h]
    vis0 = (rel0 < 128)
    vis1 = (rel1 >= 0)
    swm_gen = np.stack([np.where(vis0, 0.0, NEG), np.where(vis1, 0.0, NEG)], axis=1).astype(f32)
    cst = np.zeros((128, 1024), f32)
    jj = np.arange(128)[:, None]
    ss = np.arange(128)[None, :]
    cst[:, 0:128] = np.eye(128)
    cst[:, 128:256] = np.where(jj >= ss, -1.0, 0.0)
    cst[:, 256:384] = np.where(jj < ss, -1.0, 0.0)
    cst[:, 384:512] = 1.0
    cst[:, 512:576] = 1.0
    cst[:, 640 + 64:768] = 1.0
    cst[0:64, 768:832] = 1.0
    cst[64:128, 832:896] = 1.0
    cst_bf = cst.astype(ml_dtypes.bfloat16)
    in_maps = []
    for c in range(8):
        b, j = c // 4, c % 4
        own = [4 * k + 1 + j for k in range(NOWN)]
        hT = hTs_b[b]
        hTo = np.ascontiguousarray(np.concatenate([hT[:, n * 128:(n + 1) * 128] for n in own], axis=1))
        hTs = np.ascontiguousarray(np.concatenate([hT[:, (n - 1) * 128:n * 128] for n in own], axis=1))
        xo = np.ascontiguousarray(np.concatenate([x[b, (n - 1) * 128:n * 128] for n in own], axis=0))
        swm = np.zeros((128, 2, 2, 128), f32)
        swm[:, 0] = swm_gen
        swm[:, 1] = swm_gen
        if own[0] == 1:
            swm[0:112, 1, 0, :] = NEG
        sbm = np.zeros((128, 9, 2, 4, 128), f32)
        for kr in range(8):
            for cc in range(2):
                d = kr - 4 * cc - j
                if d > 0:
                    sbm[:, kr, cc] = NEG
                elif d == 0:
                    sbm[:, kr, cc] = np.where(s_i < t_i, 0.0, NEG)[:, None, :]
        sbm[0:112, 8] = NEG
        in_maps.append(dict(hT=hT, hTo=hTo, hTs=hTs, xo=xo, w_in=w_in0, w_br=w_br0, w_out=w_out0, nw=nw, gains=gains,
                            sinkP=sinkP, biasG=biasG.reshape(128, 2048), swm=swm.reshape(128, 512), cst=cst_bf,
                            sbm=sbm.reshape(128, 9 * 1024).astype(ml_dtypes.bfloat16)))
    return in_maps


def kernel(x, meta, rel_bias, norm_w, w_in, q_gain, k_gain, sinks, w_branch, w_out):
    in_maps = host_prep(x, meta, rel_bias, norm_w, w_in, q_gain, k_gain, sinks, w_branch, w_out)
    nc = build_program()
    res = run_bass_kernel_spmd(nc, in_maps, core_ids=list(range(8)))
    B = 2
    outp = np.zeros((B, SEQ, D), np.float32)
    for c in range(8):
        b, j = c // 4, c % 4
        o = np.asarray(res.results[c]["out"], np.float32)
        for k in range(NOWN):
            n = 4 * k + 1 + j
            outp[b, (n - 1) * 128:n * 128] = o[k * 128:(k + 1) * 128]
    return outp
```
